# Optimizing a Trainium2 kernel written in Bass

```python
import jax
import jax.numpy as jnp
from jax import lax
import numpy as np

D_MODEL = 1024
BATCH = 16
SEQ = 2048
DEPTH = 2

MEM_LEN = 256
MIX_WIDTH = D_MODEL
A_WIDTH = MIX_WIDTH // 2
HGRN_HEAD_DIM = 128
HGRN_HEADS = A_WIDTH // HGRN_HEAD_DIM
HGRN_CHUNK = 64
B_WIDTH = MIX_WIDTH - A_WIDTH
CONV_WIDTH = 31
C_WIDTH = MIX_WIDTH // 2
MOBA_HEAD_DIM = 64
MOBA_HEADS = C_WIDTH // MOBA_HEAD_DIM
MOBA_BLOCK = 256
MOBA_TOPK = 3
MOBA_QCHUNK = 64
D_WIDTH = MIX_WIDTH - C_WIDTH
SGU_CHUNK = 128
SGU_GROUPS = 4
SGU_GROUP_DIM = D_WIDTH // SGU_GROUPS
XA_HEADS = 4
XA_HEAD_DIM = D_MODEL // XA_HEADS
FFN_HIDDEN = ((-(-8 * D_MODEL // 3) + 255) // 256) * 256
EVEN_IN = 4 * A_WIDTH + 2 * B_WIDTH
ODD_IN = 3 * C_WIDTH + 2 * D_WIDTH
N_EVEN = (DEPTH + 1) // 2
N_ODD = DEPTH // 2
EPS = 1e-6

kernel_name = 'hybrid_hgrn2_conv_moba_sgu_trunk'


def _rmsnorm(x, g):
    xf = x.astype(jnp.float32)
    y = xf * lax.rsqrt(jnp.mean(xf * xf, axis=-1, keepdims=True) + EPS)
    return (y * g.astype(jnp.float32)).astype(x.dtype)


def _layernorm(x, g, b):
    xf = x.astype(jnp.float32)
    mu = jnp.mean(xf, axis=-1, keepdims=True)
    var = jnp.mean(jnp.square(xf - mu), axis=-1, keepdims=True)
    y = (xf - mu) * lax.rsqrt(var + EPS) * g.astype(jnp.float32) + b.astype(jnp.float32)
    return y.astype(x.dtype)


def _hgrn2_chunkwise(q, f, v):
    bsz, seq, _ = q.shape
    n_chunks = seq // HGRN_CHUNK

    def heads(t):
        t = t.astype(jnp.float32).reshape(bsz, n_chunks, HGRN_CHUNK, HGRN_HEADS, HGRN_HEAD_DIM)
        return t.transpose(1, 0, 3, 2, 4)

    qf, ff, vf = heads(q), heads(f), heads(v)
    cum = jnp.cumsum(jnp.log(ff), axis=3)
    kf = 1.0 - ff
    q_in = qf * jnp.exp(cum)
    k_in = kf * jnp.exp(-cum)
    causal = jnp.tril(jnp.ones((HGRN_CHUNK, HGRN_CHUNK), dtype=bool))
    att = jnp.where(causal, jnp.einsum('nbhcd,nbhsd->nbhcs', q_in, k_in), 0.0)
    o_intra = jnp.einsum('nbhcs,nbhse->nbhce', att, vf)
    cum_last = cum[..., -1:, :]
    chunk_kv = jnp.einsum('nbhsd,nbhse->nbhde', kf * jnp.exp(cum_last - cum), vf)
    decay = jnp.exp(cum_last[..., 0, :])

    def step(state, xs):
        q_n, kv_n, d_n = xs
        o_n = jnp.einsum('bhcd,bhde->bhce', q_n, state)
        return d_n[..., None] * state + kv_n, o_n

    state0 = jnp.zeros((bsz, HGRN_HEADS, HGRN_HEAD_DIM, HGRN_HEAD_DIM), jnp.float32)
    _, o_inter = lax.scan(step, state0, (q_in, chunk_kv, decay))
    o = (o_intra + o_inter).transpose(1, 0, 3, 2, 4)
    return o.reshape(bsz, seq, HGRN_HEADS, HGRN_HEAD_DIM)


def _causal_depthwise_conv(x, w, b):
    y = lax.conv_general_dilated(
        x, w[:, None, :], window_strides=(1,), padding=[(CONV_WIDTH - 1, 0)],
        dimension_numbers=('NWC', 'WIO', 'NWC'), feature_group_count=x.shape[-1])
    return y + b


def _mixer_hgrn2_conv(h, w_in, w_out, lb, out_norm, dw_w, dw_b, ln_g, ln_b):
    bsz, seq, _ = h.shape
    proj = h @ w_in
    qz, fz, iz, gz, glu_a, glu_b = jnp.split(
        proj, [A_WIDTH, 2 * A_WIDTH, 3 * A_WIDTH, 4 * A_WIDTH, 4 * A_WIDTH + B_WIDTH], axis=-1)
    f = lb + (1.0 - lb) * jax.nn.sigmoid(fz.astype(jnp.float32))
    o = _hgrn2_chunkwise(jax.nn.silu(qz), f, iz)
    o = o * lax.rsqrt(jnp.mean(o * o, axis=-1, keepdims=True) + EPS)
    o_a = (o.reshape(bsz, seq, A_WIDTH) * out_norm.astype(jnp.float32)
           * jax.nn.silu(gz.astype(jnp.float32))).astype(h.dtype)
    c = glu_a * jax.nn.sigmoid(glu_b)
    c = _causal_depthwise_conv(c, dw_w, dw_b)
    o_b = jax.nn.silu(_layernorm(c, ln_g, ln_b))
    return jnp.concatenate([o_a, o_b], axis=-1) @ w_out


def _moba_attention(q, k, v):
    bsz, seq, n_h, hd = q.shape
    n_blk = -(-seq // MOBA_BLOCK)
    s_pad = n_blk * MOBA_BLOCK
    pad = s_pad - seq
    if pad:
        cfg = ((0, 0), (0, pad), (0, 0), (0, 0))
        q, k, v = jnp.pad(q, cfg), jnp.pad(k, cfg), jnp.pad(v, cfg)
    qh = q.transpose(0, 2, 1, 3)
    kb = k.transpose(0, 2, 1, 3).reshape(bsz, n_h, n_blk, MOBA_BLOCK, hd)
    vb = v.transpose(0, 2, 1, 3).reshape(bsz, n_h, n_blk, MOBA_BLOCK, hd)
    topk = min(MOBA_TOPK, n_blk - 1)
    scale = MOBA_HEAD_DIM ** -0.5
    if topk > 0:
        k_mean = jnp.mean(kb.astype(jnp.float32), axis=3)
        blk_score = jnp.einsum('bhqd,bhnd->bhqn', qh.astype(jnp.float32), k_mean)
        q_blk = jnp.arange(s_pad) // MOBA_BLOCK
        fully_past = jnp.arange(n_blk)[None, :] < q_blk[:, None]
        blk_score = jnp.where(fully_past, blk_score, -jnp.inf)
        _, idx = lax.top_k(blk_score, topk)
        idx = idx.astype(jnp.int32)
    else:
        idx = jnp.zeros((bsz, n_h, s_pad, 0), jnp.int32)
    n_qc = s_pad // MOBA_QCHUNK
    h_ar = jnp.arange(n_h)[:, None, None]

    def per_batch(args):
        q_b, kb_b, vb_b, idx_b = args

        def per_chunk(c):
            start = c * MOBA_QCHUNK
            q_c = lax.dynamic_slice_in_dim(q_b, start, MOBA_QCHUNK, axis=1)
            jq = start // MOBA_BLOCK
            k_own = lax.dynamic_index_in_dim(kb_b, jq, axis=1, keepdims=False)
            v_own = lax.dynamic_index_in_dim(vb_b, jq, axis=1, keepdims=False)
            qpos = start + jnp.arange(MOBA_QCHUNK)
            kpos = jq * MOBA_BLOCK + jnp.arange(MOBA_BLOCK)
            s_own = jnp.einsum('hqd,hpd->hqp', q_c, k_own).astype(jnp.float32) * scale
            s_own = jnp.where(kpos[None, :] <= qpos[:, None], s_own, -jnp.inf)
            if topk > 0:
                idx_c = lax.dynamic_slice_in_dim(idx_b, start, MOBA_QCHUNK, axis=1)
                kg = kb_b[h_ar, idx_c]
                vg = vb_b[h_ar, idx_c]
                s_sel = jnp.einsum('hqd,hqkpd->hqkp', q_c, kg).astype(jnp.float32) * scale
                s_sel = jnp.where((idx_c < jq)[..., None], s_sel, -jnp.inf)
                s_sel = s_sel.reshape(n_h, MOBA_QCHUNK, topk * MOBA_BLOCK)
                p = jax.nn.softmax(jnp.concatenate([s_sel, s_own], axis=-1), axis=-1).astype(v_own.dtype)
                p_sel = p[..., :topk * MOBA_BLOCK].reshape(n_h, MOBA_QCHUNK, topk, MOBA_BLOCK)
                o = (jnp.einsum('hqkp,hqkpe->hqe', p_sel, vg)
                     + jnp.einsum('hqp,hpe->hqe', p[..., topk * MOBA_BLOCK:], v_own))
            else:
                p = jax.nn.softmax(s_own, axis=-1).astype(v_own.dtype)
                o = jnp.einsum('hqp,hpe->hqe', p, v_own)
            return o

        out = lax.map(per_chunk, jnp.arange(n_qc))
        return out.transpose(1, 0, 2, 3).reshape(n_h, s_pad, hd)

    out = lax.map(per_batch, (qh, kb, vb, idx))
    return out.transpose(0, 2, 1, 3)[:, :seq]


def _mixer_moba_sgu(h, w_in, w_out, sgu_ln_g, sgu_ln_b, sgu_w, sgu_b):
    bsz, seq, _ = h.shape
    proj = h @ w_in
    qz, kz, vz, uz, zz = jnp.split(
        proj, [C_WIDTH, 2 * C_WIDTH, 3 * C_WIDTH, 3 * C_WIDTH + D_WIDTH], axis=-1)
    hs = (bsz, seq, MOBA_HEADS, MOBA_HEAD_DIM)
    o_c = _moba_attention(qz.reshape(hs), kz.reshape(hs), vz.reshape(hs)).reshape(bsz, seq, C_WIDTH)
    u = jax.nn.gelu(uz, approximate=False)
    z = jax.nn.gelu(zz, approximate=False).reshape(bsz, seq, SGU_GROUPS, SGU_GROUP_DIM)
    z = _layernorm(z, sgu_ln_g.reshape(SGU_GROUPS, SGU_GROUP_DIM), sgu_ln_b.reshape(SGU_GROUPS, SGU_GROUP_DIM))
    zc = z.reshape(bsz, seq // SGU_CHUNK, SGU_CHUNK, SGU_GROUPS, SGU_GROUP_DIM)
    w_s = jnp.where(jnp.tril(jnp.ones((SGU_CHUNK, SGU_CHUNK), dtype=bool)), sgu_w, 0.0)
    mixed = jnp.einsum('gts,bnsgc->bntgc', w_s.astype(zc.dtype), zc) + sgu_b.T[None, None, :, :, None]
    o_d = u * mixed.reshape(bsz, seq, D_WIDTH)
    return jnp.concatenate([o_c, o_d], axis=-1) @ w_out


def _cross_attention(h, mem_n, wq, wkv, wo):
    bsz, seq, _ = h.shape
    q = (h @ wq).reshape(bsz, seq, XA_HEADS, XA_HEAD_DIM)
    k, v = jnp.split(mem_n @ wkv, 2, axis=-1)
    k = k.reshape(bsz, -1, XA_HEADS, XA_HEAD_DIM)
    v = v.reshape(bsz, -1, XA_HEADS, XA_HEAD_DIM)
    s = jnp.einsum('bshd,bmhd->bhsm', q, k).astype(jnp.float32) * (XA_HEAD_DIM ** -0.5)
    p = jax.nn.softmax(s, axis=-1).astype(v.dtype)
    o = jnp.einsum('bhsm,bmhd->bshd', p, v).reshape(bsz, seq, D_MODEL)
    return o @ wo


def _swiglu(h, w_in, w_out):
    a, g = jnp.split(h @ w_in, 2, axis=-1)
    return (jax.nn.silu(a) * g) @ w_out


def setup_inputs(seed: int = 0) -> dict:
    key = jax.random.key(seed)
    ks = jax.random.split(key, 32)

    def nrm(k, shape, scale):
        return jax.random.normal(k, shape, jnp.float32) * scale

    def gain(k, shape):
        return 1.0 + 0.02 * jax.random.normal(k, shape, jnp.float32)

    return {
        'x': nrm(ks[0], (BATCH, SEQ, D_MODEL), 1.0),
        'mem': nrm(ks[1], (BATCH, MEM_LEN, D_MODEL), 1.0),
        'norm_mix': gain(ks[2], (DEPTH, D_MODEL)),
        'norm_xattn': gain(ks[3], (DEPTH, D_MODEL)),
        'norm_ffn': gain(ks[4], (DEPTH, D_MODEL)),
        'mem_norm': gain(ks[5], (D_MODEL,)),
        'final_norm': gain(ks[6], (D_MODEL,)),
        'w_in_ab': nrm(ks[7], (N_EVEN, D_MODEL, EVEN_IN), D_MODEL ** -0.5),
        'w_out_ab': nrm(ks[8], (N_EVEN, MIX_WIDTH, D_MODEL), MIX_WIDTH ** -0.5),
        'hgrn_lower_bounds': nrm(ks[9], (DEPTH + 1, A_WIDTH), 0.1),
        'hgrn_out_norm': gain(ks[10], (N_EVEN, A_WIDTH)),
        'conv_dw_w': nrm(ks[11], (N_EVEN, CONV_WIDTH, B_WIDTH), CONV_WIDTH ** -0.5),
        'conv_dw_b': nrm(ks[12], (N_EVEN, B_WIDTH), 0.02),
        'conv_ln_g': gain(ks[13], (N_EVEN, B_WIDTH)),
        'conv_ln_b': nrm(ks[14], (N_EVEN, B_WIDTH), 0.02),
        'w_in_cd': nrm(ks[15], (N_ODD, D_MODEL, ODD_IN), D_MODEL ** -0.5),
        'w_out_cd': nrm(ks[16], (N_ODD, MIX_WIDTH, D_MODEL), MIX_WIDTH ** -0.5),
        'sgu_ln_g': gain(ks[17], (N_ODD, D_WIDTH)),
        'sgu_ln_b': nrm(ks[18], (N_ODD, D_WIDTH), 0.02),
        'sgu_w': nrm(ks[19], (N_ODD, SGU_GROUPS, SGU_CHUNK, SGU_CHUNK), SGU_CHUNK ** -0.5),
        'sgu_b': gain(ks[20], (N_ODD, SGU_GROUPS, SGU_CHUNK)),
        'xa_wq': nrm(ks[21], (DEPTH, D_MODEL, D_MODEL), D_MODEL ** -0.5),
        'xa_wkv': nrm(ks[22], (DEPTH, D_MODEL, 2 * D_MODEL), D_MODEL ** -0.5),
        'xa_wo': nrm(ks[23], (DEPTH, D_MODEL, D_MODEL), D_MODEL ** -0.5),
        'ffn_w_in': nrm(ks[24], (DEPTH, D_MODEL, 2 * FFN_HIDDEN), D_MODEL ** -0.5),
        'ffn_w_out': nrm(ks[25], (DEPTH, FFN_HIDDEN, D_MODEL), FFN_HIDDEN ** -0.5),
    }


def reference(x, mem, norm_mix, norm_xattn, norm_ffn, mem_norm, final_norm,
              w_in_ab, w_out_ab, hgrn_lower_bounds, hgrn_out_norm,
              conv_dw_w, conv_dw_b, conv_ln_g, conv_ln_b,
              w_in_cd, w_out_cd, sgu_ln_g, sgu_ln_b, sgu_w, sgu_b,
              xa_wq, xa_wkv, xa_wo, ffn_w_in, ffn_w_out):
    lb_all = jnp.cumsum(jax.nn.softmax(hgrn_lower_bounds.astype(jnp.float32), axis=0), axis=0)
    mem_n = _rmsnorm(mem, mem_norm)
    for l in range(DEPTH):
        h = _rmsnorm(x, norm_mix[l])
        if l % 2 == 0:
            e = l // 2
            x = x + _mixer_hgrn2_conv(h, w_in_ab[e], w_out_ab[e], lb_all[l], hgrn_out_norm[e],
                                      conv_dw_w[e], conv_dw_b[e], conv_ln_g[e], conv_ln_b[e])
        else:
            o = l // 2
            x = x + _mixer_moba_sgu(h, w_in_cd[o], w_out_cd[o], sgu_ln_g[o], sgu_ln_b[o],
                                    sgu_w[o], sgu_b[o])
        x = x + _cross_attention(_rmsnorm(x, norm_xattn[l]), mem_n, xa_wq[l], xa_wkv[l], xa_wo[l])
        x = x + _swiglu(_rmsnorm(x, norm_ffn[l]), ffn_w_in[l], ffn_w_out[l])
    return _rmsnorm(x, final_norm)
```

```python
from contextlib import ExitStack
import numpy as np
import concourse.bass as bass
import concourse.mybir as mybir
from concourse.bass_utils import run_bass_kernel_spmd

F32 = mybir.dt.float32
BF16 = mybir.dt.bfloat16
AF = mybir.ActivationFunctionType
ALU = mybir.AluOpType
AX = mybir.AxisListType

ENG = ("pe", "act", "dve", "pool", "sp")
SEM_ROLL = 12000
N_DMA_SEMS = 8
EMBED_WAIT = True


import types


def _freeze(fn):
    if getattr(fn, "__closure__", None) is None:
        return fn
    cells = []
    for c in fn.__closure__:
        try:
            cells.append(types.CellType(c.cell_contents))
        except ValueError:
            cells.append(c)
    return types.FunctionType(fn.__code__, fn.__globals__, fn.__name__, fn.__defaults__, tuple(cells))


class Res:
    __slots__ = ("name", "w", "r", "wx", "excl")

    def __init__(self, name, excl=False):
        self.name = name
        self.excl = excl
        self.w = None
        self.r = []
        self.wx = []


class Op:
    __slots__ = ("eng", "fn", "waits", "signal", "sig_no", "dma_slot", "dma_cnt", "idx", "ka")

    def __init__(self, eng, fn):
        self.eng = eng
        self.fn = _freeze(fn)
        self.waits = []
        self.signal = False
        self.sig_no = None
        self.dma_slot = None
        self.dma_cnt = None
        self.idx = None
        self.ka = 0


class Prog:
    def __init__(self, nc):
        self.nc = nc
        self.ops = {e: [] for e in ENG}
        self.dma_rr = {e: 0 for e in ENG}
        self.dma_count = {e: [0] * N_DMA_SEMS for e in ENG}
        self.pending = {}
        self.keepalive = None

    def barrier(self):
        toks = []
        for e in ENG:
            for o in reversed(self.ops[e]):
                if o.dma_slot is None:
                    toks.append(("c", e, o.idx))
                    break
            for slot, cnt in enumerate(self.dma_count[e]):
                if cnt > 0:
                    toks.append(("d", e, slot, cnt))
        self.pending = {e: list(toks) for e in ENG}

    def _pend(self, eng):
        t = self.pending.pop(eng, [])
        return [d for d in t if not (d[0] == "c" and d[1] == eng)]

    def _deps(self, eng, reads, writes, is_dma):
        deps = []
        for r in reads:
            if r.w is not None:
                deps.append(r.w)
            deps.extend(r.wx)
            if r.excl:
                deps.extend(t for t in r.r if t[1] != eng)
        for w in writes:
            if w.w is not None:
                deps.append(w.w)
            deps.extend(w.wx)
            deps.extend(w.r)
        out = []
        for d in deps:
            if d[0] == "c":
                if d[1] == eng and not is_dma:
                    if eng == "pe":
                        continue
                out.append(d)
            else:
                out.append(d)
        return out

    def op(self, eng, fn, reads=(), writes=()):
        o = Op(eng, fn)
        o.idx = len(self.ops[eng])
        o.waits = self._deps(eng, reads, writes, False)
        if eng != "pe":
            raw = set()
            for r in reads:
                if r.w is not None and r.w[0] == "c" and r.w[1] == eng:
                    raw.add(r.w)
            o.waits = [d for d in o.waits if not (d[0] == "c" and d[1] == eng and d not in raw)]
        o.waits = o.waits + self._pend(eng)
        self.ops[eng].append(o)
        tok = ("c", eng, o.idx)
        for r in reads:
            r.r.append(tok)
        for w in writes:
            w.w = tok
            w.r = []
            w.wx = []
        return o

    def dma(self, qeng, fn, reads=(), writes=(), extra=False):
        o = Op(qeng, fn)
        o.idx = len(self.ops[qeng])
        o.waits = self._deps(qeng, reads, () if extra else writes, True) + self._pend(qeng)
        slot = self.dma_rr[qeng]
        self.dma_rr[qeng] = (slot + 1) % N_DMA_SEMS
        prev = self.dma_count[qeng][slot]
        if prev > 0:
            o.waits.append(("d", qeng, slot, prev))
        self.dma_count[qeng][slot] = prev + 1
        o.dma_slot = slot
        o.dma_cnt = prev + 1
        self.ops[qeng].append(o)
        tok = ("d", qeng, slot, prev + 1)
        for r in reads:
            r.r.append(tok)
        for w in writes:
            if extra:
                w.wx.append(tok)
            else:
                w.w = tok
                w.r = []
                w.wx = []
        return o

    def emit(self, final_waits=()):
        nc = self.nc
        for e in ENG:
            for o in self.ops[e]:
                for d in o.waits:
                    if d[0] == "c":
                        self.ops[d[1]][d[2]].signal = True
        for d in final_waits:
            if d[0] == "c":
                self.ops[d[1]][d[2]].signal = True
        nsig = {}
        for e in ENG:
            c = 0
            for o in self.ops[e]:
                if o.signal:
                    c += 1
                    o.sig_no = c
            nsig[e] = c
        from contextlib import ExitStack
        with ExitStack() as es:
            csem = {}
            for e in ENG:
                n = max(1, -(-nsig[e] // SEM_ROLL))
                csem[e] = [es.enter_context(nc.semaphore(f"c_{e}_{i}")) for i in range(n)]
            dsem = {}
            for e in ENG:
                if any(self.dma_count[e]):
                    dsem[e] = [es.enter_context(nc.semaphore(f"d_{e}_{i}")) for i in range(N_DMA_SEMS)]
            block = es.enter_context(nc.Block())

            def lower(d):
                if d[0] == "c":
                    s = self.ops[d[1]][d[2]].sig_no - 1
                    return (csem[d[1]][s // SEM_ROLL], s % SEM_ROLL + 1)
                return (dsem[d[1]][d[2]], 16 * d[3])

            def run(e, engobj):
                seen = {}
                for o in self.ops[e]:
                    need = {}
                    for d in o.waits:
                        sem, val = lower(d)
                        k = id(sem)
                        if seen.get(k, 0) >= val:
                            continue
                        if k not in need or need[k][1] < val:
                            need[k] = (sem, val)
                    if e == "pe" and o.ka and self.keepalive is not None:
                        for _ in range(o.ka):
                            self.keepalive(engobj)
                    items_ = list(need.items())
                    embed = None
                    if EMBED_WAIT and o.dma_slot is None and e in ("pe", "act", "dve") and items_:
                        embed = items_.pop()
                    for k, (sem, val) in items_:
                        engobj.wait_ge(sem, val)
                        seen[k] = val
                    ins = o.fn(engobj)
                    if embed is not None:
                        k, (sem, val) = embed
                        ins._wait_ge(sem, val)
                        seen[k] = val
                    if o.dma_slot is not None:
                        ins.then_inc(dsem[e][o.dma_slot], 16)
                    elif o.signal:
                        s = o.sig_no - 1
                        ins.then_inc(csem[e][s // SEM_ROLL], 1)
                if e == "sp":
                    for d in final_waits:
                        sem, val = lower(d)
                        engobj.wait_ge(sem, val)

            @block.tensor
            def _(t):
                run("pe", t)

            @block.scalar
            def _(t):
                run("act", t)

            @block.vector
            def _(t):
                run("dve", t)

            @block.gpsimd
            def _(t):
                run("pool", t)

            @block.sync
            def _(t):
                run("sp", t)
S = 2048
D = 1024
KC = 8
TG = 512
NTG = 4
MEM = 256
FFH = 2816
EPS = 1e-6
WB_ELEMS = 4096
NWB = 3


def _prod(s):
    r = 1
    for v in s:
        r *= v
    return r


class KB:
    def __init__(self, nc, es, nseq, stages):
        self.nc = nc
        self.es = es
        self.P = Prog(nc)
        self.nseq = nseq
        self.stages = stages
        self.uid = 0
        self.ps = [es.enter_context(nc.psum_tensor(f"psb{i}", [128, 512], F32)) for i in range(8)]
        self.ps_res = [Res(f"ps{i}", excl=True) for i in range(8)]
        self.ps_rr = 0
        self.ARENA = 207 * 1024
        self.arena = es.enter_context(nc.sbuf_tensor("arena", [128, self.ARENA // 4], F32))
        self.aoff = 0

    def alloc(self, free_shape, dtype, name=None):
        esz = 4 if dtype == F32 else 2
        n = _prod(free_shape)
        sz = (n * esz + 63) // 64 * 64
        assert self.aoff + sz <= self.ARENA, f"arena overflow {name} {self.aoff + sz}"
        v = self.arena[:, self.aoff // 4:(self.aoff + sz) // 4]
        self.aoff += sz
        self.peak = max(getattr(self, "peak", 0), self.aoff)
        if dtype != F32:
            v = v.bitcast(dtype)
        v = v[:, 0:n]
        if len(free_shape) == 2:
            v = v.rearrange("p (a b) -> p a b", a=free_shape[0])
        elif len(free_shape) == 3:
            v = v.rearrange("p (a b c) -> p a b c", a=free_shape[0], b=free_shape[1])
        self.uid += 1
        return v, Res(f"{name}_{self.uid}")

    def mark(self):
        return self.aoff

    def release(self, m):
        self.aoff = m

    def bank(self):
        i = self.ps_rr
        self.ps_rr = (i + 1) % 5
        return self.ps[i][:], self.ps_res[i]

    def acc_bank(self):
        self.acc_rr = 1 - getattr(self, "acc_rr", 1)
        i = 6 + self.acc_rr
        return self.ps[i][:], self.ps_res[i]

    def op(self, eng, fn, reads=(), writes=()):
        return self.P.op(eng, fn, reads=reads, writes=writes)

    def mm(self, out, ores, lhsT, rhs, start, stop, reads, skip=False, ka=0):
        if skip:
            o = self.P.op("pe", lambda e: e.matmul(out, lhsT, rhs, start=start, stop=stop, skip_group_check=True), reads=reads, writes=[ores])
        else:
            o = self.P.op("pe", lambda e: e.matmul(out, lhsT, rhs, start=start, stop=stop), reads=reads, writes=[ores])
        o.ka = ka

    def barrier(self):
        self.P.barrier()

    def init_wring(self, n=NWB):
        self.wb = []
        for i in range(n):
            v, r = self.alloc([WB_ELEMS], BF16, f"wb{i}")
            self.wb.append((v, r))
        self.wrr = 0

    def wload(self, src_ap, shape):
        v, r = self.wb[self.wrr]
        self.wrr = (self.wrr + 1) % len(self.wb)
        n = _prod(shape)
        assert n <= WB_ELEMS
        vv = v[:, 0:n]
        if len(shape) == 2:
            vv = vv.rearrange("p (a b) -> p a b", a=shape[0])
        elif len(shape) == 3:
            vv = vv.rearrange("p (a b c) -> p a b c", a=shape[0], b=shape[1])
        if isinstance(src_ap, list):
            for i, sa in enumerate(src_ap):
                self.P.dma("pool", lambda e, i=i, sa=sa: e.dma_start(out=vv[:, :, i, :], in_=sa), writes=[r], extra=(i > 0))
        else:
            self.P.dma("pool", lambda e: e.dma_start(out=vv, in_=src_ap), writes=[r])
        return vv, r

    def setup_consts(self, d):
        nc = self.nc
        self.identf, self.r_identf = self.alloc([128], F32, "identf")
        self.identb, self.r_identb = self.alloc([128], BF16, "identb")
        self.onesb, self.r_onesb = self.alloc([128], BF16, "onesb")
        idf, idb, onb = self.identf, self.identb, self.onesb
        self.op("pool", lambda e: e.memset(idf, 0.0), writes=[self.r_identf])
        self.op("pool", lambda e: e.affine_select(idf, idf, pattern=[[-1, 128]], compare_op=ALU.not_equal,
                                                   fill=1.0, base=0, channel_multiplier=1),
                reads=[self.r_identf], writes=[self.r_identf])
        self.op("dve", lambda e: e.tensor_copy(idb, idf), reads=[self.r_identf], writes=[self.r_identb])
        self.op("pool", lambda e: e.memset(onb, 1.0), writes=[self.r_onesb])
        self.pv, self.r_pv = self.alloc([256], F32, "pv")
        self.g32, self.r_g32 = self.alloc([64], F32, "g32")
        self.hp, self.r_hp = self.alloc([32], F32, "hp")
        self.eps_vals = [1024, 128, 512, 1]
        self.epsc, self.r_epsc = self.alloc([4], F32, "epsc")
        self.nsq = self.alloc([8, 512], BF16, "nsq")
        self.nrr = self.alloc([512], F32, "nrr")
        m_tmp = self.mark()
        rowsA, rA = self.alloc([128], F32, "rowsA")
        rowsB, rB = self.alloc([128], F32, "rowsB")
        self.op("pool", lambda e: e.memset(rowsA, 0.0), writes=[rA])
        self.op("pool", lambda e: e.memset(rowsB, 0.0), writes=[rB])
        specs = [
            (d["norm_mix"].rearrange("l (k p) -> (l k) p", p=128), 0, 16),
            (d["norm_xattn"].rearrange("l (k p) -> (l k) p", p=128), 16, 16),
            (d["norm_ffn"].rearrange("l (k p) -> (l k) p", p=128), 32, 16),
            (d["mem_norm"].rearrange("(k p) -> k p", p=128), 48, 8),
            (d["final_norm"].rearrange("(k p) -> k p", p=128), 56, 8),
            (d["hgrn_lower_bounds"].rearrange("l (k p) -> (l k) p", p=128), 64, 12),
            (d["hgrn_out_norm"].rearrange("l (k p) -> (l k) p", p=128), 76, 4),
            (d["conv_dw_b"].rearrange("l (k p) -> (l k) p", p=128), 80, 4),
            (d["conv_ln_g"].rearrange("l (k p) -> (l k) p", p=128), 84, 4),
            (d["conv_ln_b"].rearrange("l (k p) -> (l k) p", p=128), 88, 4),
            (d["sgu_ln_g"].rearrange("l (k p) -> (l k) p", p=128), 92, 4),
            (d["sgu_ln_b"].rearrange("l (k p) -> (l k) p", p=128), 96, 4),
        ]
        for src, r0, n in specs:
            self.P.dma("sp", lambda e, src=src, r0=r0, n=n: e.dma_start(out=rowsA[r0:r0 + n, :], in_=src), writes=[rA])
        srcB = d["conv_dw_w"].rearrange("l j (k p) -> (l j k) p", p=128)
        self.P.dma("sp", lambda e: e.dma_start(out=rowsB[0:124, :], in_=srcB), writes=[rB])
        pv = self.pv
        b0, r0_ = self.bank()
        self.op("pe", lambda e: e.transpose(b0[:, 0:128], rowsA, idf), reads=[rA, self.r_identf], writes=[r0_])
        self.op("pe", lambda e: e.transpose(b0[:, 128:256], rowsB, idf), reads=[rB, self.r_identf], writes=[r0_])
        self.op("dve", lambda e: e.tensor_copy(pv, b0[:, 0:256]), reads=[r0_], writes=[self.r_pv])
        g32 = self.g32
        self.op("dve", lambda e: e.tensor_scalar(g32, pv[:, 0:64], 32.0, None, op0=ALU.mult), reads=[self.r_pv], writes=[self.r_g32])
        hp = self.hp
        self.op("act", lambda e: e.activation(hp[:, 12:24], pv[:, 64:76], AF.Exp), reads=[self.r_pv], writes=[self.r_hp])
        self.op("dve", lambda e: e.tensor_tensor(hp[:, 24:28], hp[:, 12:16], hp[:, 16:20], op=ALU.add), reads=[self.r_hp], writes=[self.r_hp])
        self.op("dve", lambda e: e.tensor_tensor(hp[:, 24:28], hp[:, 24:28], hp[:, 20:24], op=ALU.add), reads=[self.r_hp], writes=[self.r_hp])
        self.op("dve", lambda e: e.reciprocal(hp[:, 24:28], hp[:, 24:28]), reads=[self.r_hp], writes=[self.r_hp])
        self.op("dve", lambda e: e.tensor_tensor(hp[:, 0:4], hp[:, 12:16], hp[:, 24:28], op=ALU.mult), reads=[self.r_hp], writes=[self.r_hp])
        self.op("dve", lambda e: e.tensor_scalar(hp[:, 4:8], hp[:, 0:4], -1.0, 1.0, op0=ALU.mult, op1=ALU.add), reads=[self.r_hp], writes=[self.r_hp])
        self.op("dve", lambda e: e.tensor_scalar(hp[:, 8:12], pv[:, 76:80], float(np.sqrt(128.0)), None, op0=ALU.mult), reads=[self.r_pv, self.r_hp], writes=[self.r_hp])
        for i, v in enumerate(self.eps_vals):
            self.op("pool", lambda e, i=i, v=v: e.memset(self.epsc[:, i:i + 1], float(v * EPS)), writes=[self.r_epsc])
        self.barrier()
        self.release(m_tmp)
        self.kaw, r_kaw = self.alloc([512], BF16, "kaw")
        self.op("pool", lambda e: e.memset(self.kaw, 1.0), writes=[r_kaw])
        ka_out, kaw, onesb = self.ps[5][:], self.kaw, self.onesb
        self.P.keepalive = lambda pe: pe.matmul(ka_out, onesb, kaw, start=True, stop=True)
        self.barrier()
        self.consts_mark = self.mark()

    def gcol(self, base, l, k):
        c = base + l * 8 + k
        return self.g32[:, c:c + 1]

    def load_T(self, src, ntok, dst, r_dst_fn):
        m = self.mark()
        stg = [self.alloc([1024], F32, f"stg{i}") for i in range(2)]
        for i in range(ntok // 128):
            sv, sr = stg[i % 2]
            self.P.dma("sp", lambda e, i=i, sv=sv: e.dma_start(out=sv, in_=src[i * 128:(i + 1) * 128, :]), writes=[sr])
            for half in range(2):
                b, br = self.bank()
                for j in range(4):
                    k = half * 4 + j
                    self.op("pe", lambda e, b=b, j=j, k=k, sv=sv: e.transpose(b[:, j * 128:(j + 1) * 128], sv[:, k * 128:(k + 1) * 128], self.identf),
                            reads=[sr, self.r_identf], writes=[br])
                dv = dst[:, half * 4:half * 4 + 4, i * 128:(i + 1) * 128]
                bv = b.rearrange("p (a b) -> p a b", a=4)
                if half == 0:
                    self.op("dve", lambda e, dv=dv, bv=bv: e.tensor_copy(dv, bv), reads=[br], writes=r_dst_fn(i, half))
                else:
                    self.op("act", lambda e, dv=dv, bv=bv: e.activation(dv, bv, AF.Copy), reads=[br], writes=r_dst_fn(i, half))
        self.barrier()
        self.release(m)

    def rstd_rep(self, srcs, reads, n, sq, r_sq, rr, r_rr):
        nk = len(srcs)
        for k, s in enumerate(srcs):
            self.op("act", lambda e, k=k, s=s: e.activation(sq[:, k, :], s, AF.Square), reads=reads, writes=[r_sq])
        b, br = self.bank()
        for k in range(nk):
            self.mm(b, br, self.onesb, sq[:, k, :], k == 0, k == nk - 1, [r_sq, self.r_onesb])
        self.rsqrt_ps(b, float(n * EPS), rr, r_rr, br)

    def rsqrt_ps(self, src, eps_tot, rv, r_rv, r_src):
        self.op("act", lambda e: e.activation(rv, src, AF.Ln, bias=self.epsc[:, self.eps_idx(eps_tot)]), reads=[r_src, self.r_epsc], writes=[r_rv])
        self.op("act", lambda e: e.activation(rv, rv, AF.Exp, scale=-0.5), reads=[r_rv], writes=[r_rv])

    def eps_idx(self, v):
        i = self.eps_vals.index(round(v / EPS))
        return slice(i, i + 1)

    def norm_to(self, xT, rx_fn, gbase, l, hT, rh_fn, ntok=S):
        if not hasattr(self, "nsq"):
            raise RuntimeError("norm temps not allocated")
        sq, r_sq = self.nsq
        rrs = [self.nrr, self.nrr]
        tg_sz = min(512, ntok)
        for tg in range(ntok // tg_sz):
            sl = slice(tg * tg_sz, (tg + 1) * tg_sz)
            rr, r_rr = rrs[tg % 2]
            sqv = sq[:, :, 0:tg_sz]
            self.op("act", lambda e, sl=sl, sqv=sqv: e.activation(sqv, xT[:, :, sl], AF.Square), reads=rx_fn(None, tg), writes=[r_sq])
            b, br = self.bank()
            for k in range(8):
                self.mm(b[:, 0:tg_sz], br, self.onesb, sq[:, k, 0:tg_sz], k == 0, k == 7, [r_sq, self.r_onesb])
            rv = rr[:, 0:tg_sz]
            self.rsqrt_ps(b[:, 0:tg_sz], float(1024 * EPS), rv, r_rr, br)
            for k in range(8):
                g = self.gcol(gbase, l, k)
                self.op("dve", lambda e, k=k, sl=sl, g=g, rv=rv: e.scalar_tensor_tensor(hT[:, k, sl], xT[:, k, sl], g, rv, op0=ALU.mult, op1=ALU.mult),
                        reads=rx_fn(k, tg) + [r_rr, self.r_g32], writes=rh_fn(tg))

    def final_store(self, xT, rx_fn, out_ap):
        m = self.mark()
        sq, r_sq = self.alloc([8, 512], BF16, "fsq")
        rr, r_rr = self.alloc([512], F32, "frr")
        yT, r_y = self.alloc([8, 512], F32, "fy")
        stg = [self.alloc([1024], F32, f"fstg{i}") for i in range(2)]
        outs = []
        n = 0
        for tg in range(NTG):
            sl = slice(tg * 512, (tg + 1) * 512)
            self.op("act", lambda e, sl=sl: e.activation(sq, xT[:, :, sl], AF.Square), reads=rx_fn(None, tg), writes=[r_sq])
            b, br = self.bank()
            for k in range(8):
                self.mm(b, br, self.onesb, sq[:, k, :], k == 0, k == 7, [r_sq, self.r_onesb])
            self.rsqrt_ps(b, float(1024 * EPS), rr, r_rr, br)
            for k in range(8):
                g = self.g32[:, 56 + k:57 + k]
                self.op("dve", lambda e, k=k, sl=sl, g=g: e.scalar_tensor_tensor(yT[:, k, :], xT[:, k, sl], g, rr, op0=ALU.mult, op1=ALU.mult),
                        reads=rx_fn(k, tg) + [r_rr, self.r_g32], writes=[r_y])
            for tt in range(4):
                sv, sr = stg[n % 2]
                n += 1
                for half in range(2):
                    b2, br2 = self.bank()
                    for j in range(4):
                        k = half * 4 + j
                        self.op("pe", lambda e, b2=b2, j=j, k=k, tt=tt: e.transpose(b2[:, j * 128:(j + 1) * 128], yT[:, k, tt * 128:(tt + 1) * 128], self.identf),
                                reads=[r_y, self.r_identf], writes=[br2])
                    if half == 0:
                        self.op("dve", lambda e, sv=sv, b2=b2: e.tensor_copy(sv[:, 0:512], b2), reads=[br2], writes=[sr])
                    else:
                        self.op("act", lambda e, sv=sv, b2=b2: e.activation(sv[:, 512:1024], b2, AF.Copy), reads=[br2], writes=[sr])
                t0 = tg * 512 + tt * 128
                o = self.P.dma("sp", lambda e, sv=sv, t0=t0: e.dma_start(out=out_ap[t0:t0 + 128, :], in_=sv), reads=[sr])
                outs.append(("d", "sp", o.dma_slot, o.dma_cnt))
        self.barrier()
        self.release(m)
        return outs

    def ffn(self, xT, rx_fn, hT, rh_fn, w_in, w_out):
        m = self.mark()
        self.init_wring(7)
        hid = [self.alloc([4, 512], BF16, f"hid{i}") for i in range(3)]
        sa = [self.alloc([512], F32, f"sa{i}") for i in range(4)]
        w_in_v = w_in.rearrange("(k p) n -> p k n", p=128)
        w_out_v = w_out.rearrange("(j p) n -> p j n", p=128)
        nchunks = FFH // 128
        step = 0
        for c0 in range(0, nchunks, 4):
            nj = min(4, nchunks - c0)
            wa, r_wa = self.wload(w_in_v[:, :, c0 * 128:(c0 + nj) * 128], [8, nj * 128])
            wg, r_wg = self.wload(w_in_v[:, :, FFH + c0 * 128:FFH + (c0 + nj) * 128], [8, nj * 128])
            wo, r_wo = self.wload(w_out_v[:, c0:c0 + nj, :], [nj, 1024])
            for tg in range(NTG):
                sl = slice(tg * 512, (tg + 1) * 512)
                hv, r_hv = hid[step % 3]
                step += 1
                for j in range(nj):
                    pa, r_pa = self.bank()
                    for k in range(8):
                        self.mm(pa, r_pa, wa[:, k, j * 128:(j + 1) * 128], hT[:, k, sl], k == 0, k == 7, [r_wa] + rh_fn(tg))
                    pg, r_pg = self.bank()
                    for k in range(8):
                        self.mm(pg, r_pg, wg[:, k, j * 128:(j + 1) * 128], hT[:, k, sl], k == 0, k == 7, [r_wg] + rh_fn(tg))
                    sv, r_sv = sa[j % 4]
                    self.op("act", lambda e, sv=sv, pa=pa: e.activation(sv, pa, AF.Silu), reads=[r_pa], writes=[r_sv])
                    self.op("dve", lambda e, hv=hv, j=j, sv=sv, pg=pg: e.tensor_tensor(hv[:, j, :], sv, pg, op=ALU.mult),
                            reads=[r_sv, r_pg], writes=[r_hv])
                for oc in range(8):
                    po, r_po = self.bank()
                    for j in range(nj):
                        self.mm(po, r_po, wo[:, j, oc * 128:(oc + 1) * 128], hv[:, j, :], j == 0, j == nj - 1, [r_wo, r_hv])
                    self.op("dve", lambda e, oc=oc, sl=sl, po=po: e.tensor_tensor(xT[:, oc, sl], xT[:, oc, sl], po, op=ALU.add),
                            reads=[r_po] + rx_fn(oc, tg), writes=rx_fn(oc, tg))
        self.barrier()
        self.release(m)

    def evac(self, dst, src, reads, writes):
        self._ev = getattr(self, "_ev", 0) + 1
        if self._ev % 2 == 0:
            self.op("dve", lambda e: e.tensor_copy(dst, src), reads=reads, writes=writes)
        else:
            self.op("act", lambda e: e.activation(dst, src, AF.Copy), reads=reads, writes=writes)

    def init_xa(self):
        self.memnT, self.r_memn = self.alloc([8, MEM], BF16, "memnT")

    def prep_mem(self, mem_src):
        m = self.mark()
        memT, r_memT = self.alloc([8, MEM], F32, "memT")
        self.load_T(mem_src, MEM, memT, lambda i, half: [r_memT])
        self.norm_to(memT, lambda k, tg: [r_memT], 48, 0, self.memnT, lambda tg: [self.r_memn], ntok=MEM)
        self.barrier()
        self.release(m)

    def xattn(self, d, l):
        m = self.mark()
        self.init_wring(3)
        hT = self.hT
        qT, _ = self.alloc([8, S], BF16, "qT")
        r_q = [Res(f"q{t}") for t in range(NTG)]
        self.kT, self.r_kT = self.alloc([8, MEM], BF16, "kT")
        self.vtok, self.r_vtok = self.alloc([2, D], BF16, "vtok")
        wkv_v = d["xa_wkv"][l].rearrange("(k p) n -> p k n", p=128)
        wq_v = d["xa_wq"][l].rearrange("(k p) n -> p k n", p=128)
        wo_v = d["xa_wo"][l].rearrange("(k p) n -> p k n", p=128)
        for blk in range(2):
            w, rw = self.wload(wkv_v[:, :, blk * 512:(blk + 1) * 512], [8, 512])
            for j in range(4):
                c = blk * 4 + j
                b, br = self.bank()
                for k in range(8):
                    self.mm(b[:, 0:MEM], br, w[:, k, j * 128:(j + 1) * 128], self.memnT[:, k, :], k == 0, k == 7, [rw, self.r_memn])
                self.evac(self.kT[:, c, :], b[:, 0:MEM], [br], [self.r_kT])
        for blk in range(2):
            w, rw = self.wload(wkv_v[:, :, D + blk * 512:D + (blk + 1) * 512], [8, 512])
            for mc in range(2):
                b, br = self.bank()
                for k in range(8):
                    self.mm(b, br, self.memnT[:, k, mc * 128:(mc + 1) * 128], w[:, k, :], k == 0, k == 7, [rw, self.r_memn])
                self.evac(self.vtok[:, mc, blk * 512:(blk + 1) * 512], b, [br], [self.r_vtok])
        for blk in range(2):
            w, rw = self.wload(wq_v[:, :, blk * 512:(blk + 1) * 512], [8, 512])
            for tg in range(NTG):
                sl = slice(tg * 512, (tg + 1) * 512)
                for j in range(4):
                    c = blk * 4 + j
                    b, br = self.bank()
                    for k in range(8):
                        self.mm(b, br, w[:, k, j * 128:(j + 1) * 128], hT[:, k, sl], k == 0, k == 7, [rw] + self.rh_fn(tg))
                    self.evac(qT[:, c, sl], b, [br], [r_q[tg]])
        NPP = 3
        pT = [[self.alloc([512], BF16, f"pT{i}{j}") for j in range(2)] for i in range(NPP)]
        rec = [self.alloc([512], F32, f"rec{i}") for i in range(2)]
        items = [(tg, h) for tg in range(NTG) for h in range(4)]

        def scores(i):
            tg, h = items[i]
            sl = slice(tg * 512, (tg + 1) * 512)
            pp = pT[i % NPP]
            for mc in range(2):
                b, br = self.bank()
                for dc in range(2):
                    self.mm(b, br, self.kT[:, 2 * h + dc, mc * 128:(mc + 1) * 128], qT[:, 2 * h + dc, sl], dc == 0, dc == 1, [self.r_kT, r_q[tg]])
                pv_, r_pv_ = pp[mc]
                self.op("act", lambda e: e.activation(pv_, b, AF.Exp, scale=1.0 / 16.0), reads=[br], writes=[r_pv_])

        scores(0)
        for i, (tg, h) in enumerate(items):
            sl = slice(tg * 512, (tg + 1) * 512)
            if i + 1 < len(items):
                scores(i + 1)
            pp = pT[i % NPP]
            rc, r_rc = rec[i % 2]
            bd, brd = self.bank()
            for mc in range(2):
                self.mm(bd, brd, self.onesb, pp[mc][0], mc == 0, mc == 1, [self.r_onesb, pp[mc][1]])
            self.op("act", lambda e: e.activation(rc, bd, AF.Ln), reads=[brd], writes=[r_rc])
            self.op("act", lambda e: e.activation(rc, rc, AF.Exp, scale=-1.0), reads=[r_rc], writes=[r_rc])
            for dc in range(2):
                bo, bro = self.bank()
                for mc in range(2):
                    self.mm(bo, bro, self.vtok[:, mc, (2 * h + dc) * 128:(2 * h + dc + 1) * 128], pp[mc][0], mc == 0, mc == 1, [self.r_vtok, pp[mc][1]])
                self.op("dve", lambda e: e.tensor_tensor(hT[:, 2 * h + dc, sl], bo, rc, op=ALU.mult),
                        reads=[bro, r_rc], writes=self.rh_fn(tg))
        for blk in range(2):
            w, rw = self.wload(wo_v[:, :, blk * 512:(blk + 1) * 512], [8, 512])
            for tg in range(NTG):
                sl = slice(tg * 512, (tg + 1) * 512)
                for j in range(4):
                    c = blk * 4 + j
                    b, br = self.bank()
                    for k in range(8):
                        self.mm(b, br, w[:, k, j * 128:(j + 1) * 128], hT[:, k, sl], k == 0, k == 7, [rw] + self.rh_fn(tg))
                    self.op("dve", lambda e, c=c, sl=sl, b=b: e.tensor_tensor(self.xT[:, c, sl], self.xT[:, c, sl], b, op=ALU.add),
                            reads=[br] + self.rx_fn(c, tg), writes=self.rx_fn(c, tg))
        self.barrier()
        self.release(m)

    def setup_layer_consts(self, d):
        idf = self.identf
        trif, r_trif = self.alloc([128], F32, "trif")
        self.trib, self.r_trib = self.alloc([128], BF16, "trib")
        self.op("pool", lambda e: e.memset(trif, 1.0), writes=[r_trif])
        self.op("pool", lambda e: e.affine_select(trif, trif, pattern=[[1, 128]], compare_op=ALU.is_ge, fill=0.0, base=0, channel_multiplier=-1),
                reads=[r_trif], writes=[r_trif])
        self.op("dve", lambda e: e.tensor_copy(self.trib, trif), reads=[r_trif], writes=[self.r_trib])
        self.bdb, self.r_bdb = self.alloc([128], BF16, "bdb")
        self.op("pool", lambda e: e.memset(trif[0:64, 64:128], 0.0), reads=[r_trif], writes=[r_trif])
        self.op("dve", lambda e: e.tensor_copy(self.bdb, trif), reads=[r_trif], writes=[self.r_bdb])
        self.hmask, self.r_hmask = self.alloc([2], F32, "hmask")
        self.op("pool", lambda e: e.memset(self.hmask, 0.0), writes=[self.r_hmask])
        self.op("pool", lambda e: e.memset(self.hmask[0:64, 0:1], 1.0), reads=[self.r_hmask], writes=[self.r_hmask])
        self.op("pool", lambda e: e.memset(self.hmask[64:128, 1:2], 1.0), reads=[self.r_hmask], writes=[self.r_hmask])
        self.vb, self.r_vb = self.alloc([8, 8], F32, "vb")
        self.op("pool", lambda e: e.memset(self.vb, 0.0), writes=[self.r_vb])
        for jq in range(8):
            self.op("pool", lambda e, jq=jq: e.memset(self.vb[:, jq, jq:8], -1.0e30), reads=[self.r_vb], writes=[self.r_vb])
        self.vball, self.r_vball = self.alloc([256], F32, "vball")
        for qt in range(16):
            for hh in range(2):
                o_ = (qt * 2 + hh) * 8
                self.op("pool", lambda e: e.tensor_copy(self.vball[:, o_:o_ + 8], self.vb[:, qt // 2, :]), reads=[self.r_vb], writes=[self.r_vball])
        self.oh40, self.r_oh40 = self.alloc([8, 128], BF16, "oh40")
        self.op("pool", lambda e: e.memset(self.oh40, 0.0), writes=[self.r_oh40])
        self.op("dve", lambda e: e.tensor_copy(self.oh40[0:8], self.identb[0:8, 0:8].unsqueeze(2).to_broadcast([8, 8, 128])),
                reads=[self.r_identb, self.r_oh40], writes=[self.r_oh40])
        self.op("dve", lambda e: e.tensor_copy(self.oh40[32:40], self.identb[32:40, 32:40].unsqueeze(2).to_broadcast([8, 8, 128])),
                reads=[self.r_identb, self.r_oh40], writes=[self.r_oh40])
        self.m40, self.r_m40 = self.alloc([2], F32, "m40")
        self.op("pool", lambda e: e.memset(self.m40, 0.0), writes=[self.r_m40])
        self.op("pool", lambda e: e.memset(self.m40[0:32, 0:1], 1.0), reads=[self.r_m40], writes=[self.r_m40])
        self.op("pool", lambda e: e.memset(self.m40[32:64, 1:2], 1.0), reads=[self.r_m40], writes=[self.r_m40])
        self.sguw, self.r_sguw = self.alloc([4, 128], BF16, "sguw")
        self.brep, self.r_brep = self.alloc([4, 128], F32, "brep")
        self.P.dma("sp", lambda e: e.dma_start(out=self.brep, in_=d["sgu_b"][0].partition_broadcast(128)), writes=[self.r_brep])
        self.gs, self.r_gs = self.alloc([4], F32, "gs")
        self.op("dve", lambda e: e.tensor_scalar(self.gs, self.pv[:, 92:96], float(np.sqrt(128.0)), None, op0=ALU.mult), reads=[self.r_pv], writes=[self.r_gs])
        m = self.mark()
        wtmp, r_wtmp = self.alloc([4, 128], F32, "sgutmp")
        self.P.dma("sp", lambda e: e.dma_start(out=wtmp, in_=d["sgu_w"][0].rearrange("g t s -> t g s")), writes=[r_wtmp])
        for g in range(4):
            self.op("pool", lambda e, g=g: e.affine_select(wtmp[:, g, :], wtmp[:, g, :], pattern=[[-1, 128]], compare_op=ALU.is_ge, fill=0.0,
                                                            base=0, channel_multiplier=1), reads=[r_wtmp], writes=[r_wtmp])
        b, br = self.bank()
        for g in range(4):
            self.op("pe", lambda e, g=g: e.transpose(b[:, g * 128:(g + 1) * 128], wtmp[:, g, :], idf), reads=[r_wtmp, self.r_identf], writes=[br])
        self.op("dve", lambda e: e.tensor_copy(self.sguw, b.rearrange("p (g t) -> p g t", g=4)), reads=[br], writes=[self.r_sguw])
        self.barrier()
        self.release(m)

    def out_proj(self, oB, r_oB, w_out):
        wv = w_out.rearrange("(k p) n -> p k n", p=128)
        for blk in range(2):
            w, rw = self.wload(wv[:, :, blk * 512:(blk + 1) * 512], [8, 512])
            for tg in range(NTG):
                sl = slice(tg * 512, (tg + 1) * 512)
                for j in range(4):
                    c = blk * 4 + j
                    b, br = self.bank()
                    for k in range(8):
                        self.mm(b, br, w[:, k, j * 128:(j + 1) * 128], oB[:, k, sl], k == 0, k == 7, [rw, r_oB[tg]])
                    self.op("dve", lambda e, c=c, sl=sl, b=b: e.tensor_tensor(self.xT[:, c, sl], self.xT[:, c, sl], b, op=ALU.add),
                            reads=[br] + self.rx_fn(c, tg), writes=self.rx_fn(c, tg))

    def mixer1(self, d):
        m0 = self.mark()
        self.init_wring(2)
        hT = self.hT
        w_in = d["w_in_cd"][0]
        oB, _ = self.alloc([8, S], BF16, "oB")
        r_oB = [Res(f"oB{t}") for t in range(NTG)]
        m1 = self.mark()
        qz, r_qz = self.alloc([2, S], BF16, "qz")
        kc, r_kc = self.alloc([S], BF16, "kc")
        vt, r_vt = self.alloc([16, 130], BF16, "vt")
        kmean, r_kmean = self.alloc([8], BF16, "kmean")
        kmf, r_kmf = self.alloc([8], F32, "kmf")
        ms, r_ms = self.alloc([32, 8], F32, "ms")
        ms1, r_ms1 = self.alloc([32, 8], F32, "ms1")
        eq, r_eq = self.alloc([32, 8], F32, "eq")
        rmax, r_rmax = self.alloc([32], F32, "rmax")
        bqa, r_bqa = self.alloc([16, 40], BF16, "bqa")
        biasT, r_biasT = self.alloc([2, S], BF16, "biasTT")
        NPT = 5
        pT = [self.alloc([2, 256], BF16, f"mpT{i}") for i in range(NPT)]
        otok = [self.alloc([128], BF16, f"otok{i}") for i in range(2)]
        rcs = [self.alloc([2], F32, f"mrc{i}") for i in range(2)]
        self.op("pool", lambda e: e.memset(vt, 1.0), writes=[r_vt])
        self.op("pool", lambda e: e.memset(bqa, 0.0), writes=[r_bqa])
        wvv = w_in.rearrange("(k p) n -> p k n", p=128)
        pstep = 0
        for c in range(0 if "nomoba" in self.stages else 4):
            w, rw = self.wload([wvv[:, :, s_ * 512 + c * 128:s_ * 512 + (c + 1) * 128] for s_ in range(3)], [8, 3, 128])
            for tg in range(NTG):
                sl = slice(tg * 512, (tg + 1) * 512)
                b, br = self.bank()
                for k in range(8):
                    self.mm(b, br, w[:, k, 0, :], hT[:, k, sl], k == 0, k == 7, [rw] + self.rh_fn(tg))
                for hh in range(2):
                    self.op("act", lambda e: e.activation(qz[:, hh, sl], b, AF.Copy, scale=self.hmask[:, hh:hh + 1]), reads=[br, self.r_hmask], writes=[r_qz])
            for tg in range(NTG):
                sl = slice(tg * 512, (tg + 1) * 512)
                b, br = self.bank()
                for k in range(8):
                    self.mm(b, br, w[:, k, 1, :], hT[:, k, sl], k == 0, k == 7, [rw] + self.rh_fn(tg))
                self.evac(kc[:, sl], b, [br], [r_kc])
            for g4 in range(4):
                b, br = self.bank()
                for tt in range(4):
                    t0 = (g4 * 4 + tt) * 128
                    for k in range(8):
                        self.mm(b[:, tt * 128:(tt + 1) * 128], br, hT[:, k, t0:t0 + 128], w[:, k, 2, :], k == 0, k == 7, [rw] + self.rh_fn(g4))
                dstv = vt[:, g4 * 4:(g4 + 1) * 4, :].rearrange("p t (h e) -> p t h e", h=2)[:, :, :, 0:64]
                srcv = b.rearrange("p (t h e) -> p t h e", t=4, h=2)
                self.evac(dstv, srcv, [br], [r_vt])
            self.op("dve", lambda e: e.tensor_reduce(out=kmf, in_=kc.rearrange("p (n t) -> p n t", n=8), axis=AX.X, op=ALU.add),
                    reads=[r_kc], writes=[r_kmf])
            self.op("dve", lambda e: e.tensor_scalar(kmean, kmf, 1.0 / 256.0, None, op0=ALU.mult), reads=[r_kmf], writes=[r_kmean])
            bs, brs = self.bank()
            for qt in range(16):
                for hh in range(2):
                    self.mm(bs[:, (qt * 2 + hh) * 8:(qt * 2 + hh + 1) * 8], brs, qz[:, hh, qt * 128:(qt + 1) * 128], kmean, True, True, [r_qz, r_kmean])
            msv = ms.rearrange("p a n -> p (a n)")
            self.op("dve", lambda e: e.tensor_tensor(msv, bs[:, 0:256], self.vball, op=ALU.add), reads=[brs, self.r_vball], writes=[r_ms])
            src, r_src = ms, r_ms
            for rnd in range(2):
                self.op("dve", lambda e: e.tensor_reduce(out=rmax, in_=src, axis=AX.X, op=ALU.max), reads=[r_src], writes=[r_rmax])
                self.op("dve", lambda e: e.tensor_tensor(eq, src, rmax.unsqueeze(2).to_broadcast([128, 32, 8]), op=ALU.is_ge),
                        reads=[r_src, r_rmax], writes=[r_eq])
                self.op("dve", lambda e: e.scalar_tensor_tensor(ms1, eq, -3.0e30, src, op0=ALU.mult, op1=ALU.add),
                        reads=[r_eq, r_src], writes=[r_ms1])
                src, r_src = ms1, r_ms1
            self.op("dve", lambda e: e.tensor_reduce(out=rmax, in_=ms1, axis=AX.X, op=ALU.max), reads=[r_ms1], writes=[r_rmax])
            self.op("dve", lambda e: e.tensor_tensor(eq, ms, rmax.unsqueeze(2).to_broadcast([128, 32, 8]), op=ALU.is_ge),
                    reads=[r_ms, r_rmax], writes=[r_eq])
            eq4 = eq.rearrange("p (t h) n -> p t h n", h=2)
            for hh in range(2):
                self.op("dve", lambda e: e.tensor_scalar(bqa[:, :, hh * 32:hh * 32 + 8], eq4[:, :, hh, :], -1.0, 30000.0, op0=ALU.add, op1=ALU.mult),
                        reads=[r_eq], writes=[r_bqa])
            for tg in range(NTG):
                bt, brt = self.bank()
                for tt in range(4):
                    qt = tg * 4 + tt
                    self.mm(bt[0:40, tt * 128:(tt + 1) * 128], brt, bqa[:, qt, :], self.identb, True, True, [r_bqa, self.r_identb])
                for hh in range(2):
                    self.op("act", lambda e: e.activation(biasT[0:40, hh, tg * 512:(tg + 1) * 512], bt[0:40, :], AF.Copy, scale=self.m40[0:40, hh:hh + 1]),
                            reads=[brt, self.r_m40], writes=[r_biasT])
            for jq in range(0 if "noattn" in self.stages else 8):
                q0 = jq * 256
                pos = [self.acc_bank(), self.acc_bank()]
                nchunks = 2 * jq + 2
                def score_stage(ci):
                    pt, r_pt = pT[(pbase + ci) % NPT]
                    k0 = ci * 128
                    bsc, brsc = self.bank()
                    own = ci >= 2 * jq
                    isB = ci == 2 * jq + 1
                    if not own:
                        n = ci // 2
                        nobias = jq <= 3
                        self.mm(bsc, brsc, kc[:, k0:k0 + 128], qz[:, :, q0:q0 + 256], True, nobias, [r_kc, r_qz])
                        if not nobias:
                            self.mm(bsc, brsc, self.oh40[0:40, n, :], biasT[0:40, :, q0:q0 + 256], False, True, [self.r_oh40, r_biasT])
                        self.op("act", lambda e: e.activation(pt, bsc.rearrange("p (h q) -> p h q", h=2), AF.Exp, scale=0.125), reads=[brsc], writes=[r_pt])
                    elif not isB:
                        self.mm(bsc, brsc, kc[:, k0:k0 + 128], qz[:, :, q0:q0 + 256], True, True, [r_kc, r_qz])
                        self.op("act", lambda e: e.activation(pt, bsc.rearrange("p (h q) -> p h q", h=2), AF.Exp, scale=0.125), reads=[brsc], writes=[r_pt])
                        self.op("dve", lambda e: e.tensor_tensor(pt[:, :, 0:128], pt[:, :, 0:128], self.trib.unsqueeze(1).to_broadcast([128, 2, 128]), op=ALU.mult),
                                reads=[r_pt, self.r_trib], writes=[r_pt])
                    else:
                        self.mm(bsc[:, 0:256], brsc, kc[:, k0:k0 + 128], qz[:, :, q0 + 128:q0 + 256], True, True, [r_kc, r_qz])
                        self.op("act", lambda e: e.activation(pt[:, :, 128:256], bsc[:, 0:256].rearrange("p (h q) -> p h q", h=2), AF.Exp, scale=0.125),
                                reads=[brsc], writes=[r_pt])
                        self.op("dve", lambda e: e.tensor_tensor(pt[:, :, 128:256], pt[:, :, 128:256], self.trib.unsqueeze(1).to_broadcast([128, 2, 128]), op=ALU.mult),
                                reads=[r_pt, self.r_trib], writes=[r_pt])

                pbase = pstep
                LOOK = 2
                for ci in range(min(LOOK, nchunks)):
                    score_stage(ci)
                for ci in range(nchunks):
                    if ci + LOOK < nchunks:
                        score_stage(ci + LOOK)
                    pt, r_pt = pT[(pbase + ci) % NPT]
                    isB = ci == 2 * jq + 1
                    for hh in range(2):
                        po, rpo = pos[hh]
                        vv = vt[:, ci, hh * 65:(hh + 1) * 65]
                        if not isB:
                            self.mm(po[:, 0:65], rpo, pt[:, hh, 0:128], vv, ci == 0, ci == 2 * jq, [r_pt, r_vt], skip=True)
                        self.mm(po[:, 128:193], rpo, pt[:, hh, 128:256], vv, False, ci == nchunks - 1, [r_pt, r_vt], skip=True)
                pstep += nchunks
                for hh in range(2):
                    po, rpo = pos[hh]
                    hs = slice(hh * 64, (hh + 1) * 64)
                    rc, r_rc = rcs[hh]
                    pov = po.rearrange("p (t e) -> p t e", t=4)
                    self.op("dve", lambda e: e.reciprocal(rc, pov[:, 0:2, 64]), reads=[rpo], writes=[r_rc])
                    for t2 in range(2):
                        ot, r_ot = otok[t2]
                        self.op("dve", lambda e: e.tensor_scalar(ot[:, hs], po[:, t2 * 128:t2 * 128 + 64], rc[:, t2:t2 + 1], None, op0=ALU.mult),
                                reads=[rpo, r_rc], writes=[r_ot])
                bt2, brt2 = self.bank()
                btb2 = bt2.bitcast(BF16)
                for t2 in range(2):
                    ot, r_ot = otok[t2]
                    self.op("pe", lambda e: e.transpose(btb2[:, t2 * 128:(t2 + 1) * 128], ot, self.identb), reads=[r_ot, self.r_identb], writes=[brt2])
                self.evac(oB[:, c, q0:q0 + 256], btb2[:, 0:256], [brt2], [r_oB[jq // 2]])
        self.barrier()
        self.release(m1)
        def sgu_bufs(tag):
            B = {}
            B["uf"] = [self.alloc([512], F32, f"uf{tag}{i}") for i in range(2)]
            for nm, dt_ in (("zf", F32), ("zb", BF16), ("zc", F32), ("zq", BF16), ("rr", F32), ("zn", BF16), ("tmp", F32)):
                B[nm] = self.alloc([512], dt_, f"{nm}{tag}")
            B["znt"] = self.alloc([4, 128], BF16, f"znt{tag}")
            return B

        def sgu_gen(g, B):
            zf, r_zf = B["zf"]
            zb, r_zb = B["zb"]
            zc, r_zc = B["zc"]
            zq, r_zq = B["zq"]
            rr, r_rr = B["rr"]
            zn, r_zn = B["zn"]
            tmp, r_tmp = B["tmp"]
            znt, r_znt = B["znt"]
            w, rw = self.wload([wvv[:, :, 1536 + s_ * 512 + g * 128:1536 + s_ * 512 + (g + 1) * 128] for s_ in range(2)], [8, 2, 128])
            for tg in range(NTG):
                sl = slice(tg * 512, (tg + 1) * 512)
                uv, r_uv = B["uf"][tg % 2]
                pu, rpu = self.bank()
                for k in range(8):
                    self.mm(pu, rpu, w[:, k, 0, :], hT[:, k, sl], k == 0, k == 7, [rw] + self.rh_fn(tg))
                self.op("act", lambda e: e.activation(uv, pu, AF.Gelu), reads=[rpu], writes=[r_uv])
                pz, rpz = self.bank()
                for k in range(8):
                    self.mm(pz, rpz, w[:, k, 1, :], hT[:, k, sl], k == 0, k == 7, [rw] + self.rh_fn(tg))
                self.op("act", lambda e: e.activation(zf, pz, AF.Gelu), reads=[rpz], writes=[r_zf])
                yield
                self.op("dve", lambda e: e.tensor_copy(zb, zf), reads=[r_zf], writes=[r_zb])
                p1, rp1 = self.bank()
                self.mm(p1, rp1, self.onesb, zb, True, True, [self.r_onesb, r_zb])
                self.op("dve", lambda e: e.scalar_tensor_tensor(zc, p1, -1.0 / 128.0, zf, op0=ALU.mult, op1=ALU.add), reads=[rp1, r_zf], writes=[r_zc])
                yield
                self.op("act", lambda e: e.activation(zq, zc, AF.Square), reads=[r_zc], writes=[r_zq])
                p2, rp2 = self.bank()
                self.mm(p2, rp2, self.onesb, zq, True, True, [self.r_onesb, r_zq])
                self.rsqrt_ps(p2, float(128 * EPS), rr, r_rr, rp2)
                yield
                self.op("dve", lambda e: e.tensor_tensor(zc, zc, rr, op=ALU.mult), reads=[r_zc, r_rr], writes=[r_zc])
                self.op("act", lambda e: e.activation(zn, zc, AF.Identity, bias=self.pv[:, 96 + g:97 + g], scale=self.gs[:, g:g + 1]),
                        reads=[r_zc, self.r_pv, self.r_gs], writes=[r_zn])
                pt_, rpt_ = self.bank()
                ptb = pt_.bitcast(BF16)
                for tt in range(4):
                    self.op("pe", lambda e: e.transpose(ptb[:, tt * 128:(tt + 1) * 128], zn[:, tt * 128:(tt + 1) * 128], self.identb),
                            reads=[r_zn, self.r_identb], writes=[rpt_])
                self.evac(znt, ptb[:, 0:512].rearrange("p (t c) -> p t c", t=4), [rpt_], [r_znt])
                yield
                pm, rpm = self.bank()
                for tt in range(4):
                    self.mm(pm[:, tt * 128:(tt + 1) * 128], rpm, znt[:, tt, :], self.sguw[:, g, :], True, True, [r_znt, self.r_sguw])
                self.op("dve", lambda e: e.tensor_tensor(tmp.rearrange("p (t c) -> p t c", t=4), pm.rearrange("p (t c) -> p t c", t=4),
                                                          self.brep[:, g:g + 1, :].to_broadcast([128, 4, 128]), op=ALU.add),
                        reads=[rpm, self.r_brep], writes=[r_tmp])
                self.op("dve", lambda e: e.tensor_tensor(oB[:, 4 + g, sl], tmp, uv, op=ALU.mult), reads=[r_tmp, r_uv], writes=[r_oB[tg]])
                yield

        if "nosgu" not in self.stages:
            bufsets = [sgu_bufs("a"), sgu_bufs("b")]
            for gp in range(2):
                gens = [sgu_gen(2 * gp + i, bufsets[i]) for i in range(2)]
                alive = list(gens)
                lag = 2
                step = 0
                while alive:
                    for gi, gen in enumerate(gens):
                        if gen not in alive:
                            continue
                        if gi == 1 and step < lag:
                            continue
                        try:
                            next(gen)
                        except StopIteration:
                            alive.remove(gen)
                    step += 1
        if "dbg" in self.stages:
            o = self.P.dma("sp", lambda e: e.dma_start(out=d["dbg"], in_=oB), reads=r_oB)
            self.dbg_tok = ("d", "sp", o.dma_slot, o.dma_cnt)
        self.out_proj(oB, r_oB, d["w_out_cd"][0])
        self.barrier()
        self.release(m0)

    def mixer0(self, d):
        import os
        KA_T = int(os.environ.get("KA_T", "6"))
        KA_I = int(os.environ.get("KA_I", "10"))
        KA_N = int(os.environ.get("KA_N", "4"))
        m0 = self.mark()
        self.init_wring(2)
        hT = self.hT
        wvv = d["w_in_ab"][0].rearrange("(k p) n -> p k n", p=128)
        oB, _ = self.alloc([8, S], BF16, "oB0")
        r_oB = [Res(f"oB0_{t}") for t in range(NTG)]
        hp = self.hp
        m1 = self.mark()
        F = lambda n: self.alloc([512], F32, n)
        qs, r_qs = F("qs")
        ff, r_ff = F("ff")
        gate, r_gate = F("gate")
        bcum, r_bcum = F("bcum")
        rb, r_rb = F("rb")
        omf, r_omf = F("omf")
        kinf, r_kinf = F("kinf")
        rr, r_rr = F("hrr")
        otmp, r_otmp = F("otmp")
        zeros, r_zeros = self.alloc([64], F32, "zeros")
        vT, r_vT = self.alloc([512], BF16, "vT")
        qin, r_qin = self.alloc([512], BF16, "qin")
        kin, r_kin = self.alloc([512], BF16, "kin")
        kdT, r_kdT = self.alloc([512], BF16, "kdT")
        sqb, r_sqb = self.alloc([512], BF16, "hsq")
        vtok, r_vtok = self.alloc([4, 128], BF16, "hvtok")
        kdtok, r_kdtok = self.alloc([4, 128], BF16, "hkdtok")
        kdtok2, _ = self.alloc([4, 128], BF16, "hkdtok2")
        attm, r_attm = self.alloc([4, 128], BF16, "attm")
        Sall, r_Sall = self.alloc([9, 128], F32, "Sall")
        Sball, r_Sball = self.alloc([8, 128], BF16, "Sball")
        self.op("pool", lambda e: e.memset(zeros, 0.0), writes=[r_zeros])
        for h in range(0 if "nohgrn" in self.stages else 4):
            w, rw = self.wload([wvv[:, :, s_ * 512 + h * 128:s_ * 512 + (h + 1) * 128] for s_ in range(4)], [8, 4, 128])
            self.op("pool", lambda e: e.memset(Sall[:, 0, :], 0.0), writes=[r_Sall])
            for tg in range(NTG):
                sl = slice(tg * 512, (tg + 1) * 512)
                pb = []
                for s_ in range(4):
                    b, br = self.bank()
                    for k in range(8):
                        self.mm(b, br, w[:, k, s_, :], hT[:, k, sl], k == 0, k == 7, [rw] + self.rh_fn(tg))
                    pb.append((b, br))
                (pq, rpq), (pf, rpf), (pi, rpi), (pg, rpg) = pb
                self.op("act", lambda e: e.activation(qs, pq, AF.Silu), reads=[rpq], writes=[r_qs])
                self.op("act", lambda e: e.activation(ff, pf, AF.Sigmoid), reads=[rpf], writes=[r_ff])
                self.op("act", lambda e: e.activation(gate, pg, AF.Silu), reads=[rpg], writes=[r_gate])
                self.op("dve", lambda e: e.tensor_copy(vT, pi), reads=[rpi], writes=[r_vT])
                self.op("dve", lambda e: e.tensor_scalar(ff, ff, hp[:, 4 + h:5 + h], hp[:, h:h + 1], op0=ALU.mult, op1=ALU.add),
                        reads=[r_ff, self.r_hp], writes=[r_ff])
                for c in range(8):
                    cs = slice(c * 64, (c + 1) * 64)
                    self.op("dve", lambda e: e.tensor_tensor_scan(bcum[:, cs], ff[:, cs], zeros, 1.0, op0=ALU.mult, op1=ALU.add),
                            reads=[r_ff, r_zeros], writes=[r_bcum])
                self.op("act", lambda e: e.activation(rb, bcum, AF.Ln), reads=[r_bcum], writes=[r_rb])
                self.op("act", lambda e: e.activation(rb, rb, AF.Exp, scale=-1.0), reads=[r_rb], writes=[r_rb])
                self.op("dve", lambda e: e.tensor_scalar(omf, ff, -1.0, 1.0, op0=ALU.mult, op1=ALU.add), reads=[r_ff], writes=[r_omf])
                self.op("dve", lambda e: e.tensor_tensor(qin, qs, bcum, op=ALU.mult), reads=[r_qs, r_bcum], writes=[r_qin])
                self.op("dve", lambda e: e.tensor_tensor(kinf, omf, rb, op=ALU.mult), reads=[r_omf, r_rb], writes=[r_kinf])
                self.op("act", lambda e: e.activation(kin, kinf, AF.Copy), reads=[r_kinf], writes=[r_kin])
                blast = bcum.rearrange("p (c t) -> p c t", c=8)[:, :, 63:64]
                self.op("dve", lambda e: e.tensor_tensor(kdT.rearrange("p (c t) -> p c t", c=8), kinf.rearrange("p (c t) -> p c t", c=8),
                                                          blast.to_broadcast([128, 8, 64]), op=ALU.mult), reads=[r_kinf, r_bcum], writes=[r_kdT])
                import os
                HC = int(os.environ.get("DBG_H", "9"))
                if HC < 2:
                    continue
                for src, r_src, dst, r_dst in ((vT, r_vT, vtok, r_vtok), (kdT, r_kdT, kdtok, r_kdtok)):
                    bt, brt = self.bank()
                    btb = bt.bitcast(BF16)
                    for tt in range(4):
                        o_ = self.op("pe", lambda e: e.transpose(btb[:, tt * 128:(tt + 1) * 128], src[:, tt * 128:(tt + 1) * 128], self.identb),
                                     reads=[r_src, self.r_identb], writes=[brt])
                        if tt == 0:
                            o_.ka = KA_T
                    bview = btb[:, 0:512].rearrange("p (t c) -> p t c", t=4)
                    if dst is vtok:
                        self.evac(dst, bview, [brt], [r_dst])
                    else:
                        self.op("act", lambda e: e.activation(kdtok, bview, AF.Copy, scale=self.hmask[:, 0:1]), reads=[brt, self.r_hmask], writes=[r_kdtok])
                        self.op("act", lambda e: e.activation(kdtok2, bview, AF.Copy, scale=self.hmask[:, 1:2]), reads=[brt, self.r_hmask], writes=[r_kdtok])
                if HC < 3:
                    continue
                po, rpo = self.acc_bank()
                ba, bra = self.bank()
                for tt in range(4):
                    ts_ = slice(tt * 128, (tt + 1) * 128)
                    self.mm(ba[:, ts_], bra, kin[:, ts_], qin[:, ts_], True, True, [r_kin, r_qin])
                self.op("dve", lambda e: e.tensor_tensor(attm, ba.rearrange("p (t c) -> p t c", t=4),
                                                          self.bdb.unsqueeze(1).to_broadcast([128, 4, 128]), op=ALU.mult),
                        reads=[bra, self.r_bdb], writes=[r_attm])
                if HC == 31:
                    continue
                bks = [self.bank(), self.bank()]
                for c in range(8):
                    tt, half = c // 2, c % 2
                    bk, brk = bks[c // 4]
                    ps_ = slice(half * 64, (half + 1) * 64)
                    self.mm(bk[:, (c % 4) * 128:(c % 4 + 1) * 128], brk, (kdtok if half == 0 else kdtok2)[:, tt, :], vtok[:, tt, :], True, True, [r_kdtok, r_vtok])
                if HC == 32:
                    continue
                for c in range(8):
                    bk, brk = bks[c // 4]
                    self.op("dve", lambda e: e.scalar_tensor_tensor(Sall[:, c + 1, :], Sall[:, c, :], bcum[:, c * 64 + 63:c * 64 + 64],
                                                                     bk[:, (c % 4) * 128:(c % 4 + 1) * 128], op0=ALU.mult, op1=ALU.add),
                            reads=[r_Sall, r_bcum, brk], writes=[r_Sall])
                self.op("act", lambda e: e.activation(Sball, Sall[:, 0:8, :], AF.Copy), reads=[r_Sall], writes=[r_Sball])
                if HC == 33:
                    continue
                for c in range(8):
                    cs = slice(c * 64, (c + 1) * 64)
                    self.mm(po[:, cs], rpo, Sball[:, c, :], qin[:, cs], (c == 0), False, [r_Sball, r_qin], skip=True, ka=(KA_I if c == 0 else 0))
                for tt in range(4):
                    ts_ = slice(tt * 128, (tt + 1) * 128)
                    self.mm(po[:, ts_], rpo, vtok[:, tt, :], attm[:, tt, :], False, (tt == 3), [r_vtok, r_attm], skip=True)
                self.op("dve", lambda e: e.tensor_copy(Sall[:, 0, :], Sall[:, 8, :]), reads=[r_Sall], writes=[r_Sall])
                if HC < 4:
                    continue
                self.op("act", lambda e: e.activation(sqb, po, AF.Square), reads=[rpo], writes=[r_sqb])
                bss, brss = self.bank()
                self.mm(bss, brss, self.onesb, sqb, True, True, [self.r_onesb, r_sqb], ka=KA_N)
                self.rsqrt_ps(bss, float(128 * EPS), rr, r_rr, brss)
                self.op("dve", lambda e: e.tensor_tensor(otmp, po, rr, op=ALU.mult), reads=[rpo, r_rr], writes=[r_otmp])
                self.op("dve", lambda e: e.scalar_tensor_tensor(oB[:, h, sl], otmp, hp[:, 8 + h:9 + h], gate, op0=ALU.mult, op1=ALU.mult),
                        reads=[r_otmp, r_gate, self.r_hp], writes=[r_oB[tg]])
        self.barrier()
        self.release(m1)
        cpad, r_cpad = self.alloc([4, 30 + S], BF16, "cpad")
        sg2 = [self.alloc([512], F32, f"sg2{i}") for i in range(2)]
        self.op("pool", lambda e: e.memset(cpad[:, :, 0:30], 0.0), writes=[r_cpad])
        for cc in range(4):
            w, rw = self.wload([wvv[:, :, 2048 + s_ * 512 + cc * 128:2048 + s_ * 512 + (cc + 1) * 128] for s_ in range(2)], [8, 2, 128])
            for tg in range(NTG):
                sl = slice(tg * 512, (tg + 1) * 512)
                pa, rpa = self.bank()
                for k in range(8):
                    self.mm(pa, rpa, w[:, k, 0, :], hT[:, k, sl], k == 0, k == 7, [rw] + self.rh_fn(tg))
                pb_, rpb = self.bank()
                for k in range(8):
                    self.mm(pb_, rpb, w[:, k, 1, :], hT[:, k, sl], k == 0, k == 7, [rw] + self.rh_fn(tg))
                sg, r_sg = sg2[tg % 2]
                self.op("act", lambda e: e.activation(sg, pb_, AF.Sigmoid), reads=[rpb], writes=[r_sg])
                self.op("dve", lambda e: e.tensor_tensor(cpad[:, cc, 30 + tg * 512:30 + (tg + 1) * 512], pa, sg, op=ALU.mult),
                        reads=[rpa, r_sg], writes=[r_cpad])
        yv = self.hT_raw.rearrange("p (a b) -> p a b", a=4)
        diag, r_diag = self.alloc([31, 128], BF16, "diag")
        wv = self.pv[:, 128:252].rearrange("p (j c) -> p j c", c=4)
        for cc in range(4):
            self.op("dve", lambda e: e.tensor_tensor(diag, self.identb.unsqueeze(1).to_broadcast([128, 31, 128]),
                                                      wv[:, :, cc].unsqueeze(2).to_broadcast([128, 31, 128]), op=ALU.mult),
                    reads=[self.r_identb, self.r_pv], writes=[r_diag])
            for tg in range(NTG):
                pc, rpc = self.bank()
                for j in range(31):
                    self.mm(pc, rpc, diag[:, j, :], cpad[:, cc, tg * 512 + j:tg * 512 + j + 512], j == 0, j == 30, [r_diag, r_cpad])
                self.op("act", lambda e: e.activation(yv[:, cc, tg * 512:(tg + 1) * 512], pc, AF.Identity, bias=self.pv[:, 80 + cc:81 + cc]),
                        reads=[rpc, self.r_pv], writes=self.rh_fn(tg))
        ybs = [self.alloc([512], BF16, f"yb{i}") for i in range(2)]
        yqs = [self.alloc([512], BF16, f"yq{i}") for i in range(2)]
        mean, r_mean = self.alloc([512], F32, "cmean")
        var, r_var = self.alloc([512], F32, "cvar")
        for tg in range(NTG):
            sl = slice(tg * 512, (tg + 1) * 512)
            p1, rp1 = self.bank()
            p2, rp2 = self.bank()
            for cc in range(4):
                yb, r_yb = ybs[cc % 2]
                yq, r_yq = yqs[cc % 2]
                self.op("dve", lambda e: e.tensor_copy(yb, yv[:, cc, sl]), reads=self.rh_fn(tg), writes=[r_yb])
                self.op("act", lambda e: e.activation(yq, yv[:, cc, sl], AF.Square), reads=self.rh_fn(tg), writes=[r_yq])
                self.mm(p1, rp1, self.onesb, yb, cc == 0, cc == 3, [self.r_onesb, r_yb])
                self.mm(p2, rp2, self.onesb, yq, cc == 0, cc == 3, [self.r_onesb, r_yq])
            self.op("act", lambda e: e.activation(mean, p1, AF.Copy, scale=1.0 / 512.0), reads=[rp1], writes=[r_mean])
            self.op("dve", lambda e: e.tensor_tensor(var, mean, mean, op=ALU.mult), reads=[r_mean], writes=[r_var])
            self.op("dve", lambda e: e.scalar_tensor_tensor(var, p2, 1.0 / 512.0, var, op0=ALU.mult, op1=ALU.subtract), reads=[rp2, r_var], writes=[r_var])
            self.op("act", lambda e: e.activation(var, var, AF.Ln, bias=self.epsc[:, self.eps_idx(EPS)]), reads=[r_var, self.r_epsc], writes=[r_var])
            self.op("act", lambda e: e.activation(var, var, AF.Exp, scale=-0.5), reads=[r_var], writes=[r_var])
            ysl = yv[:, :, sl]
            self.op("dve", lambda e: e.tensor_tensor(ysl, ysl, mean.unsqueeze(1).to_broadcast([128, 4, 512]), op=ALU.subtract),
                    reads=self.rh_fn(tg) + [r_mean], writes=self.rh_fn(tg))
            self.op("dve", lambda e: e.tensor_tensor(ysl, ysl, var.unsqueeze(1).to_broadcast([128, 4, 512]), op=ALU.mult),
                    reads=self.rh_fn(tg) + [r_var], writes=self.rh_fn(tg))
            for cc in range(4):
                self.op("act", lambda e: e.activation(oB[:, 4 + cc, sl], yv[:, cc, sl], AF.Silu, bias=self.pv[:, 88 + cc:89 + cc], scale=self.pv[:, 84 + cc:85 + cc]),
                        reads=self.rh_fn(tg) + [self.r_pv], writes=[r_oB[tg]])
        if "dbg" in self.stages:
            o = self.P.dma("sp", lambda e: e.dma_start(out=d["dbg"], in_=oB), reads=r_oB)
        self.out_proj(oB, r_oB, d["w_out_ab"][0])
        self.barrier()
        self.release(m0)


    def init_stream(self):
        self.xT, _ = self.alloc([8, S], F32, "xT")
        self.rx = [[Res(f"x{k}_{tg}") for tg in range(NTG)] for k in range(8)]
        a0 = self.aoff
        self.hT, _ = self.alloc([8, S], BF16, "hT")
        self.hT_raw = self.arena[:, a0 // 4:a0 // 4 + 4 * S]
        self.rh = [Res(f"h{tg}") for tg in range(NTG)]

    def rx_fn(self, k, tg):
        ks = range(8) if k is None else [k]
        tgs = range(NTG) if tg is None else [tg]
        return [self.rx[a][b] for a in ks for b in tgs]

    def rh_fn(self, tg):
        return [self.rh[t] for t in (range(NTG) if tg is None else [tg])]


INPUT_NAMES = ["x", "mem", "norm_mix", "norm_xattn", "norm_ffn", "mem_norm", "final_norm",
               "w_in_ab", "w_out_ab", "hgrn_lower_bounds", "hgrn_out_norm", "conv_dw_w", "conv_dw_b",
               "conv_ln_g", "conv_ln_b", "w_in_cd", "w_out_cd", "sgu_ln_g", "sgu_ln_b", "sgu_w", "sgu_b",
               "xa_wq", "xa_wkv", "xa_wo", "ffn_w_in", "ffn_w_out"]


def build_program(shapes, nseq, stages):
    nc = bass.Bass("TRN2", target_bir_lowering=False)
    d = {}
    for n in INPUT_NAMES:
        shp = list(shapes[n])
        if n in ("x", "mem"):
            shp[0] = nseq
        d[n] = nc.dram_tensor(n, shp, F32, kind="ExternalInput").ap()
    out = nc.dram_tensor("out", [nseq, S, D], F32, kind="ExternalOutput").ap()
    if "dbg" in stages:
        d["dbg"] = nc.dram_tensor("dbg", [128, 8, S], BF16, kind="ExternalOutput").ap()
    with ExitStack() as es:
        kb = KB(nc, es, nseq, stages)
        kb.setup_consts(d)
        kb.init_stream()
        kb.init_xa()
        if hasattr(kb, "setup_layer_consts"):
            kb.setup_layer_consts(d)
        finals = []
        for s in range(nseq):
            kb.load_T(d["x"][s], S, kb.xT, lambda i, half: [kb.rx[k][i // 4] for k in range(half * 4, half * 4 + 4)])
            if "xa" in stages:
                kb.prep_mem(d["mem"][s])
            for l in range(2):
                if f"mix{l}" in stages:
                    kb.norm_to(kb.xT, kb.rx_fn, 0, l, kb.hT, kb.rh_fn)
                    if "nomix" in stages:
                        pass
                    elif l == 0:
                        kb.mixer0(d)
                    else:
                        kb.mixer1(d)
                if "xa" in stages or f"xa{l}" in stages:
                    kb.norm_to(kb.xT, kb.rx_fn, 16, l, kb.hT, kb.rh_fn)
                    kb.xattn(d, l)
                if "ffn" in stages or f"ffn{l}" in stages:
                    kb.norm_to(kb.xT, kb.rx_fn, 32, l, kb.hT, kb.rh_fn)
                    kb.ffn(kb.xT, kb.rx_fn, kb.hT, kb.rh_fn, d["ffn_w_in"][l], d["ffn_w_out"][l])
            finals += kb.final_store(kb.xT, kb.rx_fn, out[s])
        kb.P.emit(final_waits=finals)
    return nc


ALL_STAGES = ("mix0", "mix1", "xa", "ffn")
_CACHE = {}


def kernel(**inputs):
    n_cores = 8
    nseq = 2
    shapes = {k: np.shape(v) for k, v in inputs.items()}
    key = "full"
    if key not in _CACHE:
        _CACHE[key] = build_program(shapes, nseq, ALL_STAGES)
    nc = _CACHE[key]
    arrs = {k: np.ascontiguousarray(np.asarray(v, dtype=np.float32)) for k, v in inputs.items()}
    in_maps = []
    for c in range(n_cores):
        m = {}
        for k in INPUT_NAMES:
            if k in ("x", "mem"):
                m[k] = np.ascontiguousarray(arrs[k][c * nseq:(c + 1) * nseq])
            else:
                m[k] = arrs[k]
        in_maps.append(m)
    res = run_bass_kernel_spmd(nc, in_maps, core_ids=list(range(n_cores)))
    outs = [np.asarray(r["out"]) for r in res.results]
    return np.concatenate(outs, axis=0).astype(np.float32)
```

```python
from contextlib import ExitStack
import numpy as np
import concourse.bass as bass
import concourse.mybir as mybir
from concourse.bass_utils import run_bass_kernel_spmd

F32 = mybir.dt.float32
BF16 = mybir.dt.bfloat16
AF = mybir.ActivationFunctionType
ALU = mybir.AluOpType
AX = mybir.AxisListType

ENG = ("pe", "act", "dve", "pool", "sp")
SEM_ROLL = 12000
N_DMA_SEMS = 8
EMBED_WAIT = True


import types


def _freeze(fn):
    if getattr(fn, "__closure__", None) is None:
        return fn
    cells = []
    for c in fn.__closure__:
        try:
            cells.append(types.CellType(c.cell_contents))
        except ValueError:
            cells.append(c)
    return types.FunctionType(fn.__code__, fn.__globals__, fn.__name__, fn.__defaults__, tuple(cells))


class Res:
    __slots__ = ("name", "w", "r", "wx", "excl")

    def __init__(self, name, excl=False):
        self.name = name
        self.excl = excl
        self.w = None
        self.r = []
        self.wx = []


class Op:
    __slots__ = ("eng", "fn", "waits", "signal", "sig_no", "dma_slot", "dma_cnt", "idx", "ka")

    def __init__(self, eng, fn):
        self.eng = eng
        self.fn = _freeze(fn)
        self.waits = []
        self.signal = False
        self.sig_no = None
        self.dma_slot = None
        self.dma_cnt = None
        self.idx = None
        self.ka = 0


class Prog:
    def __init__(self, nc):
        self.nc = nc
        self.ops = {e: [] for e in ENG}
        self.dma_rr = {e: 0 for e in ENG}
        self.dma_count = {e: [0] * N_DMA_SEMS for e in ENG}
        self.pending = {}
        self.keepalive = None

    def barrier(self):
        toks = []
        for e in ENG:
            for o in reversed(self.ops[e]):
                if o.dma_slot is None:
                    toks.append(("c", e, o.idx))
                    break
            for slot, cnt in enumerate(self.dma_count[e]):
                if cnt > 0:
                    toks.append(("d", e, slot, cnt))
        self.pending = {e: list(toks) for e in ENG}

    def _pend(self, eng):
        t = self.pending.pop(eng, [])
        return [d for d in t if not (d[0] == "c" and d[1] == eng)]

    def _deps(self, eng, reads, writes, is_dma):
        deps = []
        for r in reads:
            if r.w is not None:
                deps.append(r.w)
            deps.extend(r.wx)
            if r.excl:
                deps.extend(t for t in r.r if t[1] != eng)
        for w in writes:
            if w.w is not None:
                deps.append(w.w)
            deps.extend(w.wx)
            deps.extend(w.r)
        out = []
        for d in deps:
            if d[0] == "c":
                if d[1] == eng and not is_dma:
                    if eng == "pe":
                        continue
                out.append(d)
            else:
                out.append(d)
        return out

    def op(self, eng, fn, reads=(), writes=()):
        o = Op(eng, fn)
        o.idx = len(self.ops[eng])
        o.waits = self._deps(eng, reads, writes, False)
        if eng != "pe":
            raw = set()
            for r in reads:
                if r.w is not None and r.w[0] == "c" and r.w[1] == eng:
                    raw.add(r.w)
            o.waits = [d for d in o.waits if not (d[0] == "c" and d[1] == eng and d not in raw)]
        o.waits = o.waits + self._pend(eng)
        self.ops[eng].append(o)
        tok = ("c", eng, o.idx)
        for r in reads:
            r.r.append(tok)
        for w in writes:
            w.w = tok
            w.r = []
            w.wx = []
        return o

    def dma(self, qeng, fn, reads=(), writes=(), extra=False):
        o = Op(qeng, fn)
        o.idx = len(self.ops[qeng])
        o.waits = self._deps(qeng, reads, () if extra else writes, True) + self._pend(qeng)
        slot = self.dma_rr[qeng]
        self.dma_rr[qeng] = (slot + 1) % N_DMA_SEMS
        prev = self.dma_count[qeng][slot]
        if prev > 0:
            o.waits.append(("d", qeng, slot, prev))
        self.dma_count[qeng][slot] = prev + 1
        o.dma_slot = slot
        o.dma_cnt = prev + 1
        self.ops[qeng].append(o)
        tok = ("d", qeng, slot, prev + 1)
        for r in reads:
            r.r.append(tok)
        for w in writes:
            if extra:
                w.wx.append(tok)
            else:
                w.w = tok
                w.r = []
                w.wx = []
        return o

    def emit(self, final_waits=()):
        nc = self.nc
        for e in ENG:
            for o in self.ops[e]:
                for d in o.waits:
                    if d[0] == "c":
                        self.ops[d[1]][d[2]].signal = True
        for d in final_waits:
            if d[0] == "c":
                self.ops[d[1]][d[2]].signal = True
        nsig = {}
        for e in ENG:
            c = 0
            for o in self.ops[e]:
                if o.signal:
                    c += 1
                    o.sig_no = c
            nsig[e] = c
        from contextlib import ExitStack
        with ExitStack() as es:
            csem = {}
            for e in ENG:
                n = max(1, -(-nsig[e] // SEM_ROLL))
                csem[e] = [es.enter_context(nc.semaphore(f"c_{e}_{i}")) for i in range(n)]
            dsem = {}
            for e in ENG:
                if any(self.dma_count[e]):
                    dsem[e] = [es.enter_context(nc.semaphore(f"d_{e}_{i}")) for i in range(N_DMA_SEMS)]
            block = es.enter_context(nc.Block())

            def lower(d):
                if d[0] == "c":
                    s = self.ops[d[1]][d[2]].sig_no - 1
                    return (csem[d[1]][s // SEM_ROLL], s % SEM_ROLL + 1)
                return (dsem[d[1]][d[2]], 16 * d[3])

            def run(e, engobj):
                seen = {}
                for o in self.ops[e]:
                    need = {}
                    for d in o.waits:
                        sem, val = lower(d)
                        k = id(sem)
                        if seen.get(k, 0) >= val:
                            continue
                        if k not in need or need[k][1] < val:
                            need[k] = (sem, val)
                    if e == "pe" and o.ka and self.keepalive is not None:
                        for _ in range(o.ka):
                            self.keepalive(engobj)
                    items_ = list(need.items())
                    embed = None
                    if EMBED_WAIT and o.dma_slot is None and e in ("pe", "act", "dve") and items_:
                        embed = items_.pop()
                    for k, (sem, val) in items_:
                        engobj.wait_ge(sem, val)
                        seen[k] = val
                    ins = o.fn(engobj)
                    if embed is not None:
                        k, (sem, val) = embed
                        ins._wait_ge(sem, val)
                        seen[k] = val
                    if o.dma_slot is not None:
                        ins.then_inc(dsem[e][o.dma_slot], 16)
                    elif o.signal:
                        s = o.sig_no - 1
                        ins.then_inc(csem[e][s // SEM_ROLL], 1)
                if e == "sp":
                    for d in final_waits:
                        sem, val = lower(d)
                        engobj.wait_ge(sem, val)

            @block.tensor
            def _(t):
                run("pe", t)

            @block.scalar
            def _(t):
                run("act", t)

            @block.vector
            def _(t):
                run("dve", t)

            @block.gpsimd
            def _(t):
                run("pool", t)

            @block.sync
            def _(t):
                run("sp", t)
S = 2048
D = 1024
KC = 8
TG = 512
NTG = 4
MEM = 256
FFH = 2816
EPS = 1e-6
WB_ELEMS = 4096
NWB = 3


def _prod(s):
    r = 1
    for v in s:
        r *= v
    return r


class KB:
    def __init__(self, nc, es, nseq, stages):
        self.nc = nc
        self.es = es
        self.P = Prog(nc)
        self.nseq = nseq
        self.stages = stages
        self.uid = 0
        self.ps = [es.enter_context(nc.psum_tensor(f"psb{i}", [128, 512], F32)) for i in range(8)]
        self.ps_res = [Res(f"ps{i}", excl=True) for i in range(8)]
        self.ps_rr = 0
        self.ARENA = 207 * 1024
        self.arena = es.enter_context(nc.sbuf_tensor("arena", [128, self.ARENA // 4], F32))
        self.aoff = 0

    def alloc(self, free_shape, dtype, name=None):
        esz = 4 if dtype == F32 else 2
        n = _prod(free_shape)
        sz = (n * esz + 63) // 64 * 64
        assert self.aoff + sz <= self.ARENA, f"arena overflow {name} {self.aoff + sz}"
        v = self.arena[:, self.aoff // 4:(self.aoff + sz) // 4]
        self.aoff += sz
        self.peak = max(getattr(self, "peak", 0), self.aoff)
        if dtype != F32:
            v = v.bitcast(dtype)
        v = v[:, 0:n]
        if len(free_shape) == 2:
            v = v.rearrange("p (a b) -> p a b", a=free_shape[0])
        elif len(free_shape) == 3:
            v = v.rearrange("p (a b c) -> p a b c", a=free_shape[0], b=free_shape[1])
        self.uid += 1
        return v, Res(f"{name}_{self.uid}")

    def mark(self):
        return self.aoff

    def release(self, m):
        self.aoff = m

    def bank(self):
        i = self.ps_rr
        self.ps_rr = (i + 1) % 5
        return self.ps[i][:], self.ps_res[i]

    def acc_bank(self):
        self.acc_rr = 1 - getattr(self, "acc_rr", 1)
        i = 6 + self.acc_rr
        return self.ps[i][:], self.ps_res[i]

    def op(self, eng, fn, reads=(), writes=()):
        return self.P.op(eng, fn, reads=reads, writes=writes)

    def mm(self, out, ores, lhsT, rhs, start, stop, reads, skip=False, ka=0):
        if skip:
            o = self.P.op("pe", lambda e: e.matmul(out, lhsT, rhs, start=start, stop=stop, skip_group_check=True), reads=reads, writes=[ores])
        else:
            o = self.P.op("pe", lambda e: e.matmul(out, lhsT, rhs, start=start, stop=stop), reads=reads, writes=[ores])
        o.ka = ka

    def barrier(self):
        self.P.barrier()

    def init_wring(self, n=NWB):
        self.wb = []
        for i in range(n):
            v, r = self.alloc([WB_ELEMS], BF16, f"wb{i}")
            self.wb.append((v, r))
        self.wrr = 0

    def wload(self, src_ap, shape):
        v, r = self.wb[self.wrr]
        self.wrr = (self.wrr + 1) % len(self.wb)
        n = _prod(shape)
        assert n <= WB_ELEMS
        vv = v[:, 0:n]
        if len(shape) == 2:
            vv = vv.rearrange("p (a b) -> p a b", a=shape[0])
        elif len(shape) == 3:
            vv = vv.rearrange("p (a b c) -> p a b c", a=shape[0], b=shape[1])
        if isinstance(src_ap, list):
            for i, sa in enumerate(src_ap):
                self.P.dma("pool", lambda e, i=i, sa=sa: e.dma_start(out=vv[:, :, i, :], in_=sa), writes=[r], extra=(i > 0))
        else:
            self.P.dma("pool", lambda e: e.dma_start(out=vv, in_=src_ap), writes=[r])
        return vv, r

    def setup_consts(self, d):
        nc = self.nc
        self.identf, self.r_identf = self.alloc([128], F32, "identf")
        self.identb, self.r_identb = self.alloc([128], BF16, "identb")
        self.onesb, self.r_onesb = self.alloc([128], BF16, "onesb")
        idf, idb, onb = self.identf, self.identb, self.onesb
        self.op("pool", lambda e: e.memset(idf, 0.0), writes=[self.r_identf])
        self.op("pool", lambda e: e.affine_select(idf, idf, pattern=[[-1, 128]], compare_op=ALU.not_equal,
                                                   fill=1.0, base=0, channel_multiplier=1),
                reads=[self.r_identf], writes=[self.r_identf])
        self.op("dve", lambda e: e.tensor_copy(idb, idf), reads=[self.r_identf], writes=[self.r_identb])
        self.op("pool", lambda e: e.memset(onb, 1.0), writes=[self.r_onesb])
        self.pv, self.r_pv = self.alloc([256], F32, "pv")
        self.g32, self.r_g32 = self.alloc([64], F32, "g32")
        self.hp, self.r_hp = self.alloc([32], F32, "hp")
        self.eps_vals = [1024, 128, 512, 1]
        self.epsc, self.r_epsc = self.alloc([4], F32, "epsc")
        self.nsq = self.alloc([8, 512], BF16, "nsq")
        self.nrr = self.alloc([512], F32, "nrr")
        m_tmp = self.mark()
        rowsA, rA = self.alloc([128], F32, "rowsA")
        rowsB, rB = self.alloc([128], F32, "rowsB")
        self.op("pool", lambda e: e.memset(rowsA, 0.0), writes=[rA])
        self.op("pool", lambda e: e.memset(rowsB, 0.0), writes=[rB])
        specs = [
            (d["norm_mix"].rearrange("l (k p) -> (l k) p", p=128), 0, 16),
            (d["norm_xattn"].rearrange("l (k p) -> (l k) p", p=128), 16, 16),
            (d["norm_ffn"].rearrange("l (k p) -> (l k) p", p=128), 32, 16),
            (d["mem_norm"].rearrange("(k p) -> k p", p=128), 48, 8),
            (d["final_norm"].rearrange("(k p) -> k p", p=128), 56, 8),
            (d["hgrn_lower_bounds"].rearrange("l (k p) -> (l k) p", p=128), 64, 12),
            (d["hgrn_out_norm"].rearrange("l (k p) -> (l k) p", p=128), 76, 4),
            (d["conv_dw_b"].rearrange("l (k p) -> (l k) p", p=128), 80, 4),
            (d["conv_ln_g"].rearrange("l (k p) -> (l k) p", p=128), 84, 4),
            (d["conv_ln_b"].rearrange("l (k p) -> (l k) p", p=128), 88, 4),
            (d["sgu_ln_g"].rearrange("l (k p) -> (l k) p", p=128), 92, 4),
            (d["sgu_ln_b"].rearrange("l (k p) -> (l k) p", p=128), 96, 4),
        ]
        for src, r0, n in specs:
            self.P.dma("sp", lambda e, src=src, r0=r0, n=n: e.dma_start(out=rowsA[r0:r0 + n, :], in_=src), writes=[rA])
        srcB = d["conv_dw_w"].rearrange("l j (k p) -> (l j k) p", p=128)
        self.P.dma("sp", lambda e: e.dma_start(out=rowsB[0:124, :], in_=srcB), writes=[rB])
        pv = self.pv
        b0, r0_ = self.bank()
        self.op("pe", lambda e: e.transpose(b0[:, 0:128], rowsA, idf), reads=[rA, self.r_identf], writes=[r0_])
        self.op("pe", lambda e: e.transpose(b0[:, 128:256], rowsB, idf), reads=[rB, self.r_identf], writes=[r0_])
        self.op("dve", lambda e: e.tensor_copy(pv, b0[:, 0:256]), reads=[r0_], writes=[self.r_pv])
        g32 = self.g32
        self.op("dve", lambda e: e.tensor_scalar(g32, pv[:, 0:64], 32.0, None, op0=ALU.mult), reads=[self.r_pv], writes=[self.r_g32])
        hp = self.hp
        self.op("act", lambda e: e.activation(hp[:, 12:24], pv[:, 64:76], AF.Exp), reads=[self.r_pv], writes=[self.r_hp])
        self.op("dve", lambda e: e.tensor_tensor(hp[:, 24:28], hp[:, 12:16], hp[:, 16:20], op=ALU.add), reads=[self.r_hp], writes=[self.r_hp])
        self.op("dve", lambda e: e.tensor_tensor(hp[:, 24:28], hp[:, 24:28], hp[:, 20:24], op=ALU.add), reads=[self.r_hp], writes=[self.r_hp])
        self.op("dve", lambda e: e.reciprocal(hp[:, 24:28], hp[:, 24:28]), reads=[self.r_hp], writes=[self.r_hp])
        self.op("dve", lambda e: e.tensor_tensor(hp[:, 0:4], hp[:, 12:16], hp[:, 24:28], op=ALU.mult), reads=[self.r_hp], writes=[self.r_hp])
        self.op("dve", lambda e: e.tensor_scalar(hp[:, 4:8], hp[:, 0:4], -1.0, 1.0, op0=ALU.mult, op1=ALU.add), reads=[self.r_hp], writes=[self.r_hp])
        self.op("dve", lambda e: e.tensor_scalar(hp[:, 8:12], pv[:, 76:80], float(np.sqrt(128.0)), None, op0=ALU.mult), reads=[self.r_pv, self.r_hp], writes=[self.r_hp])
        for i, v in enumerate(self.eps_vals):
            self.op("pool", lambda e, i=i, v=v: e.memset(self.epsc[:, i:i + 1], float(v * EPS)), writes=[self.r_epsc])
        self.barrier()
        self.release(m_tmp)
        self.kaw, r_kaw = self.alloc([512], BF16, "kaw")
        self.op("pool", lambda e: e.memset(self.kaw, 1.0), writes=[r_kaw])
        ka_out, kaw, onesb = self.ps[5][:], self.kaw, self.onesb
        self.P.keepalive = lambda pe: pe.matmul(ka_out, onesb, kaw, start=True, stop=True)
        self.barrier()
        self.consts_mark = self.mark()

    def gcol(self, base, l, k):
        c = base + l * 8 + k
        return self.g32[:, c:c + 1]

    def load_T(self, src, ntok, dst, r_dst_fn):
        m = self.mark()
        stg = [self.alloc([1024], F32, f"stg{i}") for i in range(4)]
        for i in range(ntok // 128):
            sv, sr = stg[i % 4]
            self.P.dma("sp", lambda e, i=i, sv=sv: e.dma_start(out=sv, in_=src[i * 128:(i + 1) * 128, :]), writes=[sr])
            for half in range(2):
                b, br = self.bank()
                for j in range(4):
                    k = half * 4 + j
                    self.op("pe", lambda e, b=b, j=j, k=k, sv=sv: e.transpose(b[:, j * 128:(j + 1) * 128], sv[:, k * 128:(k + 1) * 128], self.identf),
                            reads=[sr, self.r_identf], writes=[br])
                dv = dst[:, half * 4:half * 4 + 4, i * 128:(i + 1) * 128]
                bv = b.rearrange("p (a b) -> p a b", a=4)
                if half == 0:
                    self.op("dve", lambda e, dv=dv, bv=bv: e.tensor_copy(dv, bv), reads=[br], writes=r_dst_fn(i, half))
                else:
                    self.op("act", lambda e, dv=dv, bv=bv: e.activation(dv, bv, AF.Copy), reads=[br], writes=r_dst_fn(i, half))
        self.barrier()
        self.release(m)

    def rstd_rep(self, srcs, reads, n, sq, r_sq, rr, r_rr):
        nk = len(srcs)
        for k, s in enumerate(srcs):
            self.op("act", lambda e, k=k, s=s: e.activation(sq[:, k, :], s, AF.Square), reads=reads, writes=[r_sq])
        b, br = self.bank()
        for k in range(nk):
            self.mm(b, br, self.onesb, sq[:, k, :], k == 0, k == nk - 1, [r_sq, self.r_onesb])
        self.rsqrt_ps(b, float(n * EPS), rr, r_rr, br)

    def rsqrt_ps(self, src, eps_tot, rv, r_rv, r_src):
        self.op("act", lambda e: e.activation(rv, src, AF.Ln, bias=self.epsc[:, self.eps_idx(eps_tot)]), reads=[r_src, self.r_epsc], writes=[r_rv])
        self.op("act", lambda e: e.activation(rv, rv, AF.Exp, scale=-0.5), reads=[r_rv], writes=[r_rv])

    def eps_idx(self, v):
        i = self.eps_vals.index(round(v / EPS))
        return slice(i, i + 1)

    def norm_to(self, xT, rx_fn, gbase, l, hT, rh_fn, ntok=S):
        if not hasattr(self, "nsq"):
            raise RuntimeError("norm temps not allocated")
        sq, r_sq = self.nsq
        rrs = [self.nrr, self.nrr]
        tg_sz = min(512, ntok)
        for tg in range(ntok // tg_sz):
            sl = slice(tg * tg_sz, (tg + 1) * tg_sz)
            rr, r_rr = rrs[tg % 2]
            sqv = sq[:, :, 0:tg_sz]
            self.op("act", lambda e, sl=sl, sqv=sqv: e.activation(sqv, xT[:, :, sl], AF.Square), reads=rx_fn(None, tg), writes=[r_sq])
            b, br = self.bank()
            for k in range(8):
                self.mm(b[:, 0:tg_sz], br, self.onesb, sq[:, k, 0:tg_sz], k == 0, k == 7, [r_sq, self.r_onesb])
            rv = rr[:, 0:tg_sz]
            self.rsqrt_ps(b[:, 0:tg_sz], float(1024 * EPS), rv, r_rr, br)
            for k in range(8):
                g = self.gcol(gbase, l, k)
                self.op("dve", lambda e, k=k, sl=sl, g=g, rv=rv: e.scalar_tensor_tensor(hT[:, k, sl], xT[:, k, sl], g, rv, op0=ALU.mult, op1=ALU.mult),
                        reads=rx_fn(k, tg) + [r_rr, self.r_g32], writes=rh_fn(tg))

    def final_store(self, xT, rx_fn, out_ap):
        m = self.mark()
        sq, r_sq = self.alloc([8, 512], BF16, "fsq")
        rr, r_rr = self.alloc([512], F32, "frr")
        yT, r_y = self.alloc([8, 512], F32, "fy")
        stg = [self.alloc([1024], F32, f"fstg{i}") for i in range(3)]
        outs = []
        n = 0
        for tg in range(NTG):
            sl = slice(tg * 512, (tg + 1) * 512)
            self.op("act", lambda e, sl=sl: e.activation(sq, xT[:, :, sl], AF.Square), reads=rx_fn(None, tg), writes=[r_sq])
            b, br = self.bank()
            for k in range(8):
                self.mm(b, br, self.onesb, sq[:, k, :], k == 0, k == 7, [r_sq, self.r_onesb])
            self.rsqrt_ps(b, float(1024 * EPS), rr, r_rr, br)
            for k in range(8):
                g = self.g32[:, 56 + k:57 + k]
                self.op("dve", lambda e, k=k, sl=sl, g=g: e.scalar_tensor_tensor(yT[:, k, :], xT[:, k, sl], g, rr, op0=ALU.mult, op1=ALU.mult),
                        reads=rx_fn(k, tg) + [r_rr, self.r_g32], writes=[r_y])
            for tt in range(4):
                sv, sr = stg[n % 3]
                n += 1
                for half in range(2):
                    b2, br2 = self.bank()
                    for j in range(4):
                        k = half * 4 + j
                        self.op("pe", lambda e, b2=b2, j=j, k=k, tt=tt: e.transpose(b2[:, j * 128:(j + 1) * 128], yT[:, k, tt * 128:(tt + 1) * 128], self.identf),
                                reads=[r_y, self.r_identf], writes=[br2])
                    if half == 0:
                        self.op("dve", lambda e, sv=sv, b2=b2: e.tensor_copy(sv[:, 0:512], b2), reads=[br2], writes=[sr])
                    else:
                        self.op("act", lambda e, sv=sv, b2=b2: e.activation(sv[:, 512:1024], b2, AF.Copy), reads=[br2], writes=[sr])
                t0 = tg * 512 + tt * 128
                o = self.P.dma("sp", lambda e, sv=sv, t0=t0: e.dma_start(out=out_ap[t0:t0 + 128, :], in_=sv), reads=[sr])
                outs.append(("d", "sp", o.dma_slot, o.dma_cnt))
        self.barrier()
        self.release(m)
        return outs

    def ffn(self, xT, rx_fn, hT, rh_fn, w_in, w_out):
        m = self.mark()
        self.init_wring(7)
        hid = [self.alloc([4, 512], BF16, f"hid{i}") for i in range(3)]
        sa = [self.alloc([512], F32, f"sa{i}") for i in range(4)]
        w_in_v = w_in.rearrange("(k p) n -> p k n", p=128)
        w_out_v = w_out.rearrange("(j p) n -> p j n", p=128)
        nchunks = FFH // 128
        step = 0
        for c0 in range(0, nchunks, 4):
            nj = min(4, nchunks - c0)
            wa, r_wa = self.wload(w_in_v[:, :, c0 * 128:(c0 + nj) * 128], [8, nj * 128])
            wg, r_wg = self.wload(w_in_v[:, :, FFH + c0 * 128:FFH + (c0 + nj) * 128], [8, nj * 128])
            wo, r_wo = self.wload(w_out_v[:, c0:c0 + nj, :], [nj, 1024])
            for tg in range(NTG):
                sl = slice(tg * 512, (tg + 1) * 512)
                hv, r_hv = hid[step % 3]
                step += 1
                for j in range(nj):
                    pa, r_pa = self.bank()
                    for k in range(8):
                        self.mm(pa, r_pa, wa[:, k, j * 128:(j + 1) * 128], hT[:, k, sl], k == 0, k == 7, [r_wa] + rh_fn(tg))
                    pg, r_pg = self.bank()
                    for k in range(8):
                        self.mm(pg, r_pg, wg[:, k, j * 128:(j + 1) * 128], hT[:, k, sl], k == 0, k == 7, [r_wg] + rh_fn(tg))
                    sv, r_sv = sa[j % 4]
                    self.op("act", lambda e, sv=sv, pa=pa: e.activation(sv, pa, AF.Silu), reads=[r_pa], writes=[r_sv])
                    self.op("dve", lambda e, hv=hv, j=j, sv=sv, pg=pg: e.tensor_tensor(hv[:, j, :], sv, pg, op=ALU.mult),
                            reads=[r_sv, r_pg], writes=[r_hv])
                for oc in range(8):
                    po, r_po = self.bank()
                    for j in range(nj):
                        self.mm(po, r_po, wo[:, j, oc * 128:(oc + 1) * 128], hv[:, j, :], j == 0, j == nj - 1, [r_wo, r_hv])
                    self.op("dve", lambda e, oc=oc, sl=sl, po=po: e.tensor_tensor(xT[:, oc, sl], xT[:, oc, sl], po, op=ALU.add),
                            reads=[r_po] + rx_fn(oc, tg), writes=rx_fn(oc, tg))
        self.barrier()
        self.release(m)

    def evac(self, dst, src, reads, writes):
        self._ev = getattr(self, "_ev", 0) + 1
        if self._ev % 2 == 0:
            self.op("dve", lambda e: e.tensor_copy(dst, src), reads=reads, writes=writes)
        else:
            self.op("act", lambda e: e.activation(dst, src, AF.Copy), reads=reads, writes=writes)

    def init_xa(self):
        self.memnT, self.r_memn = self.alloc([8, MEM], BF16, "memnT")

    def prep_mem(self, mem_src):
        m = self.mark()
        memT, r_memT = self.alloc([8, MEM], F32, "memT")
        self.load_T(mem_src, MEM, memT, lambda i, half: [r_memT])
        self.norm_to(memT, lambda k, tg: [r_memT], 48, 0, self.memnT, lambda tg: [self.r_memn], ntok=MEM)
        self.barrier()
        self.release(m)

    def xattn(self, d, l):
        m = self.mark()
        self.init_wring(3)
        hT = self.hT
        qT, _ = self.alloc([8, S], BF16, "qT")
        r_q = [Res(f"q{t}") for t in range(NTG)]
        self.kT, self.r_kT = self.alloc([8, MEM], BF16, "kT")
        self.vtok, self.r_vtok = self.alloc([2, D], BF16, "vtok")
        wkv_v = d["xa_wkv"][l].rearrange("(k p) n -> p k n", p=128)
        wq_v = d["xa_wq"][l].rearrange("(k p) n -> p k n", p=128)
        wo_v = d["xa_wo"][l].rearrange("(k p) n -> p k n", p=128)
        for blk in range(2):
            w, rw = self.wload(wkv_v[:, :, blk * 512:(blk + 1) * 512], [8, 512])
            for j in range(4):
                c = blk * 4 + j
                b, br = self.bank()
                for k in range(8):
                    self.mm(b[:, 0:MEM], br, w[:, k, j * 128:(j + 1) * 128], self.memnT[:, k, :], k == 0, k == 7, [rw, self.r_memn])
                self.evac(self.kT[:, c, :], b[:, 0:MEM], [br], [self.r_kT])
        for blk in range(2):
            w, rw = self.wload(wkv_v[:, :, D + blk * 512:D + (blk + 1) * 512], [8, 512])
            for mc in range(2):
                b, br = self.bank()
                for k in range(8):
                    self.mm(b, br, self.memnT[:, k, mc * 128:(mc + 1) * 128], w[:, k, :], k == 0, k == 7, [rw, self.r_memn])
                self.evac(self.vtok[:, mc, blk * 512:(blk + 1) * 512], b, [br], [self.r_vtok])
        for blk in range(2):
            w, rw = self.wload(wq_v[:, :, blk * 512:(blk + 1) * 512], [8, 512])
            for tg in range(NTG):
                sl = slice(tg * 512, (tg + 1) * 512)
                for j in range(4):
                    c = blk * 4 + j
                    b, br = self.bank()
                    for k in range(8):
                        self.mm(b, br, w[:, k, j * 128:(j + 1) * 128], hT[:, k, sl], k == 0, k == 7, [rw] + self.rh_fn(tg))
                    self.evac(qT[:, c, sl], b, [br], [r_q[tg]])
        NPP = 3
        pT = [[self.alloc([512], BF16, f"pT{i}{j}") for j in range(2)] for i in range(NPP)]
        rec = [self.alloc([512], F32, f"rec{i}") for i in range(2)]
        items = [(tg, h) for tg in range(NTG) for h in range(4)]

        def scores(i):
            tg, h = items[i]
            sl = slice(tg * 512, (tg + 1) * 512)
            pp = pT[i % NPP]
            for mc in range(2):
                b, br = self.bank()
                for dc in range(2):
                    self.mm(b, br, self.kT[:, 2 * h + dc, mc * 128:(mc + 1) * 128], qT[:, 2 * h + dc, sl], dc == 0, dc == 1, [self.r_kT, r_q[tg]])
                pv_, r_pv_ = pp[mc]
                self.op("act", lambda e: e.activation(pv_, b, AF.Exp, scale=1.0 / 16.0), reads=[br], writes=[r_pv_])

        scores(0)
        for i, (tg, h) in enumerate(items):
            sl = slice(tg * 512, (tg + 1) * 512)
            if i + 1 < len(items):
                scores(i + 1)
            pp = pT[i % NPP]
            rc, r_rc = rec[i % 2]
            bd, brd = self.bank()
            for mc in range(2):
                self.mm(bd, brd, self.onesb, pp[mc][0], mc == 0, mc == 1, [self.r_onesb, pp[mc][1]])
            self.op("act", lambda e: e.activation(rc, bd, AF.Ln), reads=[brd], writes=[r_rc])
            self.op("act", lambda e: e.activation(rc, rc, AF.Exp, scale=-1.0), reads=[r_rc], writes=[r_rc])
            for dc in range(2):
                bo, bro = self.bank()
                for mc in range(2):
                    self.mm(bo, bro, self.vtok[:, mc, (2 * h + dc) * 128:(2 * h + dc + 1) * 128], pp[mc][0], mc == 0, mc == 1, [self.r_vtok, pp[mc][1]])
                self.op("dve", lambda e: e.tensor_tensor(hT[:, 2 * h + dc, sl], bo, rc, op=ALU.mult),
                        reads=[bro, r_rc], writes=self.rh_fn(tg))
        for blk in range(2):
            w, rw = self.wload(wo_v[:, :, blk * 512:(blk + 1) * 512], [8, 512])
            for tg in range(NTG):
                sl = slice(tg * 512, (tg + 1) * 512)
                for j in range(4):
                    c = blk * 4 + j
                    b, br = self.bank()
                    for k in range(8):
                        self.mm(b, br, w[:, k, j * 128:(j + 1) * 128], hT[:, k, sl], k == 0, k == 7, [rw] + self.rh_fn(tg))
                    self.op("dve", lambda e, c=c, sl=sl, b=b: e.tensor_tensor(self.xT[:, c, sl], self.xT[:, c, sl], b, op=ALU.add),
                            reads=[br] + self.rx_fn(c, tg), writes=self.rx_fn(c, tg))
        self.barrier()
        self.release(m)

    def setup_layer_consts(self, d):
        idf = self.identf
        trif, r_trif = self.alloc([128], F32, "trif")
        self.trib, self.r_trib = self.alloc([128], BF16, "trib")
        self.op("pool", lambda e: e.memset(trif, 1.0), writes=[r_trif])
        self.op("pool", lambda e: e.affine_select(trif, trif, pattern=[[1, 128]], compare_op=ALU.is_ge, fill=0.0, base=0, channel_multiplier=-1),
                reads=[r_trif], writes=[r_trif])
        self.op("dve", lambda e: e.tensor_copy(self.trib, trif), reads=[r_trif], writes=[self.r_trib])
        self.bdb, self.r_bdb = self.alloc([128], BF16, "bdb")
        self.op("pool", lambda e: e.memset(trif[0:64, 64:128], 0.0), reads=[r_trif], writes=[r_trif])
        self.op("dve", lambda e: e.tensor_copy(self.bdb, trif), reads=[r_trif], writes=[self.r_bdb])
        self.hmask, self.r_hmask = self.alloc([2], F32, "hmask")
        self.op("pool", lambda e: e.memset(self.hmask, 0.0), writes=[self.r_hmask])
        self.op("pool", lambda e: e.memset(self.hmask[0:64, 0:1], 1.0), reads=[self.r_hmask], writes=[self.r_hmask])
        self.op("pool", lambda e: e.memset(self.hmask[64:128, 1:2], 1.0), reads=[self.r_hmask], writes=[self.r_hmask])
        self.vb, self.r_vb = self.alloc([8, 8], F32, "vb")
        self.op("pool", lambda e: e.memset(self.vb, 0.0), writes=[self.r_vb])
        for jq in range(8):
            self.op("pool", lambda e, jq=jq: e.memset(self.vb[:, jq, jq:8], -1.0e30), reads=[self.r_vb], writes=[self.r_vb])
        self.vball, self.r_vball = self.alloc([256], F32, "vball")
        for qt in range(16):
            for hh in range(2):
                o_ = (qt * 2 + hh) * 8
                self.op("pool", lambda e: e.tensor_copy(self.vball[:, o_:o_ + 8], self.vb[:, qt // 2, :]), reads=[self.r_vb], writes=[self.r_vball])
        self.oh40, self.r_oh40 = self.alloc([8, 128], BF16, "oh40")
        self.op("pool", lambda e: e.memset(self.oh40, 0.0), writes=[self.r_oh40])
        self.op("dve", lambda e: e.tensor_copy(self.oh40[0:8], self.identb[0:8, 0:8].unsqueeze(2).to_broadcast([8, 8, 128])),
                reads=[self.r_identb, self.r_oh40], writes=[self.r_oh40])
        self.op("dve", lambda e: e.tensor_copy(self.oh40[32:40], self.identb[32:40, 32:40].unsqueeze(2).to_broadcast([8, 8, 128])),
                reads=[self.r_identb, self.r_oh40], writes=[self.r_oh40])
        self.m40, self.r_m40 = self.alloc([2], F32, "m40")
        self.op("pool", lambda e: e.memset(self.m40, 0.0), writes=[self.r_m40])
        self.op("pool", lambda e: e.memset(self.m40[0:32, 0:1], 1.0), reads=[self.r_m40], writes=[self.r_m40])
        self.op("pool", lambda e: e.memset(self.m40[32:64, 1:2], 1.0), reads=[self.r_m40], writes=[self.r_m40])
        self.sguw, self.r_sguw = self.alloc([4, 128], BF16, "sguw")
        self.brep, self.r_brep = self.alloc([4, 128], F32, "brep")
        self.P.dma("sp", lambda e: e.dma_start(out=self.brep, in_=d["sgu_b"][0].partition_broadcast(128)), writes=[self.r_brep])
        self.gs, self.r_gs = self.alloc([4], F32, "gs")
        self.op("dve", lambda e: e.tensor_scalar(self.gs, self.pv[:, 92:96], float(np.sqrt(128.0)), None, op0=ALU.mult), reads=[self.r_pv], writes=[self.r_gs])
        m = self.mark()
        wtmp, r_wtmp = self.alloc([4, 128], F32, "sgutmp")
        self.P.dma("sp", lambda e: e.dma_start(out=wtmp, in_=d["sgu_w"][0].rearrange("g t s -> t g s")), writes=[r_wtmp])
        for g in range(4):
            self.op("pool", lambda e, g=g: e.affine_select(wtmp[:, g, :], wtmp[:, g, :], pattern=[[-1, 128]], compare_op=ALU.is_ge, fill=0.0,
                                                            base=0, channel_multiplier=1), reads=[r_wtmp], writes=[r_wtmp])
        b, br = self.bank()
        for g in range(4):
            self.op("pe", lambda e, g=g: e.transpose(b[:, g * 128:(g + 1) * 128], wtmp[:, g, :], idf), reads=[r_wtmp, self.r_identf], writes=[br])
        self.op("dve", lambda e: e.tensor_copy(self.sguw, b.rearrange("p (g t) -> p g t", g=4)), reads=[br], writes=[self.r_sguw])
        self.barrier()
        self.release(m)

    def out_proj(self, oB, r_oB, w_out):
        wv = w_out.rearrange("(k p) n -> p k n", p=128)
        for blk in range(2):
            w, rw = self.wload(wv[:, :, blk * 512:(blk + 1) * 512], [8, 512])
            for tg in range(NTG):
                sl = slice(tg * 512, (tg + 1) * 512)
                for j in range(4):
                    c = blk * 4 + j
                    b, br = self.bank()
                    for k in range(8):
                        self.mm(b, br, w[:, k, j * 128:(j + 1) * 128], oB[:, k, sl], k == 0, k == 7, [rw, r_oB[tg]])
                    self.op("dve", lambda e, c=c, sl=sl, b=b: e.tensor_tensor(self.xT[:, c, sl], self.xT[:, c, sl], b, op=ALU.add),
                            reads=[br] + self.rx_fn(c, tg), writes=self.rx_fn(c, tg))

    def mixer1(self, d):
        m0 = self.mark()
        self.init_wring(2)
        hT = self.hT
        w_in = d["w_in_cd"][0]
        oB, _ = self.alloc([8, S], BF16, "oB")
        r_oB = [Res(f"oB{t}") for t in range(NTG)]
        m1 = self.mark()
        qz, r_qz = self.alloc([2, S], BF16, "qz")
        kc, r_kc = self.alloc([S], BF16, "kc")
        vt, r_vt = self.alloc([16, 130], BF16, "vt")
        kmean, r_kmean = self.alloc([8], BF16, "kmean")
        kmf, r_kmf = self.alloc([8], F32, "kmf")
        ms, r_ms = self.alloc([32, 8], F32, "ms")
        ms1, r_ms1 = self.alloc([32, 8], F32, "ms1")
        eq, r_eq = self.alloc([32, 8], F32, "eq")
        rmax, r_rmax = self.alloc([32], F32, "rmax")
        bqa, r_bqa = self.alloc([16, 40], BF16, "bqa")
        biasT, r_biasT = self.alloc([2, S], BF16, "biasTT")
        NPT = 5
        pT = [self.alloc([2, 256], BF16, f"mpT{i}") for i in range(NPT)]
        otok = [self.alloc([128], BF16, f"otok{i}") for i in range(2)]
        rcs = [self.alloc([2], F32, f"mrc{i}") for i in range(2)]
        self.op("pool", lambda e: e.memset(vt, 1.0), writes=[r_vt])
        self.op("pool", lambda e: e.memset(bqa, 0.0), writes=[r_bqa])
        wvv = w_in.rearrange("(k p) n -> p k n", p=128)
        pstep = 0
        for c in range(0 if "nomoba" in self.stages else 4):
            w, rw = self.wload([wvv[:, :, s_ * 512 + c * 128:s_ * 512 + (c + 1) * 128] for s_ in range(3)], [8, 3, 128])
            for tg in range(NTG):
                sl = slice(tg * 512, (tg + 1) * 512)
                b, br = self.bank()
                for k in range(8):
                    self.mm(b, br, w[:, k, 0, :], hT[:, k, sl], k == 0, k == 7, [rw] + self.rh_fn(tg))
                for hh in range(2):
                    self.op("act", lambda e: e.activation(qz[:, hh, sl], b, AF.Copy, scale=self.hmask[:, hh:hh + 1]), reads=[br, self.r_hmask], writes=[r_qz])
            for tg in range(NTG):
                sl = slice(tg * 512, (tg + 1) * 512)
                b, br = self.bank()
                for k in range(8):
                    self.mm(b, br, w[:, k, 1, :], hT[:, k, sl], k == 0, k == 7, [rw] + self.rh_fn(tg))
                self.evac(kc[:, sl], b, [br], [r_kc])
            for g4 in range(4):
                b, br = self.bank()
                for tt in range(4):
                    t0 = (g4 * 4 + tt) * 128
                    for k in range(8):
                        self.mm(b[:, tt * 128:(tt + 1) * 128], br, hT[:, k, t0:t0 + 128], w[:, k, 2, :], k == 0, k == 7, [rw] + self.rh_fn(g4))
                dstv = vt[:, g4 * 4:(g4 + 1) * 4, :].rearrange("p t (h e) -> p t h e", h=2)[:, :, :, 0:64]
                srcv = b.rearrange("p (t h e) -> p t h e", t=4, h=2)
                self.evac(dstv, srcv, [br], [r_vt])
            self.op("dve", lambda e: e.tensor_reduce(out=kmf, in_=kc.rearrange("p (n t) -> p n t", n=8), axis=AX.X, op=ALU.add),
                    reads=[r_kc], writes=[r_kmf])
            self.op("dve", lambda e: e.tensor_scalar(kmean, kmf, 1.0 / 256.0, None, op0=ALU.mult), reads=[r_kmf], writes=[r_kmean])
            bs, brs = self.bank()
            for qt in range(16):
                for hh in range(2):
                    self.mm(bs[:, (qt * 2 + hh) * 8:(qt * 2 + hh + 1) * 8], brs, qz[:, hh, qt * 128:(qt + 1) * 128], kmean, True, True, [r_qz, r_kmean])
            msv = ms.rearrange("p a n -> p (a n)")
            self.op("dve", lambda e: e.tensor_tensor(msv, bs[:, 0:256], self.vball, op=ALU.add), reads=[brs, self.r_vball], writes=[r_ms])
            src, r_src = ms, r_ms
            for rnd in range(2):
                self.op("dve", lambda e: e.tensor_reduce(out=rmax, in_=src, axis=AX.X, op=ALU.max), reads=[r_src], writes=[r_rmax])
                self.op("dve", lambda e: e.tensor_tensor(eq, src, rmax.unsqueeze(2).to_broadcast([128, 32, 8]), op=ALU.is_ge),
                        reads=[r_src, r_rmax], writes=[r_eq])
                self.op("dve", lambda e: e.scalar_tensor_tensor(ms1, eq, -3.0e30, src, op0=ALU.mult, op1=ALU.add),
                        reads=[r_eq, r_src], writes=[r_ms1])
                src, r_src = ms1, r_ms1
            self.op("dve", lambda e: e.tensor_reduce(out=rmax, in_=ms1, axis=AX.X, op=ALU.max), reads=[r_ms1], writes=[r_rmax])
            self.op("dve", lambda e: e.tensor_tensor(eq, ms, rmax.unsqueeze(2).to_broadcast([128, 32, 8]), op=ALU.is_ge),
                    reads=[r_ms, r_rmax], writes=[r_eq])
            eq4 = eq.rearrange("p (t h) n -> p t h n", h=2)
            for hh in range(2):
                self.op("dve", lambda e: e.tensor_scalar(bqa[:, :, hh * 32:hh * 32 + 8], eq4[:, :, hh, :], -1.0, 30000.0, op0=ALU.add, op1=ALU.mult),
                        reads=[r_eq], writes=[r_bqa])
            for tg in range(NTG):
                bt, brt = self.bank()
                for tt in range(4):
                    qt = tg * 4 + tt
                    self.mm(bt[0:40, tt * 128:(tt + 1) * 128], brt, bqa[:, qt, :], self.identb, True, True, [r_bqa, self.r_identb])
                for hh in range(2):
                    self.op("act", lambda e: e.activation(biasT[0:40, hh, tg * 512:(tg + 1) * 512], bt[0:40, :], AF.Copy, scale=self.m40[0:40, hh:hh + 1]),
                            reads=[brt, self.r_m40], writes=[r_biasT])
            for jq in range(0 if "noattn" in self.stages else 8):
                q0 = jq * 256
                pos = [self.acc_bank(), self.acc_bank()]
                nchunks = 2 * jq + 2
                def score_stage(ci):
                    pt, r_pt = pT[(pbase + ci) % NPT]
                    k0 = ci * 128
                    bsc, brsc = self.bank()
                    own = ci >= 2 * jq
                    isB = ci == 2 * jq + 1
                    if not own:
                        n = ci // 2
                        nobias = jq <= 3
                        self.mm(bsc, brsc, kc[:, k0:k0 + 128], qz[:, :, q0:q0 + 256], True, nobias, [r_kc, r_qz])
                        if not nobias:
                            self.mm(bsc, brsc, self.oh40[0:40, n, :], biasT[0:40, :, q0:q0 + 256], False, True, [self.r_oh40, r_biasT])
                        self.op("act", lambda e: e.activation(pt, bsc.rearrange("p (h q) -> p h q", h=2), AF.Exp, scale=0.125), reads=[brsc], writes=[r_pt])
                    elif not isB:
                        self.mm(bsc, brsc, kc[:, k0:k0 + 128], qz[:, :, q0:q0 + 256], True, True, [r_kc, r_qz])
                        self.op("act", lambda e: e.activation(pt, bsc.rearrange("p (h q) -> p h q", h=2), AF.Exp, scale=0.125), reads=[brsc], writes=[r_pt])
                        self.op("dve", lambda e: e.tensor_tensor(pt[:, :, 0:128], pt[:, :, 0:128], self.trib.unsqueeze(1).to_broadcast([128, 2, 128]), op=ALU.mult),
                                reads=[r_pt, self.r_trib], writes=[r_pt])
                    else:
                        self.mm(bsc[:, 0:256], brsc, kc[:, k0:k0 + 128], qz[:, :, q0 + 128:q0 + 256], True, True, [r_kc, r_qz])
                        self.op("act", lambda e: e.activation(pt[:, :, 128:256], bsc[:, 0:256].rearrange("p (h q) -> p h q", h=2), AF.Exp, scale=0.125),
                                reads=[brsc], writes=[r_pt])
                        self.op("dve", lambda e: e.tensor_tensor(pt[:, :, 128:256], pt[:, :, 128:256], self.trib.unsqueeze(1).to_broadcast([128, 2, 128]), op=ALU.mult),
                                reads=[r_pt, self.r_trib], writes=[r_pt])

                pbase = pstep
                LOOK = 2
                for ci in range(min(LOOK, nchunks)):
                    score_stage(ci)
                for ci in range(nchunks):
                    if ci + LOOK < nchunks:
                        score_stage(ci + LOOK)
                    pt, r_pt = pT[(pbase + ci) % NPT]
                    isB = ci == 2 * jq + 1
                    for hh in range(2):
                        po, rpo = pos[hh]
                        vv = vt[:, ci, hh * 65:(hh + 1) * 65]
                        if not isB:
                            self.mm(po[:, 0:65], rpo, pt[:, hh, 0:128], vv, ci == 0, ci == 2 * jq, [r_pt, r_vt], skip=True)
                        self.mm(po[:, 128:193], rpo, pt[:, hh, 128:256], vv, False, ci == nchunks - 1, [r_pt, r_vt], skip=True)
                pstep += nchunks
                for hh in range(2):
                    po, rpo = pos[hh]
                    hs = slice(hh * 64, (hh + 1) * 64)
                    rc, r_rc = rcs[hh]
                    pov = po.rearrange("p (t e) -> p t e", t=4)
                    self.op("dve", lambda e: e.reciprocal(rc, pov[:, 0:2, 64]), reads=[rpo], writes=[r_rc])
                    for t2 in range(2):
                        ot, r_ot = otok[t2]
                        self.op("dve", lambda e: e.tensor_scalar(ot[:, hs], po[:, t2 * 128:t2 * 128 + 64], rc[:, t2:t2 + 1], None, op0=ALU.mult),
                                reads=[rpo, r_rc], writes=[r_ot])
                bt2, brt2 = self.bank()
                btb2 = bt2.bitcast(BF16)
                for t2 in range(2):
                    ot, r_ot = otok[t2]
                    self.op("pe", lambda e: e.transpose(btb2[:, t2 * 128:(t2 + 1) * 128], ot, self.identb), reads=[r_ot, self.r_identb], writes=[brt2])
                self.evac(oB[:, c, q0:q0 + 256], btb2[:, 0:256], [brt2], [r_oB[jq // 2]])
        self.barrier()
        self.release(m1)
        def sgu_bufs(tag):
            B = {}
            B["uf"] = [self.alloc([512], F32, f"uf{tag}{i}") for i in range(2)]
            for nm, dt_ in (("zf", F32), ("zb", BF16), ("zc", F32), ("zq", BF16), ("rr", F32), ("zn", BF16), ("tmp", F32)):
                B[nm] = self.alloc([512], dt_, f"{nm}{tag}")
            B["znt"] = self.alloc([4, 128], BF16, f"znt{tag}")
            return B

        def sgu_gen(g, B):
            zf, r_zf = B["zf"]
            zb, r_zb = B["zb"]
            zc, r_zc = B["zc"]
            zq, r_zq = B["zq"]
            rr, r_rr = B["rr"]
            zn, r_zn = B["zn"]
            tmp, r_tmp = B["tmp"]
            znt, r_znt = B["znt"]
            w, rw = self.wload([wvv[:, :, 1536 + s_ * 512 + g * 128:1536 + s_ * 512 + (g + 1) * 128] for s_ in range(2)], [8, 2, 128])
            for tg in range(NTG):
                sl = slice(tg * 512, (tg + 1) * 512)
                uv, r_uv = B["uf"][tg % 2]
                pu, rpu = self.bank()
                for k in range(8):
                    self.mm(pu, rpu, w[:, k, 0, :], hT[:, k, sl], k == 0, k == 7, [rw] + self.rh_fn(tg))
                self.op("act", lambda e: e.activation(uv, pu, AF.Gelu), reads=[rpu], writes=[r_uv])
                pz, rpz = self.bank()
                for k in range(8):
                    self.mm(pz, rpz, w[:, k, 1, :], hT[:, k, sl], k == 0, k == 7, [rw] + self.rh_fn(tg))
                self.op("act", lambda e: e.activation(zf, pz, AF.Gelu), reads=[rpz], writes=[r_zf])
                yield
                self.op("dve", lambda e: e.tensor_copy(zb, zf), reads=[r_zf], writes=[r_zb])
                p1, rp1 = self.bank()
                self.mm(p1, rp1, self.onesb, zb, True, True, [self.r_onesb, r_zb])
                self.op("dve", lambda e: e.scalar_tensor_tensor(zc, p1, -1.0 / 128.0, zf, op0=ALU.mult, op1=ALU.add), reads=[rp1, r_zf], writes=[r_zc])
                yield
                self.op("act", lambda e: e.activation(zq, zc, AF.Square), reads=[r_zc], writes=[r_zq])
                p2, rp2 = self.bank()
                self.mm(p2, rp2, self.onesb, zq, True, True, [self.r_onesb, r_zq])
                self.rsqrt_ps(p2, float(128 * EPS), rr, r_rr, rp2)
                yield
                self.op("dve", lambda e: e.tensor_tensor(zc, zc, rr, op=ALU.mult), reads=[r_zc, r_rr], writes=[r_zc])
                self.op("act", lambda e: e.activation(zn, zc, AF.Identity, bias=self.pv[:, 96 + g:97 + g], scale=self.gs[:, g:g + 1]),
                        reads=[r_zc, self.r_pv, self.r_gs], writes=[r_zn])
                pt_, rpt_ = self.bank()
                ptb = pt_.bitcast(BF16)
                for tt in range(4):
                    self.op("pe", lambda e: e.transpose(ptb[:, tt * 128:(tt + 1) * 128], zn[:, tt * 128:(tt + 1) * 128], self.identb),
                            reads=[r_zn, self.r_identb], writes=[rpt_])
                self.evac(znt, ptb[:, 0:512].rearrange("p (t c) -> p t c", t=4), [rpt_], [r_znt])
                yield
                pm, rpm = self.bank()
                for tt in range(4):
                    self.mm(pm[:, tt * 128:(tt + 1) * 128], rpm, znt[:, tt, :], self.sguw[:, g, :], True, True, [r_znt, self.r_sguw])
                self.op("dve", lambda e: e.tensor_tensor(tmp.rearrange("p (t c) -> p t c", t=4), pm.rearrange("p (t c) -> p t c", t=4),
                                                          self.brep[:, g:g + 1, :].to_broadcast([128, 4, 128]), op=ALU.add),
                        reads=[rpm, self.r_brep], writes=[r_tmp])
                self.op("dve", lambda e: e.tensor_tensor(oB[:, 4 + g, sl], tmp, uv, op=ALU.mult), reads=[r_tmp, r_uv], writes=[r_oB[tg]])
                yield

        if "nosgu" not in self.stages:
            bufsets = [sgu_bufs("a"), sgu_bufs("b")]
            for gp in range(2):
                gens = [sgu_gen(2 * gp + i, bufsets[i]) for i in range(2)]
                alive = list(gens)
                lag = 2
                step = 0
                while alive:
                    for gi, gen in enumerate(gens):
                        if gen not in alive:
                            continue
                        if gi == 1 and step < lag:
                            continue
                        try:
                            next(gen)
                        except StopIteration:
                            alive.remove(gen)
                    step += 1
        if "dbg" in self.stages:
            o = self.P.dma("sp", lambda e: e.dma_start(out=d["dbg"], in_=oB), reads=r_oB)
            self.dbg_tok = ("d", "sp", o.dma_slot, o.dma_cnt)
        self.out_proj(oB, r_oB, d["w_out_cd"][0])
        self.barrier()
        self.release(m0)

    def mixer0(self, d):
        import os
        KA_T = int(os.environ.get("KA_T", "6"))
        KA_I = int(os.environ.get("KA_I", "10"))
        KA_N = int(os.environ.get("KA_N", "4"))
        m0 = self.mark()
        self.init_wring(2)
        hT = self.hT
        wvv = d["w_in_ab"][0].rearrange("(k p) n -> p k n", p=128)
        oB, _ = self.alloc([8, S], BF16, "oB0")
        r_oB = [Res(f"oB0_{t}") for t in range(NTG)]
        hp = self.hp
        m1 = self.mark()
        F = lambda n: self.alloc([512], F32, n)
        qs, r_qs = F("qs")
        ff, r_ff = F("ff")
        gate, r_gate = F("gate")
        bcum, r_bcum = F("bcum")
        rb, r_rb = F("rb")
        omf, r_omf = F("omf")
        kinf, r_kinf = F("kinf")
        rr, r_rr = F("hrr")
        otmp, r_otmp = F("otmp")
        zeros, r_zeros = self.alloc([64], F32, "zeros")
        vT, r_vT = self.alloc([512], BF16, "vT")
        qin, r_qin = self.alloc([512], BF16, "qin")
        kin, r_kin = self.alloc([512], BF16, "kin")
        kdT, r_kdT = self.alloc([512], BF16, "kdT")
        sqb, r_sqb = self.alloc([512], BF16, "hsq")
        vtok, r_vtok = self.alloc([4, 128], BF16, "hvtok")
        kdtok, r_kdtok = self.alloc([4, 128], BF16, "hkdtok")
        kdtok2, _ = self.alloc([4, 128], BF16, "hkdtok2")
        attm, r_attm = self.alloc([4, 128], BF16, "attm")
        Sall, r_Sall = self.alloc([9, 128], F32, "Sall")
        Sball, r_Sball = self.alloc([8, 128], BF16, "Sball")
        self.op("pool", lambda e: e.memset(zeros, 0.0), writes=[r_zeros])
        for h in range(0 if "nohgrn" in self.stages else 4):
            w, rw = self.wload([wvv[:, :, s_ * 512 + h * 128:s_ * 512 + (h + 1) * 128] for s_ in range(4)], [8, 4, 128])
            self.op("pool", lambda e: e.memset(Sall[:, 0, :], 0.0), writes=[r_Sall])
            for tg in range(NTG):
                sl = slice(tg * 512, (tg + 1) * 512)
                pb = []
                for s_ in range(4):
                    b, br = self.bank()
                    for k in range(8):
                        self.mm(b, br, w[:, k, s_, :], hT[:, k, sl], k == 0, k == 7, [rw] + self.rh_fn(tg))
                    pb.append((b, br))
                (pq, rpq), (pf, rpf), (pi, rpi), (pg, rpg) = pb
                self.op("act", lambda e: e.activation(qs, pq, AF.Silu), reads=[rpq], writes=[r_qs])
                self.op("act", lambda e: e.activation(ff, pf, AF.Sigmoid), reads=[rpf], writes=[r_ff])
                self.op("act", lambda e: e.activation(gate, pg, AF.Silu), reads=[rpg], writes=[r_gate])
                self.op("dve", lambda e: e.tensor_copy(vT, pi), reads=[rpi], writes=[r_vT])
                self.op("dve", lambda e: e.tensor_scalar(ff, ff, hp[:, 4 + h:5 + h], hp[:, h:h + 1], op0=ALU.mult, op1=ALU.add),
                        reads=[r_ff, self.r_hp], writes=[r_ff])
                for c in range(8):
                    cs = slice(c * 64, (c + 1) * 64)
                    self.op("dve", lambda e: e.tensor_tensor_scan(bcum[:, cs], ff[:, cs], zeros, 1.0, op0=ALU.mult, op1=ALU.add),
                            reads=[r_ff, r_zeros], writes=[r_bcum])
                self.op("act", lambda e: e.activation(rb, bcum, AF.Ln), reads=[r_bcum], writes=[r_rb])
                self.op("act", lambda e: e.activation(rb, rb, AF.Exp, scale=-1.0), reads=[r_rb], writes=[r_rb])
                self.op("dve", lambda e: e.tensor_scalar(omf, ff, -1.0, 1.0, op0=ALU.mult, op1=ALU.add), reads=[r_ff], writes=[r_omf])
                self.op("dve", lambda e: e.tensor_tensor(qin, qs, bcum, op=ALU.mult), reads=[r_qs, r_bcum], writes=[r_qin])
                self.op("dve", lambda e: e.tensor_tensor(kinf, omf, rb, op=ALU.mult), reads=[r_omf, r_rb], writes=[r_kinf])
                self.op("act", lambda e: e.activation(kin, kinf, AF.Copy), reads=[r_kinf], writes=[r_kin])
                blast = bcum.rearrange("p (c t) -> p c t", c=8)[:, :, 63:64]
                self.op("dve", lambda e: e.tensor_tensor(kdT.rearrange("p (c t) -> p c t", c=8), kinf.rearrange("p (c t) -> p c t", c=8),
                                                          blast.to_broadcast([128, 8, 64]), op=ALU.mult), reads=[r_kinf, r_bcum], writes=[r_kdT])
                import os
                HC = int(os.environ.get("DBG_H", "9"))
                if HC < 2:
                    continue
                for src, r_src, dst, r_dst in ((vT, r_vT, vtok, r_vtok), (kdT, r_kdT, kdtok, r_kdtok)):
                    bt, brt = self.bank()
                    btb = bt.bitcast(BF16)
                    for tt in range(4):
                        o_ = self.op("pe", lambda e: e.transpose(btb[:, tt * 128:(tt + 1) * 128], src[:, tt * 128:(tt + 1) * 128], self.identb),
                                     reads=[r_src, self.r_identb], writes=[brt])
                        if tt == 0:
                            o_.ka = KA_T
                    bview = btb[:, 0:512].rearrange("p (t c) -> p t c", t=4)
                    if dst is vtok:
                        self.evac(dst, bview, [brt], [r_dst])
                    else:
                        self.op("act", lambda e: e.activation(kdtok, bview, AF.Copy, scale=self.hmask[:, 0:1]), reads=[brt, self.r_hmask], writes=[r_kdtok])
                        self.op("act", lambda e: e.activation(kdtok2, bview, AF.Copy, scale=self.hmask[:, 1:2]), reads=[brt, self.r_hmask], writes=[r_kdtok])
                if HC < 3:
                    continue
                po, rpo = self.acc_bank()
                ba, bra = self.bank()
                for tt in range(4):
                    ts_ = slice(tt * 128, (tt + 1) * 128)
                    self.mm(ba[:, ts_], bra, kin[:, ts_], qin[:, ts_], True, True, [r_kin, r_qin])
                self.op("dve", lambda e: e.tensor_tensor(attm, ba.rearrange("p (t c) -> p t c", t=4),
                                                          self.bdb.unsqueeze(1).to_broadcast([128, 4, 128]), op=ALU.mult),
                        reads=[bra, self.r_bdb], writes=[r_attm])
                if HC == 31:
                    continue
                bks = [self.bank(), self.bank()]
                for c in range(8):
                    tt, half = c // 2, c % 2
                    bk, brk = bks[c // 4]
                    ps_ = slice(half * 64, (half + 1) * 64)
                    self.mm(bk[:, (c % 4) * 128:(c % 4 + 1) * 128], brk, (kdtok if half == 0 else kdtok2)[:, tt, :], vtok[:, tt, :], True, True, [r_kdtok, r_vtok])
                if HC == 32:
                    continue
                for c in range(8):
                    bk, brk = bks[c // 4]
                    self.op("dve", lambda e: e.scalar_tensor_tensor(Sall[:, c + 1, :], Sall[:, c, :], bcum[:, c * 64 + 63:c * 64 + 64],
                                                                     bk[:, (c % 4) * 128:(c % 4 + 1) * 128], op0=ALU.mult, op1=ALU.add),
                            reads=[r_Sall, r_bcum, brk], writes=[r_Sall])
                self.op("act", lambda e: e.activation(Sball, Sall[:, 0:8, :], AF.Copy), reads=[r_Sall], writes=[r_Sball])
                if HC == 33:
                    continue
                for c in range(8):
                    cs = slice(c * 64, (c + 1) * 64)
                    self.mm(po[:, cs], rpo, Sball[:, c, :], qin[:, cs], (c == 0), False, [r_Sball, r_qin], skip=True, ka=(KA_I if c == 0 else 0))
                for tt in range(4):
                    ts_ = slice(tt * 128, (tt + 1) * 128)
                    self.mm(po[:, ts_], rpo, vtok[:, tt, :], attm[:, tt, :], False, (tt == 3), [r_vtok, r_attm], skip=True)
                self.op("dve", lambda e: e.tensor_copy(Sall[:, 0, :], Sall[:, 8, :]), reads=[r_Sall], writes=[r_Sall])
                if HC < 4:
                    continue
                self.op("act", lambda e: e.activation(sqb, po, AF.Square), reads=[rpo], writes=[r_sqb])
                bss, brss = self.bank()
                self.mm(bss, brss, self.onesb, sqb, True, True, [self.r_onesb, r_sqb], ka=KA_N)
                self.rsqrt_ps(bss, float(128 * EPS), rr, r_rr, brss)
                self.op("dve", lambda e: e.tensor_tensor(otmp, po, rr, op=ALU.mult), reads=[rpo, r_rr], writes=[r_otmp])
                self.op("dve", lambda e: e.scalar_tensor_tensor(oB[:, h, sl], otmp, hp[:, 8 + h:9 + h], gate, op0=ALU.mult, op1=ALU.mult),
                        reads=[r_otmp, r_gate, self.r_hp], writes=[r_oB[tg]])
        self.barrier()
        self.release(m1)
        cpad, r_cpad = self.alloc([4, 30 + S], BF16, "cpad")
        sg2 = [self.alloc([512], F32, f"sg2{i}") for i in range(2)]
        self.op("pool", lambda e: e.memset(cpad[:, :, 0:30], 0.0), writes=[r_cpad])
        for cc in range(4):
            w, rw = self.wload([wvv[:, :, 2048 + s_ * 512 + cc * 128:2048 + s_ * 512 + (cc + 1) * 128] for s_ in range(2)], [8, 2, 128])
            for tg in range(NTG):
                sl = slice(tg * 512, (tg + 1) * 512)
                pa, rpa = self.bank()
                for k in range(8):
                    self.mm(pa, rpa, w[:, k, 0, :], hT[:, k, sl], k == 0, k == 7, [rw] + self.rh_fn(tg))
                pb_, rpb = self.bank()
                for k in range(8):
                    self.mm(pb_, rpb, w[:, k, 1, :], hT[:, k, sl], k == 0, k == 7, [rw] + self.rh_fn(tg))
                sg, r_sg = sg2[tg % 2]
                self.op("act", lambda e: e.activation(sg, pb_, AF.Sigmoid), reads=[rpb], writes=[r_sg])
                self.op("dve", lambda e: e.tensor_tensor(cpad[:, cc, 30 + tg * 512:30 + (tg + 1) * 512], pa, sg, op=ALU.mult),
                        reads=[rpa, r_sg], writes=[r_cpad])
        yv = self.hT_raw.rearrange("p (a b) -> p a b", a=4)
        diag, r_diag = self.alloc([31, 128], BF16, "diag")
        wv = self.pv[:, 128:252].rearrange("p (j c) -> p j c", c=4)
        for cc in range(4):
            self.op("dve", lambda e: e.tensor_tensor(diag, self.identb.unsqueeze(1).to_broadcast([128, 31, 128]),
                                                      wv[:, :, cc].unsqueeze(2).to_broadcast([128, 31, 128]), op=ALU.mult),
                    reads=[self.r_identb, self.r_pv], writes=[r_diag])
            for tg in range(NTG):
                pc, rpc = self.bank()
                for j in range(31):
                    self.mm(pc, rpc, diag[:, j, :], cpad[:, cc, tg * 512 + j:tg * 512 + j + 512], j == 0, j == 30, [r_diag, r_cpad])
                self.op("act", lambda e: e.activation(yv[:, cc, tg * 512:(tg + 1) * 512], pc, AF.Identity, bias=self.pv[:, 80 + cc:81 + cc]),
                        reads=[rpc, self.r_pv], writes=self.rh_fn(tg))
        ybs = [self.alloc([512], BF16, f"yb{i}") for i in range(2)]
        yqs = [self.alloc([512], BF16, f"yq{i}") for i in range(2)]
        mean, r_mean = self.alloc([512], F32, "cmean")
        var, r_var = self.alloc([512], F32, "cvar")
        for tg in range(NTG):
            sl = slice(tg * 512, (tg + 1) * 512)
            p1, rp1 = self.bank()
            p2, rp2 = self.bank()
            for cc in range(4):
                yb, r_yb = ybs[cc % 2]
                yq, r_yq = yqs[cc % 2]
                self.op("dve", lambda e: e.tensor_copy(yb, yv[:, cc, sl]), reads=self.rh_fn(tg), writes=[r_yb])
                self.op("act", lambda e: e.activation(yq, yv[:, cc, sl], AF.Square), reads=self.rh_fn(tg), writes=[r_yq])
                self.mm(p1, rp1, self.onesb, yb, cc == 0, cc == 3, [self.r_onesb, r_yb])
                self.mm(p2, rp2, self.onesb, yq, cc == 0, cc == 3, [self.r_onesb, r_yq])
            self.op("act", lambda e: e.activation(mean, p1, AF.Copy, scale=1.0 / 512.0), reads=[rp1], writes=[r_mean])
            self.op("dve", lambda e: e.tensor_tensor(var, mean, mean, op=ALU.mult), reads=[r_mean], writes=[r_var])
            self.op("dve", lambda e: e.scalar_tensor_tensor(var, p2, 1.0 / 512.0, var, op0=ALU.mult, op1=ALU.subtract), reads=[rp2, r_var], writes=[r_var])
            self.op("act", lambda e: e.activation(var, var, AF.Ln, bias=self.epsc[:, self.eps_idx(EPS)]), reads=[r_var, self.r_epsc], writes=[r_var])
            self.op("act", lambda e: e.activation(var, var, AF.Exp, scale=-0.5), reads=[r_var], writes=[r_var])
            ysl = yv[:, :, sl]
            self.op("dve", lambda e: e.tensor_tensor(ysl, ysl, mean.unsqueeze(1).to_broadcast([128, 4, 512]), op=ALU.subtract),
                    reads=self.rh_fn(tg) + [r_mean], writes=self.rh_fn(tg))
            self.op("dve", lambda e: e.tensor_tensor(ysl, ysl, var.unsqueeze(1).to_broadcast([128, 4, 512]), op=ALU.mult),
                    reads=self.rh_fn(tg) + [r_var], writes=self.rh_fn(tg))
            for cc in range(4):
                self.op("act", lambda e: e.activation(oB[:, 4 + cc, sl], yv[:, cc, sl], AF.Silu, bias=self.pv[:, 88 + cc:89 + cc], scale=self.pv[:, 84 + cc:85 + cc]),
                        reads=self.rh_fn(tg) + [self.r_pv], writes=[r_oB[tg]])
        if "dbg" in self.stages:
            o = self.P.dma("sp", lambda e: e.dma_start(out=d["dbg"], in_=oB), reads=r_oB)
        self.out_proj(oB, r_oB, d["w_out_ab"][0])
        self.barrier()
        self.release(m0)


    def init_stream(self):
        self.xT, _ = self.alloc([8, S], F32, "xT")
        self.rx = [[Res(f"x{k}_{tg}") for tg in range(NTG)] for k in range(8)]
        a0 = self.aoff
        self.hT, _ = self.alloc([8, S], BF16, "hT")
        self.hT_raw = self.arena[:, a0 // 4:a0 // 4 + 4 * S]
        self.rh = [Res(f"h{tg}") for tg in range(NTG)]

    def rx_fn(self, k, tg):
        ks = range(8) if k is None else [k]
        tgs = range(NTG) if tg is None else [tg]
        return [self.rx[a][b] for a in ks for b in tgs]

    def rh_fn(self, tg):
        return [self.rh[t] for t in (range(NTG) if tg is None else [tg])]


INPUT_NAMES = ["x", "mem", "norm_mix", "norm_xattn", "norm_ffn", "mem_norm", "final_norm",
               "w_in_ab", "w_out_ab", "hgrn_lower_bounds", "hgrn_out_norm", "conv_dw_w", "conv_dw_b",
               "conv_ln_g", "conv_ln_b", "w_in_cd", "w_out_cd", "sgu_ln_g", "sgu_ln_b", "sgu_w", "sgu_b",
               "xa_wq", "xa_wkv", "xa_wo", "ffn_w_in", "ffn_w_out"]


def build_program(shapes, nseq, stages):
    nc = bass.Bass("TRN2", target_bir_lowering=False)
    d = {}
    for n in INPUT_NAMES:
        shp = list(shapes[n])
        if n in ("x", "mem"):
            shp[0] = nseq
        d[n] = nc.dram_tensor(n, shp, F32, kind="ExternalInput").ap()
    out = nc.dram_tensor("out", [nseq, S, D], F32, kind="ExternalOutput").ap()
    if "dbg" in stages:
        d["dbg"] = nc.dram_tensor("dbg", [128, 8, S], BF16, kind="ExternalOutput").ap()
    with ExitStack() as es:
        kb = KB(nc, es, nseq, stages)
        kb.setup_consts(d)
        kb.init_stream()
        kb.init_xa()
        if hasattr(kb, "setup_layer_consts"):
            kb.setup_layer_consts(d)
        finals = []
        for s in range(nseq):
            kb.load_T(d["x"][s], S, kb.xT, lambda i, half: [kb.rx[k][i // 4] for k in range(half * 4, half * 4 + 4)])
            if "xa" in stages:
                kb.prep_mem(d["mem"][s])
            for l in range(2):
                if f"mix{l}" in stages:
                    kb.norm_to(kb.xT, kb.rx_fn, 0, l, kb.hT, kb.rh_fn)
                    if "nomix" in stages:
                        pass
                    elif l == 0:
                        kb.mixer0(d)
                    else:
                        kb.mixer1(d)
                if "xa" in stages or f"xa{l}" in stages:
                    kb.norm_to(kb.xT, kb.rx_fn, 16, l, kb.hT, kb.rh_fn)
                    kb.xattn(d, l)
                if "ffn" in stages or f"ffn{l}" in stages:
                    kb.norm_to(kb.xT, kb.rx_fn, 32, l, kb.hT, kb.rh_fn)
                    kb.ffn(kb.xT, kb.rx_fn, kb.hT, kb.rh_fn, d["ffn_w_in"][l], d["ffn_w_out"][l])
            finals += kb.final_store(kb.xT, kb.rx_fn, out[s])
        kb.P.emit(final_waits=finals)
    return nc


ALL_STAGES = ("mix0", "mix1", "xa", "ffn")
_CACHE = {}


def kernel(**inputs):
    n_cores = 8
    nseq = 2
    shapes = {k: np.shape(v) for k, v in inputs.items()}
    key = "full"
    if key not in _CACHE:
        _CACHE[key] = build_program(shapes, nseq, ALL_STAGES)
    nc = _CACHE[key]
    arrs = {k: np.ascontiguousarray(np.asarray(v, dtype=np.float32)) for k, v in inputs.items()}
    in_maps = []
    for c in range(n_cores):
        m = {}
        for k in INPUT_NAMES:
            if k in ("x", "mem"):
                m[k] = np.ascontiguousarray(arrs[k][c * nseq:(c + 1) * nseq])
            else:
                m[k] = arrs[k]
        in_maps.append(m)
    res = run_bass_kernel_spmd(nc, in_maps, core_ids=list(range(n_cores)))
    outs = [np.asarray(r["out"]) for r in res.results]
    return np.concatenate(outs, axis=0).astype(np.float32)
```

```python
from contextlib import ExitStack
import numpy as np
import concourse.bass as bass
import concourse.mybir as mybir
from concourse.bass_utils import run_bass_kernel_spmd

F32 = mybir.dt.float32
BF16 = mybir.dt.bfloat16
AF = mybir.ActivationFunctionType
ALU = mybir.AluOpType
AX = mybir.AxisListType

ENG = ("pe", "act", "dve", "pool", "sp")
SEM_ROLL = 12000
N_DMA_SEMS = 8
EMBED_WAIT = True


import types


def _freeze(fn):
    if getattr(fn, "__closure__", None) is None:
        return fn
    cells = []
    for c in fn.__closure__:
        try:
            cells.append(types.CellType(c.cell_contents))
        except ValueError:
            cells.append(c)
    return types.FunctionType(fn.__code__, fn.__globals__, fn.__name__, fn.__defaults__, tuple(cells))


class Res:
    __slots__ = ("name", "w", "r", "wx", "excl")

    def __init__(self, name, excl=False):
        self.name = name
        self.excl = excl
        self.w = None
        self.r = []
        self.wx = []


class Op:
    __slots__ = ("eng", "fn", "waits", "signal", "sig_no", "dma_slot", "dma_cnt", "idx", "ka")

    def __init__(self, eng, fn):
        self.eng = eng
        self.fn = _freeze(fn)
        self.waits = []
        self.signal = False
        self.sig_no = None
        self.dma_slot = None
        self.dma_cnt = None
        self.idx = None
        self.ka = 0


class Prog:
    def __init__(self, nc):
        self.nc = nc
        self.ops = {e: [] for e in ENG}
        self.dma_rr = {e: 0 for e in ENG}
        self.dma_count = {e: [0] * N_DMA_SEMS for e in ENG}
        self.pending = {}
        self.keepalive = None

    def barrier(self):
        toks = []
        for e in ENG:
            for o in reversed(self.ops[e]):
                if o.dma_slot is None:
                    toks.append(("c", e, o.idx))
                    break
            for slot, cnt in enumerate(self.dma_count[e]):
                if cnt > 0:
                    toks.append(("d", e, slot, cnt))
        self.pending = {e: list(toks) for e in ENG}

    def _pend(self, eng):
        t = self.pending.pop(eng, [])
        return [d for d in t if not (d[0] == "c" and d[1] == eng)]

    def _deps(self, eng, reads, writes, is_dma):
        deps = []
        for r in reads:
            if r.w is not None:
                deps.append(r.w)
            deps.extend(r.wx)
            if r.excl:
                deps.extend(t for t in r.r if t[1] != eng)
        for w in writes:
            if w.w is not None:
                deps.append(w.w)
            deps.extend(w.wx)
            deps.extend(w.r)
        out = []
        for d in deps:
            if d[0] == "c":
                if d[1] == eng and not is_dma:
                    if eng == "pe":
                        continue
                out.append(d)
            else:
                out.append(d)
        return out

    def op(self, eng, fn, reads=(), writes=()):
        o = Op(eng, fn)
        o.idx = len(self.ops[eng])
        o.waits = self._deps(eng, reads, writes, False)
        if eng != "pe":
            raw = set()
            for r in reads:
                if r.w is not None and r.w[0] == "c" and r.w[1] == eng:
                    raw.add(r.w)
            o.waits = [d for d in o.waits if not (d[0] == "c" and d[1] == eng and d not in raw)]
        o.waits = o.waits + self._pend(eng)
        self.ops[eng].append(o)
        tok = ("c", eng, o.idx)
        for r in reads:
            r.r.append(tok)
        for w in writes:
            w.w = tok
            w.r = []
            w.wx = []
        return o

    def dma(self, qeng, fn, reads=(), writes=(), extra=False):
        o = Op(qeng, fn)
        o.idx = len(self.ops[qeng])
        o.waits = self._deps(qeng, reads, () if extra else writes, True) + self._pend(qeng)
        slot = self.dma_rr[qeng]
        self.dma_rr[qeng] = (slot + 1) % N_DMA_SEMS
        prev = self.dma_count[qeng][slot]
        if prev > 0:
            o.waits.append(("d", qeng, slot, prev))
        self.dma_count[qeng][slot] = prev + 1
        o.dma_slot = slot
        o.dma_cnt = prev + 1
        self.ops[qeng].append(o)
        tok = ("d", qeng, slot, prev + 1)
        for r in reads:
            r.r.append(tok)
        for w in writes:
            if extra:
                w.wx.append(tok)
            else:
                w.w = tok
                w.r = []
                w.wx = []
        return o

    def emit(self, final_waits=()):
        nc = self.nc
        for e in ENG:
            for o in self.ops[e]:
                for d in o.waits:
                    if d[0] == "c":
                        self.ops[d[1]][d[2]].signal = True
        for d in final_waits:
            if d[0] == "c":
                self.ops[d[1]][d[2]].signal = True
        nsig = {}
        for e in ENG:
            c = 0
            for o in self.ops[e]:
                if o.signal:
                    c += 1
                    o.sig_no = c
            nsig[e] = c
        from contextlib import ExitStack
        with ExitStack() as es:
            csem = {}
            for e in ENG:
                n = max(1, -(-nsig[e] // SEM_ROLL))
                csem[e] = [es.enter_context(nc.semaphore(f"c_{e}_{i}")) for i in range(n)]
            dsem = {}
            for e in ENG:
                if any(self.dma_count[e]):
                    dsem[e] = [es.enter_context(nc.semaphore(f"d_{e}_{i}")) for i in range(N_DMA_SEMS)]
            block = es.enter_context(nc.Block())

            def lower(d):
                if d[0] == "c":
                    s = self.ops[d[1]][d[2]].sig_no - 1
                    return (csem[d[1]][s // SEM_ROLL], s % SEM_ROLL + 1)
                return (dsem[d[1]][d[2]], 16 * d[3])

            def run(e, engobj):
                seen = {}
                for o in self.ops[e]:
                    need = {}
                    for d in o.waits:
                        sem, val = lower(d)
                        k = id(sem)
                        if seen.get(k, 0) >= val:
                            continue
                        if k not in need or need[k][1] < val:
                            need[k] = (sem, val)
                    if e == "pe" and o.ka and self.keepalive is not None:
                        for _ in range(o.ka):
                            self.keepalive(engobj)
                    items_ = list(need.items())
                    embed = None
                    if EMBED_WAIT and o.dma_slot is None and e in ("pe", "act", "dve") and items_:
                        embed = items_.pop()
                    for k, (sem, val) in items_:
                        engobj.wait_ge(sem, val)
                        seen[k] = val
                    ins = o.fn(engobj)
                    if embed is not None:
                        k, (sem, val) = embed
                        ins._wait_ge(sem, val)
                        seen[k] = val
                    if o.dma_slot is not None:
                        ins.then_inc(dsem[e][o.dma_slot], 16)
                    elif o.signal:
                        s = o.sig_no - 1
                        ins.then_inc(csem[e][s // SEM_ROLL], 1)
                if e == "sp":
                    for d in final_waits:
                        sem, val = lower(d)
                        engobj.wait_ge(sem, val)

            @block.tensor
            def _(t):
                run("pe", t)

            @block.scalar
            def _(t):
                run("act", t)

            @block.vector
            def _(t):
                run("dve", t)

            @block.gpsimd
            def _(t):
                run("pool", t)

            @block.sync
            def _(t):
                run("sp", t)
S = 2048
D = 1024
KC = 8
TG = 512
NTG = 4
MEM = 256
FFH = 2816
EPS = 1e-6
WB_ELEMS = 4096
NWB = 3


def _prod(s):
    r = 1
    for v in s:
        r *= v
    return r


class KB:
    def __init__(self, nc, es, nseq, stages):
        self.nc = nc
        self.es = es
        self.P = Prog(nc)
        self.nseq = nseq
        self.stages = stages
        self.uid = 0
        self.ps = [es.enter_context(nc.psum_tensor(f"psb{i}", [128, 512], F32)) for i in range(8)]
        self.ps_res = [Res(f"ps{i}", excl=True) for i in range(8)]
        self.ps_rr = 0
        self.ARENA = 207 * 1024
        self.arena = es.enter_context(nc.sbuf_tensor("arena", [128, self.ARENA // 4], F32))
        self.aoff = 0

    def alloc(self, free_shape, dtype, name=None):
        esz = 4 if dtype == F32 else 2
        n = _prod(free_shape)
        sz = (n * esz + 63) // 64 * 64
        assert self.aoff + sz <= self.ARENA, f"arena overflow {name} {self.aoff + sz}"
        v = self.arena[:, self.aoff // 4:(self.aoff + sz) // 4]
        self.aoff += sz
        self.peak = max(getattr(self, "peak", 0), self.aoff)
        if dtype != F32:
            v = v.bitcast(dtype)
        v = v[:, 0:n]
        if len(free_shape) == 2:
            v = v.rearrange("p (a b) -> p a b", a=free_shape[0])
        elif len(free_shape) == 3:
            v = v.rearrange("p (a b c) -> p a b c", a=free_shape[0], b=free_shape[1])
        self.uid += 1
        return v, Res(f"{name}_{self.uid}")

    def mark(self):
        return self.aoff

    def release(self, m):
        self.aoff = m

    def bank(self):
        i = self.ps_rr
        self.ps_rr = (i + 1) % 5
        return self.ps[i][:], self.ps_res[i]

    def acc_bank(self):
        self.acc_rr = 1 - getattr(self, "acc_rr", 1)
        i = 6 + self.acc_rr
        return self.ps[i][:], self.ps_res[i]

    def op(self, eng, fn, reads=(), writes=()):
        return self.P.op(eng, fn, reads=reads, writes=writes)

    def mm(self, out, ores, lhsT, rhs, start, stop, reads, skip=False, ka=0):
        if skip:
            o = self.P.op("pe", lambda e: e.matmul(out, lhsT, rhs, start=start, stop=stop, skip_group_check=True), reads=reads, writes=[ores])
        else:
            o = self.P.op("pe", lambda e: e.matmul(out, lhsT, rhs, start=start, stop=stop), reads=reads, writes=[ores])
        o.ka = ka

    def barrier(self):
        self.P.barrier()

    def init_wring(self, n=NWB):
        self.wb = []
        for i in range(n):
            v, r = self.alloc([WB_ELEMS], BF16, f"wb{i}")
            self.wb.append((v, r))
        self.wrr = 0

    def wload(self, src_ap, shape):
        v, r = self.wb[self.wrr]
        self.wrr = (self.wrr + 1) % len(self.wb)
        n = _prod(shape)
        assert n <= WB_ELEMS
        vv = v[:, 0:n]
        if len(shape) == 2:
            vv = vv.rearrange("p (a b) -> p a b", a=shape[0])
        elif len(shape) == 3:
            vv = vv.rearrange("p (a b c) -> p a b c", a=shape[0], b=shape[1])
        if isinstance(src_ap, list):
            for i, sa in enumerate(src_ap):
                self.P.dma("pool", lambda e, i=i, sa=sa: e.dma_start(out=vv[:, :, i, :], in_=sa), writes=[r], extra=(i > 0))
        else:
            self.P.dma("pool", lambda e: e.dma_start(out=vv, in_=src_ap), writes=[r])
        return vv, r

    def setup_consts(self, d):
        nc = self.nc
        self.identf, self.r_identf = self.alloc([128], F32, "identf")
        self.identb, self.r_identb = self.alloc([128], BF16, "identb")
        self.onesb, self.r_onesb = self.alloc([128], BF16, "onesb")
        idf, idb, onb = self.identf, self.identb, self.onesb
        self.op("pool", lambda e: e.memset(idf, 0.0), writes=[self.r_identf])
        self.op("pool", lambda e: e.affine_select(idf, idf, pattern=[[-1, 128]], compare_op=ALU.not_equal,
                                                   fill=1.0, base=0, channel_multiplier=1),
                reads=[self.r_identf], writes=[self.r_identf])
        self.op("dve", lambda e: e.tensor_copy(idb, idf), reads=[self.r_identf], writes=[self.r_identb])
        self.op("pool", lambda e: e.memset(onb, 1.0), writes=[self.r_onesb])
        self.pv, self.r_pv = self.alloc([256], F32, "pv")
        self.g32, self.r_g32 = self.alloc([64], F32, "g32")
        self.hp, self.r_hp = self.alloc([32], F32, "hp")
        self.eps_vals = [1024, 128, 512, 1]
        self.epsc, self.r_epsc = self.alloc([4], F32, "epsc")
        self.nsq = self.alloc([8, 512], BF16, "nsq")
        self.nrr = self.alloc([512], F32, "nrr")
        m_tmp = self.mark()
        rowsA, rA = self.alloc([128], F32, "rowsA")
        rowsB, rB = self.alloc([128], F32, "rowsB")
        self.op("pool", lambda e: e.memset(rowsA, 0.0), writes=[rA])
        self.op("pool", lambda e: e.memset(rowsB, 0.0), writes=[rB])
        specs = [
            (d["norm_mix"].rearrange("l (k p) -> (l k) p", p=128), 0, 16),
            (d["norm_xattn"].rearrange("l (k p) -> (l k) p", p=128), 16, 16),
            (d["norm_ffn"].rearrange("l (k p) -> (l k) p", p=128), 32, 16),
            (d["mem_norm"].rearrange("(k p) -> k p", p=128), 48, 8),
            (d["final_norm"].rearrange("(k p) -> k p", p=128), 56, 8),
            (d["hgrn_lower_bounds"].rearrange("l (k p) -> (l k) p", p=128), 64, 12),
            (d["hgrn_out_norm"].rearrange("l (k p) -> (l k) p", p=128), 76, 4),
            (d["conv_dw_b"].rearrange("l (k p) -> (l k) p", p=128), 80, 4),
            (d["conv_ln_g"].rearrange("l (k p) -> (l k) p", p=128), 84, 4),
            (d["conv_ln_b"].rearrange("l (k p) -> (l k) p", p=128), 88, 4),
            (d["sgu_ln_g"].rearrange("l (k p) -> (l k) p", p=128), 92, 4),
            (d["sgu_ln_b"].rearrange("l (k p) -> (l k) p", p=128), 96, 4),
        ]
        for src, r0, n in specs:
            self.P.dma("sp", lambda e, src=src, r0=r0, n=n: e.dma_start(out=rowsA[r0:r0 + n, :], in_=src), writes=[rA])
        srcB = d["conv_dw_w"].rearrange("l j (k p) -> (l j k) p", p=128)
        self.P.dma("sp", lambda e: e.dma_start(out=rowsB[0:124, :], in_=srcB), writes=[rB])
        pv = self.pv
        b0, r0_ = self.bank()
        self.op("pe", lambda e: e.transpose(b0[:, 0:128], rowsA, idf), reads=[rA, self.r_identf], writes=[r0_])
        self.op("pe", lambda e: e.transpose(b0[:, 128:256], rowsB, idf), reads=[rB, self.r_identf], writes=[r0_])
        self.op("dve", lambda e: e.tensor_copy(pv, b0[:, 0:256]), reads=[r0_], writes=[self.r_pv])
        g32 = self.g32
        self.op("dve", lambda e: e.tensor_scalar(g32, pv[:, 0:64], 32.0, None, op0=ALU.mult), reads=[self.r_pv], writes=[self.r_g32])
        hp = self.hp
        self.op("act", lambda e: e.activation(hp[:, 12:24], pv[:, 64:76], AF.Exp), reads=[self.r_pv], writes=[self.r_hp])
        self.op("dve", lambda e: e.tensor_tensor(hp[:, 24:28], hp[:, 12:16], hp[:, 16:20], op=ALU.add), reads=[self.r_hp], writes=[self.r_hp])
        self.op("dve", lambda e: e.tensor_tensor(hp[:, 24:28], hp[:, 24:28], hp[:, 20:24], op=ALU.add), reads=[self.r_hp], writes=[self.r_hp])
        self.op("dve", lambda e: e.reciprocal(hp[:, 24:28], hp[:, 24:28]), reads=[self.r_hp], writes=[self.r_hp])
        self.op("dve", lambda e: e.tensor_tensor(hp[:, 0:4], hp[:, 12:16], hp[:, 24:28], op=ALU.mult), reads=[self.r_hp], writes=[self.r_hp])
        self.op("dve", lambda e: e.tensor_scalar(hp[:, 4:8], hp[:, 0:4], -1.0, 1.0, op0=ALU.mult, op1=ALU.add), reads=[self.r_hp], writes=[self.r_hp])
        self.op("dve", lambda e: e.tensor_scalar(hp[:, 8:12], pv[:, 76:80], float(np.sqrt(128.0)), None, op0=ALU.mult), reads=[self.r_pv, self.r_hp], writes=[self.r_hp])
        for i, v in enumerate(self.eps_vals):
            self.op("pool", lambda e, i=i, v=v: e.memset(self.epsc[:, i:i + 1], float(v * EPS)), writes=[self.r_epsc])
        self.barrier()
        self.release(m_tmp)
        self.kaw, r_kaw = self.alloc([512], BF16, "kaw")
        self.op("pool", lambda e: e.memset(self.kaw, 1.0), writes=[r_kaw])
        ka_out, kaw, onesb = self.ps[5][:], self.kaw, self.onesb
        self.P.keepalive = lambda pe: pe.matmul(ka_out, onesb, kaw, start=True, stop=True)
        self.barrier()
        self.consts_mark = self.mark()

    def gcol(self, base, l, k):
        c = base + l * 8 + k
        return self.g32[:, c:c + 1]

    def load_T(self, src, ntok, dst, r_dst_fn):
        m = self.mark()
        stg = [self.alloc([1024], F32, f"stg{i}") for i in range(8)]
        for i in range(ntok // 128):
            sv, sr = stg[i % 8]
            self.P.dma("sp", lambda e, i=i, sv=sv: e.dma_start(out=sv, in_=src[i * 128:(i + 1) * 128, :]), writes=[sr])
            for half in range(2):
                b, br = self.bank()
                for j in range(4):
                    k = half * 4 + j
                    self.op("pe", lambda e, b=b, j=j, k=k, sv=sv: e.transpose(b[:, j * 128:(j + 1) * 128], sv[:, k * 128:(k + 1) * 128], self.identf),
                            reads=[sr, self.r_identf], writes=[br])
                dv = dst[:, half * 4:half * 4 + 4, i * 128:(i + 1) * 128]
                bv = b.rearrange("p (a b) -> p a b", a=4)
                if half == 0:
                    self.op("dve", lambda e, dv=dv, bv=bv: e.tensor_copy(dv, bv), reads=[br], writes=r_dst_fn(i, half))
                else:
                    self.op("act", lambda e, dv=dv, bv=bv: e.activation(dv, bv, AF.Copy), reads=[br], writes=r_dst_fn(i, half))
        self.barrier()
        self.release(m)

    def rstd_rep(self, srcs, reads, n, sq, r_sq, rr, r_rr):
        nk = len(srcs)
        for k, s in enumerate(srcs):
            self.op("act", lambda e, k=k, s=s: e.activation(sq[:, k, :], s, AF.Square), reads=reads, writes=[r_sq])
        b, br = self.bank()
        for k in range(nk):
            self.mm(b, br, self.onesb, sq[:, k, :], k == 0, k == nk - 1, [r_sq, self.r_onesb])
        self.rsqrt_ps(b, float(n * EPS), rr, r_rr, br)

    def rsqrt_ps(self, src, eps_tot, rv, r_rv, r_src):
        self.op("act", lambda e: e.activation(rv, src, AF.Ln, bias=self.epsc[:, self.eps_idx(eps_tot)]), reads=[r_src, self.r_epsc], writes=[r_rv])
        self.op("act", lambda e: e.activation(rv, rv, AF.Exp, scale=-0.5), reads=[r_rv], writes=[r_rv])

    def eps_idx(self, v):
        i = self.eps_vals.index(round(v / EPS))
        return slice(i, i + 1)

    def norm_to(self, xT, rx_fn, gbase, l, hT, rh_fn, ntok=S):
        if not hasattr(self, "nsq"):
            raise RuntimeError("norm temps not allocated")
        sq, r_sq = self.nsq
        rrs = [self.nrr, self.nrr]
        tg_sz = min(512, ntok)
        for tg in range(ntok // tg_sz):
            sl = slice(tg * tg_sz, (tg + 1) * tg_sz)
            rr, r_rr = rrs[tg % 2]
            sqv = sq[:, :, 0:tg_sz]
            self.op("act", lambda e, sl=sl, sqv=sqv: e.activation(sqv, xT[:, :, sl], AF.Square), reads=rx_fn(None, tg), writes=[r_sq])
            b, br = self.bank()
            for k in range(8):
                self.mm(b[:, 0:tg_sz], br, self.onesb, sq[:, k, 0:tg_sz], k == 0, k == 7, [r_sq, self.r_onesb])
            rv = rr[:, 0:tg_sz]
            self.rsqrt_ps(b[:, 0:tg_sz], float(1024 * EPS), rv, r_rr, br)
            for k in range(8):
                g = self.gcol(gbase, l, k)
                self.op("dve", lambda e, k=k, sl=sl, g=g, rv=rv: e.scalar_tensor_tensor(hT[:, k, sl], xT[:, k, sl], g, rv, op0=ALU.mult, op1=ALU.mult),
                        reads=rx_fn(k, tg) + [r_rr, self.r_g32], writes=rh_fn(tg))

    def final_store(self, xT, rx_fn, out_ap):
        m = self.mark()
        sq, r_sq = self.alloc([8, 512], BF16, "fsq")
        rr, r_rr = self.alloc([512], F32, "frr")
        yT, r_y = self.alloc([8, 512], F32, "fy")
        stg = [self.alloc([1024], F32, f"fstg{i}") for i in range(5)]
        outs = []
        n = 0
        for tg in range(NTG):
            sl = slice(tg * 512, (tg + 1) * 512)
            self.op("act", lambda e, sl=sl: e.activation(sq, xT[:, :, sl], AF.Square), reads=rx_fn(None, tg), writes=[r_sq])
            b, br = self.bank()
            for k in range(8):
                self.mm(b, br, self.onesb, sq[:, k, :], k == 0, k == 7, [r_sq, self.r_onesb])
            self.rsqrt_ps(b, float(1024 * EPS), rr, r_rr, br)
            for k in range(8):
                g = self.g32[:, 56 + k:57 + k]
                self.op("dve", lambda e, k=k, sl=sl, g=g: e.scalar_tensor_tensor(yT[:, k, :], xT[:, k, sl], g, rr, op0=ALU.mult, op1=ALU.mult),
                        reads=rx_fn(k, tg) + [r_rr, self.r_g32], writes=[r_y])
            for tt in range(4):
                sv, sr = stg[n % 5]
                n += 1
                for half in range(2):
                    b2, br2 = self.bank()
                    for j in range(4):
                        k = half * 4 + j
                        self.op("pe", lambda e, b2=b2, j=j, k=k, tt=tt: e.transpose(b2[:, j * 128:(j + 1) * 128], yT[:, k, tt * 128:(tt + 1) * 128], self.identf),
                                reads=[r_y, self.r_identf], writes=[br2])
                    if half == 0:
                        self.op("dve", lambda e, sv=sv, b2=b2: e.tensor_copy(sv[:, 0:512], b2), reads=[br2], writes=[sr])
                    else:
                        self.op("act", lambda e, sv=sv, b2=b2: e.activation(sv[:, 512:1024], b2, AF.Copy), reads=[br2], writes=[sr])
                t0 = tg * 512 + tt * 128
                o = self.P.dma("sp", lambda e, sv=sv, t0=t0: e.dma_start(out=out_ap[t0:t0 + 128, :], in_=sv), reads=[sr])
                outs.append(("d", "sp", o.dma_slot, o.dma_cnt))
        self.barrier()
        self.release(m)
        return outs

    def ffn(self, xT, rx_fn, hT, rh_fn, w_in, w_out):
        m = self.mark()
        self.init_wring(7)
        hid = [self.alloc([4, 512], BF16, f"hid{i}") for i in range(3)]
        sa = [self.alloc([512], F32, f"sa{i}") for i in range(4)]
        w_in_v = w_in.rearrange("(k p) n -> p k n", p=128)
        w_out_v = w_out.rearrange("(j p) n -> p j n", p=128)
        nchunks = FFH // 128
        step = 0
        for c0 in range(0, nchunks, 4):
            nj = min(4, nchunks - c0)
            wa, r_wa = self.wload(w_in_v[:, :, c0 * 128:(c0 + nj) * 128], [8, nj * 128])
            wg, r_wg = self.wload(w_in_v[:, :, FFH + c0 * 128:FFH + (c0 + nj) * 128], [8, nj * 128])
            wo, r_wo = self.wload(w_out_v[:, c0:c0 + nj, :], [nj, 1024])
            for tg in range(NTG):
                sl = slice(tg * 512, (tg + 1) * 512)
                hv, r_hv = hid[step % 3]
                step += 1
                for j in range(nj):
                    pa, r_pa = self.bank()
                    for k in range(8):
                        self.mm(pa, r_pa, wa[:, k, j * 128:(j + 1) * 128], hT[:, k, sl], k == 0, k == 7, [r_wa] + rh_fn(tg))
                    pg, r_pg = self.bank()
                    for k in range(8):
                        self.mm(pg, r_pg, wg[:, k, j * 128:(j + 1) * 128], hT[:, k, sl], k == 0, k == 7, [r_wg] + rh_fn(tg))
                    sv, r_sv = sa[j % 4]
                    self.op("act", lambda e, sv=sv, pa=pa: e.activation(sv, pa, AF.Silu), reads=[r_pa], writes=[r_sv])
                    self.op("dve", lambda e, hv=hv, j=j, sv=sv, pg=pg: e.tensor_tensor(hv[:, j, :], sv, pg, op=ALU.mult),
                            reads=[r_sv, r_pg], writes=[r_hv])
                for oc in range(8):
                    po, r_po = self.bank()
                    for j in range(nj):
                        self.mm(po, r_po, wo[:, j, oc * 128:(oc + 1) * 128], hv[:, j, :], j == 0, j == nj - 1, [r_wo, r_hv])
                    self.op("dve", lambda e, oc=oc, sl=sl, po=po: e.tensor_tensor(xT[:, oc, sl], xT[:, oc, sl], po, op=ALU.add),
                            reads=[r_po] + rx_fn(oc, tg), writes=rx_fn(oc, tg))
        self.barrier()
        self.release(m)

    def evac(self, dst, src, reads, writes):
        self._ev = getattr(self, "_ev", 0) + 1
        if self._ev % 2 == 0:
            self.op("dve", lambda e: e.tensor_copy(dst, src), reads=reads, writes=writes)
        else:
            self.op("act", lambda e: e.activation(dst, src, AF.Copy), reads=reads, writes=writes)

    def init_xa(self):
        self.memnT, self.r_memn = self.alloc([8, MEM], BF16, "memnT")

    def prep_mem(self, mem_src):
        m = self.mark()
        memT, r_memT = self.alloc([8, MEM], F32, "memT")
        self.load_T(mem_src, MEM, memT, lambda i, half: [r_memT])
        self.norm_to(memT, lambda k, tg: [r_memT], 48, 0, self.memnT, lambda tg: [self.r_memn], ntok=MEM)
        self.barrier()
        self.release(m)

    def xattn(self, d, l):
        m = self.mark()
        self.init_wring(3)
        hT = self.hT
        qT, _ = self.alloc([8, S], BF16, "qT")
        r_q = [Res(f"q{t}") for t in range(NTG)]
        self.kT, self.r_kT = self.alloc([8, MEM], BF16, "kT")
        self.vtok, self.r_vtok = self.alloc([2, D], BF16, "vtok")
        wkv_v = d["xa_wkv"][l].rearrange("(k p) n -> p k n", p=128)
        wq_v = d["xa_wq"][l].rearrange("(k p) n -> p k n", p=128)
        wo_v = d["xa_wo"][l].rearrange("(k p) n -> p k n", p=128)
        for blk in range(2):
            w, rw = self.wload(wkv_v[:, :, blk * 512:(blk + 1) * 512], [8, 512])
            for j in range(4):
                c = blk * 4 + j
                b, br = self.bank()
                for k in range(8):
                    self.mm(b[:, 0:MEM], br, w[:, k, j * 128:(j + 1) * 128], self.memnT[:, k, :], k == 0, k == 7, [rw, self.r_memn])
                self.evac(self.kT[:, c, :], b[:, 0:MEM], [br], [self.r_kT])
        for blk in range(2):
            w, rw = self.wload(wkv_v[:, :, D + blk * 512:D + (blk + 1) * 512], [8, 512])
            for mc in range(2):
                b, br = self.bank()
                for k in range(8):
                    self.mm(b, br, self.memnT[:, k, mc * 128:(mc + 1) * 128], w[:, k, :], k == 0, k == 7, [rw, self.r_memn])
                self.evac(self.vtok[:, mc, blk * 512:(blk + 1) * 512], b, [br], [self.r_vtok])
        for blk in range(2):
            w, rw = self.wload(wq_v[:, :, blk * 512:(blk + 1) * 512], [8, 512])
            for tg in range(NTG):
                sl = slice(tg * 512, (tg + 1) * 512)
                for j in range(4):
                    c = blk * 4 + j
                    b, br = self.bank()
                    for k in range(8):
                        self.mm(b, br, w[:, k, j * 128:(j + 1) * 128], hT[:, k, sl], k == 0, k == 7, [rw] + self.rh_fn(tg))
                    self.evac(qT[:, c, sl], b, [br], [r_q[tg]])
        NPP = 3
        pT = [[self.alloc([512], BF16, f"pT{i}{j}") for j in range(2)] for i in range(NPP)]
        rec = [self.alloc([512], F32, f"rec{i}") for i in range(2)]
        items = [(tg, h) for tg in range(NTG) for h in range(4)]

        def scores(i):
            tg, h = items[i]
            sl = slice(tg * 512, (tg + 1) * 512)
            pp = pT[i % NPP]
            for mc in range(2):
                b, br = self.bank()
                for dc in range(2):
                    self.mm(b, br, self.kT[:, 2 * h + dc, mc * 128:(mc + 1) * 128], qT[:, 2 * h + dc, sl], dc == 0, dc == 1, [self.r_kT, r_q[tg]])
                pv_, r_pv_ = pp[mc]
                self.op("act", lambda e: e.activation(pv_, b, AF.Exp, scale=1.0 / 16.0), reads=[br], writes=[r_pv_])

        scores(0)
        for i, (tg, h) in enumerate(items):
            sl = slice(tg * 512, (tg + 1) * 512)
            if i + 1 < len(items):
                scores(i + 1)
            pp = pT[i % NPP]
            rc, r_rc = rec[i % 2]
            bd, brd = self.bank()
            for mc in range(2):
                self.mm(bd, brd, self.onesb, pp[mc][0], mc == 0, mc == 1, [self.r_onesb, pp[mc][1]])
            self.op("act", lambda e: e.activation(rc, bd, AF.Ln), reads=[brd], writes=[r_rc])
            self.op("act", lambda e: e.activation(rc, rc, AF.Exp, scale=-1.0), reads=[r_rc], writes=[r_rc])
            for dc in range(2):
                bo, bro = self.bank()
                for mc in range(2):
                    self.mm(bo, bro, self.vtok[:, mc, (2 * h + dc) * 128:(2 * h + dc + 1) * 128], pp[mc][0], mc == 0, mc == 1, [self.r_vtok, pp[mc][1]])
                self.op("dve", lambda e: e.tensor_tensor(hT[:, 2 * h + dc, sl], bo, rc, op=ALU.mult),
                        reads=[bro, r_rc], writes=self.rh_fn(tg))
        for blk in range(2):
            w, rw = self.wload(wo_v[:, :, blk * 512:(blk + 1) * 512], [8, 512])
            for tg in range(NTG):
                sl = slice(tg * 512, (tg + 1) * 512)
                for j in range(4):
                    c = blk * 4 + j
                    b, br = self.bank()
                    for k in range(8):
                        self.mm(b, br, w[:, k, j * 128:(j + 1) * 128], hT[:, k, sl], k == 0, k == 7, [rw] + self.rh_fn(tg))
                    self.op("dve", lambda e, c=c, sl=sl, b=b: e.tensor_tensor(self.xT[:, c, sl], self.xT[:, c, sl], b, op=ALU.add),
                            reads=[br] + self.rx_fn(c, tg), writes=self.rx_fn(c, tg))
        self.barrier()
        self.release(m)

    def setup_layer_consts(self, d):
        idf = self.identf
        trif, r_trif = self.alloc([128], F32, "trif")
        self.trib, self.r_trib = self.alloc([128], BF16, "trib")
        self.op("pool", lambda e: e.memset(trif, 1.0), writes=[r_trif])
        self.op("pool", lambda e: e.affine_select(trif, trif, pattern=[[1, 128]], compare_op=ALU.is_ge, fill=0.0, base=0, channel_multiplier=-1),
                reads=[r_trif], writes=[r_trif])
        self.op("dve", lambda e: e.tensor_copy(self.trib, trif), reads=[r_trif], writes=[self.r_trib])
        self.bdb, self.r_bdb = self.alloc([128], BF16, "bdb")
        self.op("pool", lambda e: e.memset(trif[0:64, 64:128], 0.0), reads=[r_trif], writes=[r_trif])
        self.op("dve", lambda e: e.tensor_copy(self.bdb, trif), reads=[r_trif], writes=[self.r_bdb])
        self.hmask, self.r_hmask = self.alloc([2], F32, "hmask")
        self.op("pool", lambda e: e.memset(self.hmask, 0.0), writes=[self.r_hmask])
        self.op("pool", lambda e: e.memset(self.hmask[0:64, 0:1], 1.0), reads=[self.r_hmask], writes=[self.r_hmask])
        self.op("pool", lambda e: e.memset(self.hmask[64:128, 1:2], 1.0), reads=[self.r_hmask], writes=[self.r_hmask])
        self.vb, self.r_vb = self.alloc([8, 8], F32, "vb")
        self.op("pool", lambda e: e.memset(self.vb, 0.0), writes=[self.r_vb])
        for jq in range(8):
            self.op("pool", lambda e, jq=jq: e.memset(self.vb[:, jq, jq:8], -1.0e30), reads=[self.r_vb], writes=[self.r_vb])
        self.vball, self.r_vball = self.alloc([256], F32, "vball")
        for qt in range(16):
            for hh in range(2):
                o_ = (qt * 2 + hh) * 8
                self.op("pool", lambda e: e.tensor_copy(self.vball[:, o_:o_ + 8], self.vb[:, qt // 2, :]), reads=[self.r_vb], writes=[self.r_vball])
        self.oh40, self.r_oh40 = self.alloc([8, 128], BF16, "oh40")
        self.op("pool", lambda e: e.memset(self.oh40, 0.0), writes=[self.r_oh40])
        self.op("dve", lambda e: e.tensor_copy(self.oh40[0:8], self.identb[0:8, 0:8].unsqueeze(2).to_broadcast([8, 8, 128])),
                reads=[self.r_identb, self.r_oh40], writes=[self.r_oh40])
        self.op("dve", lambda e: e.tensor_copy(self.oh40[32:40], self.identb[32:40, 32:40].unsqueeze(2).to_broadcast([8, 8, 128])),
                reads=[self.r_identb, self.r_oh40], writes=[self.r_oh40])
        self.m40, self.r_m40 = self.alloc([2], F32, "m40")
        self.op("pool", lambda e: e.memset(self.m40, 0.0), writes=[self.r_m40])
        self.op("pool", lambda e: e.memset(self.m40[0:32, 0:1], 1.0), reads=[self.r_m40], writes=[self.r_m40])
        self.op("pool", lambda e: e.memset(self.m40[32:64, 1:2], 1.0), reads=[self.r_m40], writes=[self.r_m40])
        self.sguw, self.r_sguw = self.alloc([4, 128], BF16, "sguw")
        self.brep, self.r_brep = self.alloc([4, 128], F32, "brep")
        self.P.dma("sp", lambda e: e.dma_start(out=self.brep, in_=d["sgu_b"][0].partition_broadcast(128)), writes=[self.r_brep])
        self.gs, self.r_gs = self.alloc([4], F32, "gs")
        self.op("dve", lambda e: e.tensor_scalar(self.gs, self.pv[:, 92:96], float(np.sqrt(128.0)), None, op0=ALU.mult), reads=[self.r_pv], writes=[self.r_gs])
        m = self.mark()
        wtmp, r_wtmp = self.alloc([4, 128], F32, "sgutmp")
        self.P.dma("sp", lambda e: e.dma_start(out=wtmp, in_=d["sgu_w"][0].rearrange("g t s -> t g s")), writes=[r_wtmp])
        for g in range(4):
            self.op("pool", lambda e, g=g: e.affine_select(wtmp[:, g, :], wtmp[:, g, :], pattern=[[-1, 128]], compare_op=ALU.is_ge, fill=0.0,
                                                            base=0, channel_multiplier=1), reads=[r_wtmp], writes=[r_wtmp])
        b, br = self.bank()
        for g in range(4):
            self.op("pe", lambda e, g=g: e.transpose(b[:, g * 128:(g + 1) * 128], wtmp[:, g, :], idf), reads=[r_wtmp, self.r_identf], writes=[br])
        self.op("dve", lambda e: e.tensor_copy(self.sguw, b.rearrange("p (g t) -> p g t", g=4)), reads=[br], writes=[self.r_sguw])
        self.barrier()
        self.release(m)

    def out_proj(self, oB, r_oB, w_out):
        wv = w_out.rearrange("(k p) n -> p k n", p=128)
        for blk in range(2):
            w, rw = self.wload(wv[:, :, blk * 512:(blk + 1) * 512], [8, 512])
            for tg in range(NTG):
                sl = slice(tg * 512, (tg + 1) * 512)
                for j in range(4):
                    c = blk * 4 + j
                    b, br = self.bank()
                    for k in range(8):
                        self.mm(b, br, w[:, k, j * 128:(j + 1) * 128], oB[:, k, sl], k == 0, k == 7, [rw, r_oB[tg]])
                    self.op("dve", lambda e, c=c, sl=sl, b=b: e.tensor_tensor(self.xT[:, c, sl], self.xT[:, c, sl], b, op=ALU.add),
                            reads=[br] + self.rx_fn(c, tg), writes=self.rx_fn(c, tg))

    def mixer1(self, d):
        m0 = self.mark()
        self.init_wring(2)
        hT = self.hT
        w_in = d["w_in_cd"][0]
        oB, _ = self.alloc([8, S], BF16, "oB")
        r_oB = [Res(f"oB{t}") for t in range(NTG)]
        m1 = self.mark()
        qz, r_qz = self.alloc([2, S], BF16, "qz")
        kc, r_kc = self.alloc([S], BF16, "kc")
        vt, r_vt = self.alloc([16, 130], BF16, "vt")
        kmean, r_kmean = self.alloc([8], BF16, "kmean")
        kmf, r_kmf = self.alloc([8], F32, "kmf")
        ms, r_ms = self.alloc([32, 8], F32, "ms")
        ms1, r_ms1 = self.alloc([32, 8], F32, "ms1")
        eq, r_eq = self.alloc([32, 8], F32, "eq")
        rmax, r_rmax = self.alloc([32], F32, "rmax")
        bqa, r_bqa = self.alloc([16, 40], BF16, "bqa")
        biasT, r_biasT = self.alloc([2, S], BF16, "biasTT")
        NPT = 5
        pT = [self.alloc([2, 256], BF16, f"mpT{i}") for i in range(NPT)]
        otok = [self.alloc([128], BF16, f"otok{i}") for i in range(2)]
        rcs = [self.alloc([2], F32, f"mrc{i}") for i in range(2)]
        self.op("pool", lambda e: e.memset(vt, 1.0), writes=[r_vt])
        self.op("pool", lambda e: e.memset(bqa, 0.0), writes=[r_bqa])
        wvv = w_in.rearrange("(k p) n -> p k n", p=128)
        pstep = 0
        for c in range(0 if "nomoba" in self.stages else 4):
            w, rw = self.wload([wvv[:, :, s_ * 512 + c * 128:s_ * 512 + (c + 1) * 128] for s_ in range(3)], [8, 3, 128])
            for tg in range(NTG):
                sl = slice(tg * 512, (tg + 1) * 512)
                b, br = self.bank()
                for k in range(8):
                    self.mm(b, br, w[:, k, 0, :], hT[:, k, sl], k == 0, k == 7, [rw] + self.rh_fn(tg))
                for hh in range(2):
                    self.op("act", lambda e: e.activation(qz[:, hh, sl], b, AF.Copy, scale=self.hmask[:, hh:hh + 1]), reads=[br, self.r_hmask], writes=[r_qz])
            for tg in range(NTG):
                sl = slice(tg * 512, (tg + 1) * 512)
                b, br = self.bank()
                for k in range(8):
                    self.mm(b, br, w[:, k, 1, :], hT[:, k, sl], k == 0, k == 7, [rw] + self.rh_fn(tg))
                self.evac(kc[:, sl], b, [br], [r_kc])
            for g4 in range(4):
                b, br = self.bank()
                for tt in range(4):
                    t0 = (g4 * 4 + tt) * 128
                    for k in range(8):
                        self.mm(b[:, tt * 128:(tt + 1) * 128], br, hT[:, k, t0:t0 + 128], w[:, k, 2, :], k == 0, k == 7, [rw] + self.rh_fn(g4))
                dstv = vt[:, g4 * 4:(g4 + 1) * 4, :].rearrange("p t (h e) -> p t h e", h=2)[:, :, :, 0:64]
                srcv = b.rearrange("p (t h e) -> p t h e", t=4, h=2)
                self.evac(dstv, srcv, [br], [r_vt])
            self.op("dve", lambda e: e.tensor_reduce(out=kmf, in_=kc.rearrange("p (n t) -> p n t", n=8), axis=AX.X, op=ALU.add),
                    reads=[r_kc], writes=[r_kmf])
            self.op("dve", lambda e: e.tensor_scalar(kmean, kmf, 1.0 / 256.0, None, op0=ALU.mult), reads=[r_kmf], writes=[r_kmean])
            bs, brs = self.bank()
            for qt in range(16):
                for hh in range(2):
                    self.mm(bs[:, (qt * 2 + hh) * 8:(qt * 2 + hh + 1) * 8], brs, qz[:, hh, qt * 128:(qt + 1) * 128], kmean, True, True, [r_qz, r_kmean])
            msv = ms.rearrange("p a n -> p (a n)")
            self.op("dve", lambda e: e.tensor_tensor(msv, bs[:, 0:256], self.vball, op=ALU.add), reads=[brs, self.r_vball], writes=[r_ms])
            src, r_src = ms, r_ms
            for rnd in range(2):
                self.op("dve", lambda e: e.tensor_reduce(out=rmax, in_=src, axis=AX.X, op=ALU.max), reads=[r_src], writes=[r_rmax])
                self.op("dve", lambda e: e.tensor_tensor(eq, src, rmax.unsqueeze(2).to_broadcast([128, 32, 8]), op=ALU.is_ge),
                        reads=[r_src, r_rmax], writes=[r_eq])
                self.op("dve", lambda e: e.scalar_tensor_tensor(ms1, eq, -3.0e30, src, op0=ALU.mult, op1=ALU.add),
                        reads=[r_eq, r_src], writes=[r_ms1])
                src, r_src = ms1, r_ms1
            self.op("dve", lambda e: e.tensor_reduce(out=rmax, in_=ms1, axis=AX.X, op=ALU.max), reads=[r_ms1], writes=[r_rmax])
            self.op("dve", lambda e: e.tensor_tensor(eq, ms, rmax.unsqueeze(2).to_broadcast([128, 32, 8]), op=ALU.is_ge),
                    reads=[r_ms, r_rmax], writes=[r_eq])
            eq4 = eq.rearrange("p (t h) n -> p t h n", h=2)
            for hh in range(2):
                self.op("dve", lambda e: e.tensor_scalar(bqa[:, :, hh * 32:hh * 32 + 8], eq4[:, :, hh, :], -1.0, 30000.0, op0=ALU.add, op1=ALU.mult),
                        reads=[r_eq], writes=[r_bqa])
            for tg in range(NTG):
                bt, brt = self.bank()
                for tt in range(4):
                    qt = tg * 4 + tt
                    self.mm(bt[0:40, tt * 128:(tt + 1) * 128], brt, bqa[:, qt, :], self.identb, True, True, [r_bqa, self.r_identb])
                for hh in range(2):
                    self.op("act", lambda e: e.activation(biasT[0:40, hh, tg * 512:(tg + 1) * 512], bt[0:40, :], AF.Copy, scale=self.m40[0:40, hh:hh + 1]),
                            reads=[brt, self.r_m40], writes=[r_biasT])
            for jq in range(0 if "noattn" in self.stages else 8):
                q0 = jq * 256
                pos = [self.acc_bank(), self.acc_bank()]
                nchunks = 2 * jq + 2
                def score_stage(ci):
                    pt, r_pt = pT[(pbase + ci) % NPT]
                    k0 = ci * 128
                    bsc, brsc = self.bank()
                    own = ci >= 2 * jq
                    isB = ci == 2 * jq + 1
                    if not own:
                        n = ci // 2
                        nobias = jq <= 3
                        self.mm(bsc, brsc, kc[:, k0:k0 + 128], qz[:, :, q0:q0 + 256], True, nobias, [r_kc, r_qz])
                        if not nobias:
                            self.mm(bsc, brsc, self.oh40[0:40, n, :], biasT[0:40, :, q0:q0 + 256], False, True, [self.r_oh40, r_biasT])
                        self.op("act", lambda e: e.activation(pt, bsc.rearrange("p (h q) -> p h q", h=2), AF.Exp, scale=0.125), reads=[brsc], writes=[r_pt])
                    elif not isB:
                        self.mm(bsc, brsc, kc[:, k0:k0 + 128], qz[:, :, q0:q0 + 256], True, True, [r_kc, r_qz])
                        self.op("act", lambda e: e.activation(pt, bsc.rearrange("p (h q) -> p h q", h=2), AF.Exp, scale=0.125), reads=[brsc], writes=[r_pt])
                        self.op("dve", lambda e: e.tensor_tensor(pt[:, :, 0:128], pt[:, :, 0:128], self.trib.unsqueeze(1).to_broadcast([128, 2, 128]), op=ALU.mult),
                                reads=[r_pt, self.r_trib], writes=[r_pt])
                    else:
                        self.mm(bsc[:, 0:256], brsc, kc[:, k0:k0 + 128], qz[:, :, q0 + 128:q0 + 256], True, True, [r_kc, r_qz])
                        self.op("act", lambda e: e.activation(pt[:, :, 128:256], bsc[:, 0:256].rearrange("p (h q) -> p h q", h=2), AF.Exp, scale=0.125),
                                reads=[brsc], writes=[r_pt])
                        self.op("dve", lambda e: e.tensor_tensor(pt[:, :, 128:256], pt[:, :, 128:256], self.trib.unsqueeze(1).to_broadcast([128, 2, 128]), op=ALU.mult),
                                reads=[r_pt, self.r_trib], writes=[r_pt])

                pbase = pstep
                LOOK = 2
                for ci in range(min(LOOK, nchunks)):
                    score_stage(ci)
                for ci in range(nchunks):
                    if ci + LOOK < nchunks:
                        score_stage(ci + LOOK)
                    pt, r_pt = pT[(pbase + ci) % NPT]
                    isB = ci == 2 * jq + 1
                    for hh in range(2):
                        po, rpo = pos[hh]
                        vv = vt[:, ci, hh * 65:(hh + 1) * 65]
                        if not isB:
                            self.mm(po[:, 0:65], rpo, pt[:, hh, 0:128], vv, ci == 0, ci == 2 * jq, [r_pt, r_vt], skip=True)
                        self.mm(po[:, 128:193], rpo, pt[:, hh, 128:256], vv, False, ci == nchunks - 1, [r_pt, r_vt], skip=True)
                pstep += nchunks
                for hh in range(2):
                    po, rpo = pos[hh]
                    hs = slice(hh * 64, (hh + 1) * 64)
                    rc, r_rc = rcs[hh]
                    pov = po.rearrange("p (t e) -> p t e", t=4)
                    self.op("dve", lambda e: e.reciprocal(rc, pov[:, 0:2, 64]), reads=[rpo], writes=[r_rc])
                    for t2 in range(2):
                        ot, r_ot = otok[t2]
                        self.op("dve", lambda e: e.tensor_scalar(ot[:, hs], po[:, t2 * 128:t2 * 128 + 64], rc[:, t2:t2 + 1], None, op0=ALU.mult),
                                reads=[rpo, r_rc], writes=[r_ot])
                bt2, brt2 = self.bank()
                btb2 = bt2.bitcast(BF16)
                for t2 in range(2):
                    ot, r_ot = otok[t2]
                    self.op("pe", lambda e: e.transpose(btb2[:, t2 * 128:(t2 + 1) * 128], ot, self.identb), reads=[r_ot, self.r_identb], writes=[brt2])
                self.evac(oB[:, c, q0:q0 + 256], btb2[:, 0:256], [brt2], [r_oB[jq // 2]])
        self.barrier()
        self.release(m1)
        def sgu_bufs(tag):
            B = {}
            B["uf"] = [self.alloc([512], F32, f"uf{tag}{i}") for i in range(2)]
            for nm, dt_ in (("zf", F32), ("zb", BF16), ("zc", F32), ("zq", BF16), ("rr", F32), ("zn", BF16), ("tmp", F32)):
                B[nm] = self.alloc([512], dt_, f"{nm}{tag}")
            B["znt"] = self.alloc([4, 128], BF16, f"znt{tag}")
            return B

        def sgu_gen(g, B):
            zf, r_zf = B["zf"]
            zb, r_zb = B["zb"]
            zc, r_zc = B["zc"]
            zq, r_zq = B["zq"]
            rr, r_rr = B["rr"]
            zn, r_zn = B["zn"]
            tmp, r_tmp = B["tmp"]
            znt, r_znt = B["znt"]
            w, rw = self.wload([wvv[:, :, 1536 + s_ * 512 + g * 128:1536 + s_ * 512 + (g + 1) * 128] for s_ in range(2)], [8, 2, 128])
            for tg in range(NTG):
                sl = slice(tg * 512, (tg + 1) * 512)
                uv, r_uv = B["uf"][tg % 2]
                pu, rpu = self.bank()
                for k in range(8):
                    self.mm(pu, rpu, w[:, k, 0, :], hT[:, k, sl], k == 0, k == 7, [rw] + self.rh_fn(tg))
                self.op("act", lambda e: e.activation(uv, pu, AF.Gelu), reads=[rpu], writes=[r_uv])
                pz, rpz = self.bank()
                for k in range(8):
                    self.mm(pz, rpz, w[:, k, 1, :], hT[:, k, sl], k == 0, k == 7, [rw] + self.rh_fn(tg))
                self.op("act", lambda e: e.activation(zf, pz, AF.Gelu), reads=[rpz], writes=[r_zf])
                yield
                self.op("dve", lambda e: e.tensor_copy(zb, zf), reads=[r_zf], writes=[r_zb])
                p1, rp1 = self.bank()
                self.mm(p1, rp1, self.onesb, zb, True, True, [self.r_onesb, r_zb])
                self.op("dve", lambda e: e.scalar_tensor_tensor(zc, p1, -1.0 / 128.0, zf, op0=ALU.mult, op1=ALU.add), reads=[rp1, r_zf], writes=[r_zc])
                yield
                self.op("act", lambda e: e.activation(zq, zc, AF.Square), reads=[r_zc], writes=[r_zq])
                p2, rp2 = self.bank()
                self.mm(p2, rp2, self.onesb, zq, True, True, [self.r_onesb, r_zq])
                self.rsqrt_ps(p2, float(128 * EPS), rr, r_rr, rp2)
                yield
                self.op("dve", lambda e: e.tensor_tensor(zc, zc, rr, op=ALU.mult), reads=[r_zc, r_rr], writes=[r_zc])
                self.op("act", lambda e: e.activation(zn, zc, AF.Identity, bias=self.pv[:, 96 + g:97 + g], scale=self.gs[:, g:g + 1]),
                        reads=[r_zc, self.r_pv, self.r_gs], writes=[r_zn])
                pt_, rpt_ = self.bank()
                ptb = pt_.bitcast(BF16)
                for tt in range(4):
                    self.op("pe", lambda e: e.transpose(ptb[:, tt * 128:(tt + 1) * 128], zn[:, tt * 128:(tt + 1) * 128], self.identb),
                            reads=[r_zn, self.r_identb], writes=[rpt_])
                self.evac(znt, ptb[:, 0:512].rearrange("p (t c) -> p t c", t=4), [rpt_], [r_znt])
                yield
                pm, rpm = self.bank()
                for tt in range(4):
                    self.mm(pm[:, tt * 128:(tt + 1) * 128], rpm, znt[:, tt, :], self.sguw[:, g, :], True, True, [r_znt, self.r_sguw])
                self.op("dve", lambda e: e.tensor_tensor(tmp.rearrange("p (t c) -> p t c", t=4), pm.rearrange("p (t c) -> p t c", t=4),
                                                          self.brep[:, g:g + 1, :].to_broadcast([128, 4, 128]), op=ALU.add),
                        reads=[rpm, self.r_brep], writes=[r_tmp])
                self.op("dve", lambda e: e.tensor_tensor(oB[:, 4 + g, sl], tmp, uv, op=ALU.mult), reads=[r_tmp, r_uv], writes=[r_oB[tg]])
                yield

        if "nosgu" not in self.stages:
            bufsets = [sgu_bufs("a"), sgu_bufs("b")]
            for gp in range(2):
                gens = [sgu_gen(2 * gp + i, bufsets[i]) for i in range(2)]
                alive = list(gens)
                lag = 2
                step = 0
                while alive:
                    for gi, gen in enumerate(gens):
                        if gen not in alive:
                            continue
                        if gi == 1 and step < lag:
                            continue
                        try:
                            next(gen)
                        except StopIteration:
                            alive.remove(gen)
                    step += 1
        if "dbg" in self.stages:
            o = self.P.dma("sp", lambda e: e.dma_start(out=d["dbg"], in_=oB), reads=r_oB)
            self.dbg_tok = ("d", "sp", o.dma_slot, o.dma_cnt)
        self.out_proj(oB, r_oB, d["w_out_cd"][0])
        self.barrier()
        self.release(m0)

    def mixer0(self, d):
        import os
        KA_T = int(os.environ.get("KA_T", "6"))
        KA_I = int(os.environ.get("KA_I", "10"))
        KA_N = int(os.environ.get("KA_N", "4"))
        m0 = self.mark()
        self.init_wring(2)
        hT = self.hT
        wvv = d["w_in_ab"][0].rearrange("(k p) n -> p k n", p=128)
        oB, _ = self.alloc([8, S], BF16, "oB0")
        r_oB = [Res(f"oB0_{t}") for t in range(NTG)]
        hp = self.hp
        m1 = self.mark()
        F = lambda n: self.alloc([512], F32, n)
        qs, r_qs = F("qs")
        ff, r_ff = F("ff")
        gate, r_gate = F("gate")
        bcum, r_bcum = F("bcum")
        rb, r_rb = F("rb")
        omf, r_omf = F("omf")
        kinf, r_kinf = F("kinf")
        rr, r_rr = F("hrr")
        otmp, r_otmp = F("otmp")
        zeros, r_zeros = self.alloc([64], F32, "zeros")
        vT, r_vT = self.alloc([512], BF16, "vT")
        qin, r_qin = self.alloc([512], BF16, "qin")
        kin, r_kin = self.alloc([512], BF16, "kin")
        kdT, r_kdT = self.alloc([512], BF16, "kdT")
        sqb, r_sqb = self.alloc([512], BF16, "hsq")
        vtok, r_vtok = self.alloc([4, 128], BF16, "hvtok")
        kdtok, r_kdtok = self.alloc([4, 128], BF16, "hkdtok")
        kdtok2, _ = self.alloc([4, 128], BF16, "hkdtok2")
        attm, r_attm = self.alloc([4, 128], BF16, "attm")
        Sall, r_Sall = self.alloc([9, 128], F32, "Sall")
        Sball, r_Sball = self.alloc([8, 128], BF16, "Sball")
        self.op("pool", lambda e: e.memset(zeros, 0.0), writes=[r_zeros])
        for h in range(0 if "nohgrn" in self.stages else 4):
            w, rw = self.wload([wvv[:, :, s_ * 512 + h * 128:s_ * 512 + (h + 1) * 128] for s_ in range(4)], [8, 4, 128])
            self.op("pool", lambda e: e.memset(Sall[:, 0, :], 0.0), writes=[r_Sall])
            for tg in range(NTG):
                sl = slice(tg * 512, (tg + 1) * 512)
                pb = []
                for s_ in range(4):
                    b, br = self.bank()
                    for k in range(8):
                        self.mm(b, br, w[:, k, s_, :], hT[:, k, sl], k == 0, k == 7, [rw] + self.rh_fn(tg))
                    pb.append((b, br))
                (pq, rpq), (pf, rpf), (pi, rpi), (pg, rpg) = pb
                self.op("act", lambda e: e.activation(qs, pq, AF.Silu), reads=[rpq], writes=[r_qs])
                self.op("act", lambda e: e.activation(ff, pf, AF.Sigmoid), reads=[rpf], writes=[r_ff])
                self.op("act", lambda e: e.activation(gate, pg, AF.Silu), reads=[rpg], writes=[r_gate])
                self.op("dve", lambda e: e.tensor_copy(vT, pi), reads=[rpi], writes=[r_vT])
                self.op("dve", lambda e: e.tensor_scalar(ff, ff, hp[:, 4 + h:5 + h], hp[:, h:h + 1], op0=ALU.mult, op1=ALU.add),
                        reads=[r_ff, self.r_hp], writes=[r_ff])
                for c in range(8):
                    cs = slice(c * 64, (c + 1) * 64)
                    self.op("dve", lambda e: e.tensor_tensor_scan(bcum[:, cs], ff[:, cs], zeros, 1.0, op0=ALU.mult, op1=ALU.add),
                            reads=[r_ff, r_zeros], writes=[r_bcum])
                self.op("act", lambda e: e.activation(rb, bcum, AF.Ln), reads=[r_bcum], writes=[r_rb])
                self.op("act", lambda e: e.activation(rb, rb, AF.Exp, scale=-1.0), reads=[r_rb], writes=[r_rb])
                self.op("dve", lambda e: e.tensor_scalar(omf, ff, -1.0, 1.0, op0=ALU.mult, op1=ALU.add), reads=[r_ff], writes=[r_omf])
                self.op("dve", lambda e: e.tensor_tensor(qin, qs, bcum, op=ALU.mult), reads=[r_qs, r_bcum], writes=[r_qin])
                self.op("dve", lambda e: e.tensor_tensor(kinf, omf, rb, op=ALU.mult), reads=[r_omf, r_rb], writes=[r_kinf])
                self.op("act", lambda e: e.activation(kin, kinf, AF.Copy), reads=[r_kinf], writes=[r_kin])
                blast = bcum.rearrange("p (c t) -> p c t", c=8)[:, :, 63:64]
                self.op("dve", lambda e: e.tensor_tensor(kdT.rearrange("p (c t) -> p c t", c=8), kinf.rearrange("p (c t) -> p c t", c=8),
                                                          blast.to_broadcast([128, 8, 64]), op=ALU.mult), reads=[r_kinf, r_bcum], writes=[r_kdT])
                import os
                HC = int(os.environ.get("DBG_H", "9"))
                if HC < 2:
                    continue
                for src, r_src, dst, r_dst in ((vT, r_vT, vtok, r_vtok), (kdT, r_kdT, kdtok, r_kdtok)):
                    bt, brt = self.bank()
                    btb = bt.bitcast(BF16)
                    for tt in range(4):
                        o_ = self.op("pe", lambda e: e.transpose(btb[:, tt * 128:(tt + 1) * 128], src[:, tt * 128:(tt + 1) * 128], self.identb),
                                     reads=[r_src, self.r_identb], writes=[brt])
                        if tt == 0:
                            o_.ka = KA_T
                    bview = btb[:, 0:512].rearrange("p (t c) -> p t c", t=4)
                    if dst is vtok:
                        self.evac(dst, bview, [brt], [r_dst])
                    else:
                        self.op("act", lambda e: e.activation(kdtok, bview, AF.Copy, scale=self.hmask[:, 0:1]), reads=[brt, self.r_hmask], writes=[r_kdtok])
                        self.op("act", lambda e: e.activation(kdtok2, bview, AF.Copy, scale=self.hmask[:, 1:2]), reads=[brt, self.r_hmask], writes=[r_kdtok])
                if HC < 3:
                    continue
                po, rpo = self.acc_bank()
                ba, bra = self.bank()
                for tt in range(4):
                    ts_ = slice(tt * 128, (tt + 1) * 128)
                    self.mm(ba[:, ts_], bra, kin[:, ts_], qin[:, ts_], True, True, [r_kin, r_qin])
                self.op("dve", lambda e: e.tensor_tensor(attm, ba.rearrange("p (t c) -> p t c", t=4),
                                                          self.bdb.unsqueeze(1).to_broadcast([128, 4, 128]), op=ALU.mult),
                        reads=[bra, self.r_bdb], writes=[r_attm])
                if HC == 31:
                    continue
                bks = [self.bank(), self.bank()]
                for c in range(8):
                    tt, half = c // 2, c % 2
                    bk, brk = bks[c // 4]
                    ps_ = slice(half * 64, (half + 1) * 64)
                    self.mm(bk[:, (c % 4) * 128:(c % 4 + 1) * 128], brk, (kdtok if half == 0 else kdtok2)[:, tt, :], vtok[:, tt, :], True, True, [r_kdtok, r_vtok])
                if HC == 32:
                    continue
                for c in range(8):
                    bk, brk = bks[c // 4]
                    self.op("dve", lambda e: e.scalar_tensor_tensor(Sall[:, c + 1, :], Sall[:, c, :], bcum[:, c * 64 + 63:c * 64 + 64],
                                                                     bk[:, (c % 4) * 128:(c % 4 + 1) * 128], op0=ALU.mult, op1=ALU.add),
                            reads=[r_Sall, r_bcum, brk], writes=[r_Sall])
                self.op("act", lambda e: e.activation(Sball, Sall[:, 0:8, :], AF.Copy), reads=[r_Sall], writes=[r_Sball])
                if HC == 33:
                    continue
                for c in range(8):
                    cs = slice(c * 64, (c + 1) * 64)
                    self.mm(po[:, cs], rpo, Sball[:, c, :], qin[:, cs], (c == 0), False, [r_Sball, r_qin], skip=True, ka=(KA_I if c == 0 else 0))
                for tt in range(4):
                    ts_ = slice(tt * 128, (tt + 1) * 128)
                    self.mm(po[:, ts_], rpo, vtok[:, tt, :], attm[:, tt, :], False, (tt == 3), [r_vtok, r_attm], skip=True)
                self.op("dve", lambda e: e.tensor_copy(Sall[:, 0, :], Sall[:, 8, :]), reads=[r_Sall], writes=[r_Sall])
                if HC < 4:
                    continue
                self.op("act", lambda e: e.activation(sqb, po, AF.Square), reads=[rpo], writes=[r_sqb])
                bss, brss = self.bank()
                self.mm(bss, brss, self.onesb, sqb, True, True, [self.r_onesb, r_sqb], ka=KA_N)
                self.rsqrt_ps(bss, float(128 * EPS), rr, r_rr, brss)
                self.op("dve", lambda e: e.tensor_tensor(otmp, po, rr, op=ALU.mult), reads=[rpo, r_rr], writes=[r_otmp])
                self.op("dve", lambda e: e.scalar_tensor_tensor(oB[:, h, sl], otmp, hp[:, 8 + h:9 + h], gate, op0=ALU.mult, op1=ALU.mult),
                        reads=[r_otmp, r_gate, self.r_hp], writes=[r_oB[tg]])
        self.barrier()
        self.release(m1)
        cpad, r_cpad = self.alloc([4, 30 + S], BF16, "cpad")
        sg2 = [self.alloc([512], F32, f"sg2{i}") for i in range(2)]
        self.op("pool", lambda e: e.memset(cpad[:, :, 0:30], 0.0), writes=[r_cpad])
        for cc in range(4):
            w, rw = self.wload([wvv[:, :, 2048 + s_ * 512 + cc * 128:2048 + s_ * 512 + (cc + 1) * 128] for s_ in range(2)], [8, 2, 128])
            for tg in range(NTG):
                sl = slice(tg * 512, (tg + 1) * 512)
                pa, rpa = self.bank()
                for k in range(8):
                    self.mm(pa, rpa, w[:, k, 0, :], hT[:, k, sl], k == 0, k == 7, [rw] + self.rh_fn(tg))
                pb_, rpb = self.bank()
                for k in range(8):
                    self.mm(pb_, rpb, w[:, k, 1, :], hT[:, k, sl], k == 0, k == 7, [rw] + self.rh_fn(tg))
                sg, r_sg = sg2[tg % 2]
                self.op("act", lambda e: e.activation(sg, pb_, AF.Sigmoid), reads=[rpb], writes=[r_sg])
                self.op("dve", lambda e: e.tensor_tensor(cpad[:, cc, 30 + tg * 512:30 + (tg + 1) * 512], pa, sg, op=ALU.mult),
                        reads=[rpa, r_sg], writes=[r_cpad])
        yv = self.hT_raw.rearrange("p (a b) -> p a b", a=4)
        diag, r_diag = self.alloc([31, 128], BF16, "diag")
        wv = self.pv[:, 128:252].rearrange("p (j c) -> p j c", c=4)
        for cc in range(4):
            self.op("dve", lambda e: e.tensor_tensor(diag, self.identb.unsqueeze(1).to_broadcast([128, 31, 128]),
                                                      wv[:, :, cc].unsqueeze(2).to_broadcast([128, 31, 128]), op=ALU.mult),
                    reads=[self.r_identb, self.r_pv], writes=[r_diag])
            for tg in range(NTG):
                pc, rpc = self.bank()
                for j in range(31):
                    self.mm(pc, rpc, diag[:, j, :], cpad[:, cc, tg * 512 + j:tg * 512 + j + 512], j == 0, j == 30, [r_diag, r_cpad])
                self.op("act", lambda e: e.activation(yv[:, cc, tg * 512:(tg + 1) * 512], pc, AF.Identity, bias=self.pv[:, 80 + cc:81 + cc]),
                        reads=[rpc, self.r_pv], writes=self.rh_fn(tg))
        ybs = [self.alloc([512], BF16, f"yb{i}") for i in range(2)]
        yqs = [self.alloc([512], BF16, f"yq{i}") for i in range(2)]
        mean, r_mean = self.alloc([512], F32, "cmean")
        var, r_var = self.alloc([512], F32, "cvar")
        for tg in range(NTG):
            sl = slice(tg * 512, (tg + 1) * 512)
            p1, rp1 = self.bank()
            p2, rp2 = self.bank()
            for cc in range(4):
                yb, r_yb = ybs[cc % 2]
                yq, r_yq = yqs[cc % 2]
                self.op("dve", lambda e: e.tensor_copy(yb, yv[:, cc, sl]), reads=self.rh_fn(tg), writes=[r_yb])
                self.op("act", lambda e: e.activation(yq, yv[:, cc, sl], AF.Square), reads=self.rh_fn(tg), writes=[r_yq])
                self.mm(p1, rp1, self.onesb, yb, cc == 0, cc == 3, [self.r_onesb, r_yb])
                self.mm(p2, rp2, self.onesb, yq, cc == 0, cc == 3, [self.r_onesb, r_yq])
            self.op("act", lambda e: e.activation(mean, p1, AF.Copy, scale=1.0 / 512.0), reads=[rp1], writes=[r_mean])
            self.op("dve", lambda e: e.tensor_tensor(var, mean, mean, op=ALU.mult), reads=[r_mean], writes=[r_var])
            self.op("dve", lambda e: e.scalar_tensor_tensor(var, p2, 1.0 / 512.0, var, op0=ALU.mult, op1=ALU.subtract), reads=[rp2, r_var], writes=[r_var])
            self.op("act", lambda e: e.activation(var, var, AF.Ln, bias=self.epsc[:, self.eps_idx(EPS)]), reads=[r_var, self.r_epsc], writes=[r_var])
            self.op("act", lambda e: e.activation(var, var, AF.Exp, scale=-0.5), reads=[r_var], writes=[r_var])
            ysl = yv[:, :, sl]
            self.op("dve", lambda e: e.tensor_tensor(ysl, ysl, mean.unsqueeze(1).to_broadcast([128, 4, 512]), op=ALU.subtract),
                    reads=self.rh_fn(tg) + [r_mean], writes=self.rh_fn(tg))
            self.op("dve", lambda e: e.tensor_tensor(ysl, ysl, var.unsqueeze(1).to_broadcast([128, 4, 512]), op=ALU.mult),
                    reads=self.rh_fn(tg) + [r_var], writes=self.rh_fn(tg))
            for cc in range(4):
                self.op("act", lambda e: e.activation(oB[:, 4 + cc, sl], yv[:, cc, sl], AF.Silu, bias=self.pv[:, 88 + cc:89 + cc], scale=self.pv[:, 84 + cc:85 + cc]),
                        reads=self.rh_fn(tg) + [self.r_pv], writes=[r_oB[tg]])
        if "dbg" in self.stages:
            o = self.P.dma("sp", lambda e: e.dma_start(out=d["dbg"], in_=oB), reads=r_oB)
        self.out_proj(oB, r_oB, d["w_out_ab"][0])
        self.barrier()
        self.release(m0)


    def init_stream(self):
        self.xT, _ = self.alloc([8, S], F32, "xT")
        self.rx = [[Res(f"x{k}_{tg}") for tg in range(NTG)] for k in range(8)]
        a0 = self.aoff
        self.hT, _ = self.alloc([8, S], BF16, "hT")
        self.hT_raw = self.arena[:, a0 // 4:a0 // 4 + 4 * S]
        self.rh = [Res(f"h{tg}") for tg in range(NTG)]

    def rx_fn(self, k, tg):
        ks = range(8) if k is None else [k]
        tgs = range(NTG) if tg is None else [tg]
        return [self.rx[a][b] for a in ks for b in tgs]

    def rh_fn(self, tg):
        return [self.rh[t] for t in (range(NTG) if tg is None else [tg])]


INPUT_NAMES = ["x", "mem", "norm_mix", "norm_xattn", "norm_ffn", "mem_norm", "final_norm",
               "w_in_ab", "w_out_ab", "hgrn_lower_bounds", "hgrn_out_norm", "conv_dw_w", "conv_dw_b",
               "conv_ln_g", "conv_ln_b", "w_in_cd", "w_out_cd", "sgu_ln_g", "sgu_ln_b", "sgu_w", "sgu_b",
               "xa_wq", "xa_wkv", "xa_wo", "ffn_w_in", "ffn_w_out"]


def build_program(shapes, nseq, stages):
    nc = bass.Bass("TRN2", target_bir_lowering=False)
    d = {}
    for n in INPUT_NAMES:
        shp = list(shapes[n])
        if n in ("x", "mem"):
            shp[0] = nseq
        d[n] = nc.dram_tensor(n, shp, F32, kind="ExternalInput").ap()
    out = nc.dram_tensor("out", [nseq, S, D], F32, kind="ExternalOutput").ap()
    if "dbg" in stages:
        d["dbg"] = nc.dram_tensor("dbg", [128, 8, S], BF16, kind="ExternalOutput").ap()
    with ExitStack() as es:
        kb = KB(nc, es, nseq, stages)
        kb.setup_consts(d)
        kb.init_stream()
        kb.init_xa()
        if hasattr(kb, "setup_layer_consts"):
            kb.setup_layer_consts(d)
        finals = []
        for s in range(nseq):
            kb.load_T(d["x"][s], S, kb.xT, lambda i, half: [kb.rx[k][i // 4] for k in range(half * 4, half * 4 + 4)])
            if "xa" in stages:
                kb.prep_mem(d["mem"][s])
            for l in range(2):
                if f"mix{l}" in stages:
                    kb.norm_to(kb.xT, kb.rx_fn, 0, l, kb.hT, kb.rh_fn)
                    if "nomix" in stages:
                        pass
                    elif l == 0:
                        kb.mixer0(d)
                    else:
                        kb.mixer1(d)
                if "xa" in stages or f"xa{l}" in stages:
                    kb.norm_to(kb.xT, kb.rx_fn, 16, l, kb.hT, kb.rh_fn)
                    kb.xattn(d, l)
                if "ffn" in stages or f"ffn{l}" in stages:
                    kb.norm_to(kb.xT, kb.rx_fn, 32, l, kb.hT, kb.rh_fn)
                    kb.ffn(kb.xT, kb.rx_fn, kb.hT, kb.rh_fn, d["ffn_w_in"][l], d["ffn_w_out"][l])
            finals += kb.final_store(kb.xT, kb.rx_fn, out[s])
        kb.P.emit(final_waits=finals)
    return nc


ALL_STAGES = ("mix0", "mix1", "xa", "ffn")
_CACHE = {}


def kernel(**inputs):
    n_cores = 8
    nseq = 2
    shapes = {k: np.shape(v) for k, v in inputs.items()}
    key = "full"
    if key not in _CACHE:
        _CACHE[key] = build_program(shapes, nseq, ALL_STAGES)
    nc = _CACHE[key]
    arrs = {k: np.ascontiguousarray(np.asarray(v, dtype=np.float32)) for k, v in inputs.items()}
    in_maps = []
    for c in range(n_cores):
        m = {}
        for k in INPUT_NAMES:
            if k in ("x", "mem"):
                m[k] = np.ascontiguousarray(arrs[k][c * nseq:(c + 1) * nseq])
            else:
                m[k] = arrs[k]
        in_maps.append(m)
    res = run_bass_kernel_spmd(nc, in_maps, core_ids=list(range(n_cores)))
    outs = [np.asarray(r["out"]) for r in res.results]
    return np.concatenate(outs, axis=0).astype(np.float32)
```
